# Optimizing a Trainium2 kernel written in Bass

```python
import math
import jax
import jax.numpy as jnp
from jax import lax
import numpy as np

D_MODEL = 1024
BATCH = 8
SEQ = 4096
DEPTH = 4

HEAD_DIM = 64
ROPE_THETA = 10000.0
NORM_EPS = 1e-6
QBLK = 128

A_HEADS = 8
A_KV_HEADS = 2
A_GROUP = A_HEADS // A_KV_HEADS
A_WIDTH = A_HEADS * HEAD_DIM
NSA_BRANCHES = 3
CMP_LEN = 32
CMP_STRIDE = 16
CMP_HIDDEN = 2 * HEAD_DIM
SEL_LEN = 64
SEL_TOPK = 16
A_WINDOW = 512
FORCED_SCORE = 1e4

B_HEADS = 8
B_KV_HEADS = 1
B_GROUP = B_HEADS // B_KV_HEADS
B_WIDTH = B_HEADS * HEAD_DIM
B_WINDOW = 128

C_WIDTH = 512
C_GROUP_CH = 16
C_GROUPS = C_WIDTH // C_GROUP_CH
C_STATE = 64
DT_MIN = 1e-3
DT_MAX = 1e-1

D_FF = 4 * D_MODEL
N_BRANCH = 3

IN_WIDTHS = (A_WIDTH, NSA_BRANCHES * 2 * A_KV_HEADS * HEAD_DIM, NSA_BRANCHES * A_HEADS, B_WIDTH, 2 * B_KV_HEADS * HEAD_DIM, C_WIDTH, N_BRANCH * D_MODEL)
IN_COLS = sum(IN_WIDTHS)

kernel_name = 'hybrid_nsa_swa_s5_block'


def rmsnorm(x, gain):
    xf = x.astype(jnp.float32)
    y = xf * lax.rsqrt(jnp.mean(xf * xf, axis=-1, keepdims=True) + NORM_EPS)
    return (y * gain.astype(jnp.float32)).astype(x.dtype)


def rope(x, positions):
    half = x.shape[-1] // 2
    inv_freq = ROPE_THETA ** (-jnp.arange(half, dtype=jnp.float32) / half)
    ang = positions.astype(jnp.float32)[..., None] * inv_freq
    cos = jnp.cos(ang)[:, :, None, :]
    sin = jnp.sin(ang)[:, :, None, :]
    xf = x.astype(jnp.float32)
    x1, x2 = xf[..., :half], xf[..., half:]
    return jnp.concatenate([x1 * cos - x2 * sin, x2 * cos + x1 * sin], axis=-1).astype(x.dtype)


def masked_softmax(s, mask, axis=-1):
    s = jnp.where(mask, s.astype(jnp.float32), -jnp.inf)
    m = jnp.max(s, axis=axis, keepdims=True)
    m = jnp.where(jnp.isfinite(m), m, 0.0)
    e = jnp.exp(s - m)
    return e / jnp.maximum(jnp.sum(e, axis=axis, keepdims=True), 1e-30)


def banded_attention(q, k, v, window, sinks=None):
    bsz, seq, hkv, grp, dh = q.shape
    nblk = seq // QBLK
    pad = -(-window // QBLK) * QBLK
    span = pad + QBLK
    kp = jnp.pad(k, ((0, 0), (pad, 0), (0, 0), (0, 0)))
    vp = jnp.pad(v, ((0, 0), (pad, 0), (0, 0), (0, 0)))
    qb = jnp.moveaxis(q.reshape(bsz, nblk, QBLK, hkv, grp, dh), 1, 0)
    scale = dh ** -0.5

    def one_block(args):
        i, qi = args
        start = i * QBLK
        ki = lax.dynamic_slice_in_dim(kp, start, span, axis=1)
        vi = lax.dynamic_slice_in_dim(vp, start, span, axis=1)
        s = jnp.einsum('bqhgd,bkhd->bhgqk', qi, ki).astype(jnp.float32) * scale
        qpos = start + jnp.arange(QBLK)
        kpos = start - pad + jnp.arange(span)
        diff = qpos[:, None] - kpos[None, :]
        mask = (diff >= 0) & (diff < window) & (kpos[None, :] >= 0)
        s = jnp.where(mask, s, -jnp.inf)
        m = jnp.max(s, axis=-1, keepdims=True)
        if sinks is not None:
            sk = sinks.astype(jnp.float32).reshape(hkv, grp)[None, :, :, None, None]
            m = jnp.maximum(m, sk)
            e = jnp.exp(s - m)
            den = jnp.sum(e, axis=-1, keepdims=True) + jnp.exp(sk - m)
        else:
            e = jnp.exp(s - m)
            den = jnp.sum(e, axis=-1, keepdims=True)
        p = (e / den).astype(vi.dtype)
        return jnp.einsum('bhgqk,bkhd->bqhgd', p, vi)

    out = lax.map(one_block, (jnp.arange(nblk), qb))
    return jnp.moveaxis(out, 0, 1).reshape(bsz, seq, hkv, grp, dh)


def nsa_attention(q, k, v, gates, cmp_pos, cmp_w1, cmp_w2):
    bsz, seq, hkv, grp, dh = q.shape
    n_cmp = (seq - CMP_LEN) // CMP_STRIDE + 1
    n_sel = seq // SEL_LEN
    n_top = min(SEL_TOPK, n_sel)
    nblk = seq // QBLK
    scale = dh ** -0.5

    tok = jnp.arange(n_cmp)[:, None] * CMP_STRIDE + jnp.arange(CMP_LEN)[None, :]

    def compress(xb, j):
        blocks = xb[:, tok] + cmp_pos[j][None, None, :, None, :]
        blocks = jnp.moveaxis(blocks, 3, 2).reshape(bsz, n_cmp, hkv, CMP_LEN * dh)
        return jax.nn.gelu(blocks @ cmp_w1[j]) @ cmp_w2[j]

    kc = compress(k[:, :, 0], 0)
    vc = compress(v[:, :, 0], 1)
    cmp_end = jnp.arange(n_cmp) * CMP_STRIDE + CMP_LEN - 1

    c_start = jnp.arange(n_cmp) * CMP_STRIDE
    s_start = jnp.arange(n_sel) * SEL_LEN
    overlap = jnp.minimum(c_start[:, None] + CMP_LEN, s_start[None, :] + SEL_LEN) - jnp.maximum(c_start[:, None], s_start[None, :])
    cmp_to_sel = jnp.clip(overlap, 0, None).astype(jnp.float32) / CMP_LEN

    ks_blk = jnp.moveaxis(k[:, :, 1].reshape(bsz, n_sel, SEL_LEN, hkv, dh), 3, 1)
    vs_blk = jnp.moveaxis(v[:, :, 1].reshape(bsz, n_sel, SEL_LEN, hkv, dh), 3, 1)
    qc = q.reshape(bsz * nblk, QBLK, hkv, grp, dh)
    b_idx = jnp.repeat(jnp.arange(bsz), nblk)
    blk_idx = jnp.tile(jnp.arange(nblk), bsz)
    head_idx = jnp.arange(hkv)[None, :, None]
    sel_ids = jnp.arange(n_sel)

    def one_block(args):
        b, i, qi = args
        qpos = i * QBLK + jnp.arange(QBLK)
        s = jnp.einsum('qhgd,nhd->qhgn', qi, kc[b]).astype(jnp.float32) * scale
        p_c = masked_softmax(s, (cmp_end[None, :] <= qpos[:, None])[:, None, None, :])
        o_c = jnp.einsum('qhgn,nhd->qhgd', p_c.astype(qi.dtype), vc[b])
        imp = jnp.einsum('qhgn,nj->qhj', p_c, cmp_to_sel)
        cur = qpos // SEL_LEN
        valid = sel_ids[None, :] <= cur[:, None]
        forced = (sel_ids[None, :] == 0) | (sel_ids[None, :] == cur[:, None]) | (sel_ids[None, :] == cur[:, None] - 1)
        score = jnp.where(forced[:, None, :], FORCED_SCORE, imp)
        score = jnp.where(valid[:, None, :], score, -jnp.inf)
        _, top = lax.top_k(score, n_top)
        kg = ks_blk[b][head_idx, top]
        vg = vs_blk[b][head_idx, top]
        s = jnp.einsum('qhgd,qhnkd->qhgnk', qi, kg).astype(jnp.float32) * scale
        kpos = top[..., None] * SEL_LEN + jnp.arange(SEL_LEN)
        mask = (kpos <= qpos[:, None, None, None])[:, :, None]
        p_s = masked_softmax(s, mask, axis=(-2, -1))
        o_s = jnp.einsum('qhgnk,qhnkd->qhgd', p_s.astype(qi.dtype), vg)
        return o_c, o_s

    o_c, o_s = lax.map(one_block, (b_idx, blk_idx, qc))
    o_c = o_c.reshape(bsz, seq, hkv, grp, dh)
    o_s = o_s.reshape(bsz, seq, hkv, grp, dh)
    o_w = banded_attention(q, k[:, :, 2], v[:, :, 2], A_WINDOW)
    out = gates[:, :, 0][..., None] * o_c + gates[:, :, 1][..., None] * o_s + gates[:, :, 2][..., None] * o_w
    return out.reshape(bsz, seq, hkv * grp * dh)


def s5_layer(u, a_re, a_im, log_dt, b_re, b_im, c_re, c_im, d, glu_w, glu_b):
    bsz, seq, _ = u.shape
    uf = u.astype(jnp.float32)
    ug = uf.reshape(bsz, seq, C_GROUPS, C_GROUP_CH)
    dt = jnp.exp(log_dt.astype(jnp.float32))[:, None]
    ar, ai = a_re.astype(jnp.float32), a_im.astype(jnp.float32)
    mag = jnp.exp(dt * ar)
    abar_r, abar_i = mag * jnp.cos(dt * ai), mag * jnp.sin(dt * ai)
    den = ar * ar + ai * ai
    nr, ni = abar_r - 1.0, abar_i
    coef_r = (nr * ar + ni * ai) / den
    coef_i = (ni * ar - nr * ai) / den
    br, bi = b_re.astype(jnp.float32), b_im.astype(jnp.float32)
    bbar_r = coef_r[..., None] * br - coef_i[..., None] * bi
    bbar_i = coef_r[..., None] * bi + coef_i[..., None] * br
    bu_r = jnp.einsum('bsgh,gph->bsgp', ug, bbar_r)
    bu_i = jnp.einsum('bsgh,gph->bsgp', ug, bbar_i)
    el_r = jnp.broadcast_to(abar_r, bu_r.shape)
    el_i = jnp.broadcast_to(abar_i, bu_i.shape)

    def combine(e1, e2):
        a1r, a1i, b1r, b1i = e1
        a2r, a2i, b2r, b2i = e2
        return (a2r * a1r - a2i * a1i, a2r * a1i + a2i * a1r,
                a2r * b1r - a2i * b1i + b2r, a2r * b1i + a2i * b1r + b2i)

    _, _, xr, xi = lax.associative_scan(combine, (el_r, el_i, bu_r, bu_i), axis=1)
    y = jnp.einsum('bsgp,ghp->bsgh', xr, c_re.astype(jnp.float32)) - jnp.einsum('bsgp,ghp->bsgh', xi, c_im.astype(jnp.float32))
    y = y.reshape(bsz, seq, C_WIDTH) + d.astype(jnp.float32) * uf
    z = jax.nn.gelu(y)
    out = z * jax.nn.sigmoid(z @ glu_w.astype(jnp.float32) + glu_b.astype(jnp.float32))
    return out.astype(u.dtype)


def setup_inputs(seed: int = 0) -> dict:
    key = jax.random.key(seed)
    ks = jax.random.split(key, 32)
    f32 = jnp.float32
    L = DEPTH

    def nrm(k, shape, scale):
        return scale * jax.random.normal(k, shape, f32)

    x = jax.random.normal(ks[0], (BATCH, SEQ, D_MODEL), f32)
    offset = jax.random.randint(ks[1], (BATCH, 1), 0, 4096, jnp.int32)
    positions = offset + jnp.arange(SEQ, dtype=jnp.int32)[None, :]
    state_idx = jnp.arange(C_STATE, dtype=f32)
    return {
        'x': x,
        'positions': positions,
        'norm_mix': 1.0 + nrm(ks[2], (L, D_MODEL), 0.02),
        'w_in': nrm(ks[3], (L, D_MODEL, IN_COLS), D_MODEL ** -0.5),
        'nsa_cmp_pos': nrm(ks[4], (L, 2, CMP_LEN, HEAD_DIM), 0.02),
        'nsa_cmp_w1': nrm(ks[5], (L, 2, CMP_LEN * HEAD_DIM, CMP_HIDDEN), (CMP_LEN * HEAD_DIM) ** -0.5),
        'nsa_cmp_w2': nrm(ks[6], (L, 2, CMP_HIDDEN, HEAD_DIM), CMP_HIDDEN ** -0.5),
        'swa_sinks': nrm(ks[7], (L, B_HEADS), 0.5),
        's5_a_re': -0.5 + nrm(ks[8], (L, C_GROUPS, C_STATE), 0.01),
        's5_a_im': math.pi * state_idx + nrm(ks[9], (L, C_GROUPS, C_STATE), 0.01),
        's5_log_dt': jax.random.uniform(ks[10], (L, C_GROUPS), f32, math.log(DT_MIN), math.log(DT_MAX)),
        's5_b_re': nrm(ks[11], (L, C_GROUPS, C_STATE, C_GROUP_CH), (2 * C_GROUP_CH) ** -0.5),
        's5_b_im': nrm(ks[12], (L, C_GROUPS, C_STATE, C_GROUP_CH), (2 * C_GROUP_CH) ** -0.5),
        's5_c_re': nrm(ks[13], (L, C_GROUPS, C_GROUP_CH, C_STATE), 0.5 ** 0.5),
        's5_c_im': nrm(ks[14], (L, C_GROUPS, C_GROUP_CH, C_STATE), 0.5 ** 0.5),
        's5_d': nrm(ks[15], (L, C_WIDTH), 0.5),
        's5_glu_w': nrm(ks[16], (L, C_WIDTH, C_WIDTH), C_WIDTH ** -0.5),
        's5_glu_b': nrm(ks[17], (L, C_WIDTH), 0.01),
        'w_branch_a': nrm(ks[18], (L, A_WIDTH, D_MODEL), A_WIDTH ** -0.5),
        'w_branch_b': nrm(ks[19], (L, B_WIDTH, D_MODEL), B_WIDTH ** -0.5),
        'w_branch_c': nrm(ks[20], (L, C_WIDTH, D_MODEL), C_WIDTH ** -0.5),
        'w_out': nrm(ks[21], (L, D_MODEL, D_MODEL), D_MODEL ** -0.5),
        'norm_mlp': 1.0 + nrm(ks[22], (L, D_MODEL), 0.02),
        'w_mlp_up': nrm(ks[23], (L, D_MODEL, D_FF), D_MODEL ** -0.5),
        'w_mlp_down': nrm(ks[24], (L, D_FF, D_MODEL), D_FF ** -0.5),
        'norm_final': 1.0 + nrm(ks[25], (D_MODEL,), 0.02),
    }


def reference(x, positions, norm_mix, w_in, nsa_cmp_pos, nsa_cmp_w1, nsa_cmp_w2, swa_sinks,
              s5_a_re, s5_a_im, s5_log_dt, s5_b_re, s5_b_im, s5_c_re, s5_c_im, s5_d, s5_glu_w, s5_glu_b,
              w_branch_a, w_branch_b, w_branch_c, w_out, norm_mlp, w_mlp_up, w_mlp_down, norm_final):
    bsz, seq, _ = x.shape
    splits = []
    acc = 0
    for w in IN_WIDTHS[:-1]:
        acc += w
        splits.append(acc)

    for l in range(DEPTH):
        h = rmsnorm(x, norm_mix[l])
        proj = h @ w_in[l]
        q_a, kv_a, g_a, q_b, kv_b, u_c, g_m = jnp.split(proj, splits, axis=-1)

        q_a = rope(q_a.reshape(bsz, seq, A_HEADS, HEAD_DIM), positions).reshape(bsz, seq, A_KV_HEADS, A_GROUP, HEAD_DIM)
        kv_a = kv_a.reshape(bsz, seq, NSA_BRANCHES, 2, A_KV_HEADS, HEAD_DIM)
        k_a = rope(kv_a[:, :, :, 0].reshape(bsz, seq, NSA_BRANCHES * A_KV_HEADS, HEAD_DIM), positions).reshape(bsz, seq, NSA_BRANCHES, A_KV_HEADS, HEAD_DIM)
        v_a = kv_a[:, :, :, 1]
        gates_a = jax.nn.sigmoid(g_a.reshape(bsz, seq, NSA_BRANCHES, A_KV_HEADS, A_GROUP))
        y_a = nsa_attention(q_a, k_a, v_a, gates_a, nsa_cmp_pos[l], nsa_cmp_w1[l], nsa_cmp_w2[l])

        q_b = rope(q_b.reshape(bsz, seq, B_HEADS, HEAD_DIM), positions).reshape(bsz, seq, B_KV_HEADS, B_GROUP, HEAD_DIM)
        kv_b = kv_b.reshape(bsz, seq, 2, B_KV_HEADS, HEAD_DIM)
        k_b = rope(kv_b[:, :, 0], positions)
        y_b = banded_attention(q_b, k_b, kv_b[:, :, 1], B_WINDOW, swa_sinks[l]).reshape(bsz, seq, B_WIDTH)

        y_c = s5_layer(u_c, s5_a_re[l], s5_a_im[l], s5_log_dt[l], s5_b_re[l], s5_b_im[l],
                       s5_c_re[l], s5_c_im[l], s5_d[l], s5_glu_w[l], s5_glu_b[l])

        g_m = jax.nn.sigmoid(g_m.reshape(bsz, seq, N_BRANCH, D_MODEL))
        merged = (g_m[:, :, 0] * (y_a @ w_branch_a[l])
                  + g_m[:, :, 1] * (y_b @ w_branch_b[l])
                  + g_m[:, :, 2] * (y_c @ w_branch_c[l]))
        x = x + merged @ w_out[l]

        h = rmsnorm(x, norm_mlp[l])
        x = x + jnp.square(jax.nn.relu(h @ w_mlp_up[l])) @ w_mlp_down[l]

    return rmsnorm(x, norm_final)
```

```python
import math
import numpy as np
import concourse.bass as bass
import concourse.mybir as mybir
from concourse.bass_utils import run_bass_kernel_spmd

F32 = mybir.dt.float32
BF16 = mybir.dt.bfloat16
I32 = mybir.dt.int32
AF = mybir.ActivationFunctionType
ALU = mybir.AluOpType
AX = mybir.AxisListType

S = 4096
D = 1024
L = 4
NT = 32
NG = 8
NEG = -30000.0
SB_BASE = 16640
SB_TOP = 229312

NSLOT = 8
SEM_CHUNK = 2000


class _St:
    __slots__ = ("w", "r")

    def __init__(self):
        self.w = None
        self.r = []


class T:
    def __init__(self, h, name=""):
        self.h = h
        self.name = name
        self.whole = _St()
        self.parts = {}

    def __getitem__(self, k):
        return self.h[k]


class Op:
    __slots__ = ("stream", "vq", "vidx", "fn", "deps", "waits", "signaled", "sem", "val", "isdma")


class KB:
    def __init__(self, nc):
        self.nc = nc
        self.ops = []
        self.vq_count = {}
        self.vq_last = {}
        self.dma_n = {"sp": 0, "act": 0, "pool": 0}
        self.pending = {}
        self.streams = ["pe", "act", "dve", "pool", "sp"]

    def _acc(self, lst):
        out = []
        for a in lst:
            if isinstance(a, T):
                out.append((a, None))
            else:
                out.append(a)
        return out

    def op(self, stream, fn, reads=(), writes=(), dma=False):
        o = Op()
        o.stream = stream
        o.fn = fn
        o.isdma = dma
        if dma:
            n = self.dma_n[stream]
            self.dma_n[stream] = n + 1
            o.vq = "%s_d%d" % (stream, n % NSLOT)
        else:
            o.vq = stream
        o.vidx = self.vq_count.get(o.vq, 0)
        self.vq_count[o.vq] = o.vidx + 1
        deps = set()
        if dma and o.vq in self.vq_last:
            deps.add(self.vq_last[o.vq])
        self.vq_last[o.vq] = o
        pb = self.pending.pop(stream, None)
        if pb:
            deps.update(pb)
        for (t, k) in self._acc(reads):
            sts = [t.whole]
            if k is None:
                sts += list(t.parts.values())
            else:
                if k not in t.parts:
                    t.parts[k] = _St()
                sts.append(t.parts[k])
            for s in sts:
                if s.w is not None:
                    deps.add(s.w)
            (t.whole if k is None else t.parts[k]).r.append(o)
        for (t, k) in self._acc(writes):
            sts = [t.whole]
            if k is None:
                sts += list(t.parts.values())
            else:
                if k not in t.parts:
                    t.parts[k] = _St()
                sts.append(t.parts[k])
            for s in sts:
                if s.w is not None:
                    deps.add(s.w)
                deps.update(s.r)
            if k is None:
                t.parts = {}
                t.whole.w = o
                t.whole.r = []
            else:
                st = t.parts[k]
                st.w = o
                st.r = []
        deps.discard(o)
        o.deps = deps
        o.waits = []
        o.signaled = False
        self.ops.append(o)
        return o

    def barrier(self):
        lasts = list(self.vq_last.values())
        for s in self.streams:
            self.pending[s] = set(lasts) | self.pending.get(s, set())

    def pe(self, fn, reads=(), writes=()):
        return self.op("pe", fn, reads, writes)

    def act(self, fn, reads=(), writes=()):
        return self.op("act", fn, reads, writes)

    def dve(self, fn, reads=(), writes=()):
        return self.op("dve", fn, reads, writes)

    def pool(self, fn, reads=(), writes=()):
        return self.op("pool", fn, reads, writes)

    def dma(self, out, in_, reads=(), writes=(), q="sp", **kw):
        return self.op(q, lambda e: e.dma_start(out=out, in_=in_, **kw), reads, writes, dma=True)

    def emit(self):
        from contextlib import ExitStack
        nc = self.nc
        waited = {}
        for c in self.ops:
            for p in sorted(c.deps, key=lambda x: x.vidx):
                if p.stream == c.stream and c.stream == "pe" and not p.isdma:
                    continue
                key = (c.stream, p.vq)
                if waited.get(key, -1) >= p.vidx:
                    continue
                waited[key] = p.vidx
                p.signaled = True
                c.waits.append(p)
        lasts = list(self.vq_last.values())
        for p in lasts:
            p.signaled = True
        cnt = {}
        need = {}
        for o in self.ops:
            if o.signaled:
                n = cnt.get(o.vq, 0)
                cnt[o.vq] = n + 1
                o.sem = (o.vq, n // SEM_CHUNK)
                o.val = (n % SEM_CHUNK + 1) * (16 if o.isdma else 1)
                need[o.sem] = True
        with ExitStack() as es:
            sems = {}
            for s in sorted(need.keys()):
                sems[s] = es.enter_context(nc.semaphore("s_%s_%d" % s))
            per = {s: [o for o in self.ops if o.stream == s] for s in self.streams}
            block = es.enter_context(nc.Block())

            def run(stream, eng):
                for o in per[stream]:
                    w = {}
                    for p in o.waits:
                        w[p.sem] = max(w.get(p.sem, 0), p.val)
                    for s, v in w.items():
                        eng.wait_ge(sems[s], v)
                    ins = o.fn(eng)
                    if o.signaled:
                        ins.then_inc(sems[o.sem], 16 if o.isdma else 1)
                if stream == "sp":
                    for p in lasts:
                        eng.wait_ge(sems[p.sem], p.val)

            @block.tensor
            def _(e):
                run("pe", e)

            @block.scalar
            def _(e):
                run("act", e)

            @block.vector
            def _(e):
                run("dve", e)

            @block.gpsimd
            def _(e):
                run("pool", e)

            @block.sync
            def _(e):
                run("sp", e)
        return len(self.ops)


class Prog:
    def __init__(self, n_layers=L, debug=False, phases=None):
        self.nl = n_layers
        self.debug = debug
        self.phases = phases
        nc = bass.Bass("TRN2", target_bir_lowering=False)
        self.nc = nc
        self.kb = KB(nc)
        self.sb_off = SB_BASE
        self.uid = 0
        self.arena = nc.alloc_sbuf_tensor_at("arena", [128, (SB_TOP - SB_BASE) // 2], BF16, offset=SB_BASE)
        self.decl_io()
        self.alloc_psum()

    def tile(self, shape, dt, name="t"):
        self.uid += 1
        nb = 2 if dt == BF16 else 4
        n = 1
        for s in shape[1:]:
            n *= s
        size = (n * nb + 63) // 64 * 64
        off = self.sb_off
        assert off + size <= SB_TOP, ("SBUF overflow", name, off, size)
        self.sb_off += size
        lo = (off - SB_BASE) // 2
        flat = self.arena[:, lo:lo + size // 2]
        if dt != BF16:
            flat = flat.bitcast(dt)
        v = flat[:, 0:n]
        sh = list(shape)
        if len(sh) == 3:
            v = v.rearrange("p (a b) -> p a b", a=sh[1])
        elif len(sh) == 4:
            v = v.rearrange("p (a b c) -> p a b c", a=sh[1], b=sh[2])
        elif len(sh) == 5:
            v = v.rearrange("p (a b c d) -> p a b c d", a=sh[1], b=sh[2], c=sh[3])
        assert sh[0] == 128
        return T(v, name)

    def mark(self):
        return self.sb_off

    def release(self, m):
        self.sb_off = m

    def dram_in(self, name, shape, dt=F32):
        return self.nc.dram_tensor(name, list(shape), dt, kind="ExternalInput")

    def dram_scr(self, name, shape, dt, out=False):
        kind = "ExternalOutput" if (out or self.debug) else "Internal"
        return self.nc.dram_tensor(name, list(shape), dt, kind=kind)

    def decl_io(self):
        nl = self.nl
        d = self.dram_in
        self.i_x = d("xT_in", [128, 8, S])
        self.i_pos = d("pos", [1, S], I32)
        self.i_gains = d("gains", [2 * L + 1, 128, 8])
        self.i_wnf = d("w_nsa_f", [L, D, 1024])
        self.i_wnt = d("w_nsa_t", [L, D, 280])
        self.i_wsf = d("w_swa_f", [L, D, 640])
        self.i_wst = d("w_swa_t", [L, D, 64])
        self.i_wu = d("w_u", [L, D, 512])
        self.i_wgm = d("w_gm", [L, D, 3072])
        self.i_cw1 = d("cw1", [L, 2, 64, 32, 128])
        self.i_cw2 = d("cw2", [L, 2, 128, 64])
        self.i_cpos = d("cposT", [L, 64, 2, 32])
        self.i_sinks = d("sinks", [L, 1, 8])
        self.i_s5a = d("s5a", [L, 128, 3, 16])
        self.i_s5b = d("s5b", [L, 128, 16, 2, 128])
        self.i_s5c = d("s5c", [L, 128, 16, 2, 128])
        self.i_s5d = d("s5d", [L, 128, 2, 4])
        self.i_glu = d("glu_w", [L, 512, 512])
        self.i_wbr = d("w_br", [L, 3, 512, D])
        self.i_wout = d("w_out", [L, D, D])
        self.i_wup = d("w_up", [L, D, 4 * D])
        self.i_wdn = d("w_dn", [L, 4 * D, D])
        self.i_csel = d("csel", [256, 64])
        self.i_nfv = d("nfv", [NT, 128, 64])
        self.i_add = d("addend", [NT, 128, 64])
        self.i_rc = d("ropec", [128, 2])
        s = self.dram_scr
        self.o_out = self.nc.dram_tensor("outT", [128, 8, S], F32, kind="ExternalOutput")
        self.d_x = s("xT", [128, 8, S], F32)
        self.d_h = s("hT", [128, 8, S], BF16)
        self.d_cos = s("cosT", [128, S], F32)
        self.d_sin = s("sinT", [128, S], F32)
        self.d_ya = s("yaT", [128, 4, S], BF16)
        self.d_yb = s("ybT", [128, 4, S], BF16)
        self.d_yc = s("ycT", [128, 4, S], BF16)
        self.d_dbg = s("dbg", [128, 8192], F32)
        self.d_z = s("zT", [128, 4, S], BF16)

    def alloc_psum(self):
        nc = self.nc
        self.pf = [T(nc.alloc_psum_tensor("pf%d" % i, [128, 512], F32), "pf%d" % i) for i in range(7)]
        self.pb = T(nc.alloc_psum_tensor("pb", [128, 1024], BF16), "pb")
        self.pf_rr = 0

    def bank(self, lo=0, hi=7):
        n = hi - lo
        b = self.pf[lo + (self.pf_rr % n)]
        self.pf_rr += 1
        return b

    def mm(self, out, lhsT, rhs, start, stop, reads=(), writes=(), **kw):
        return self.kb.pe(lambda e: e.matmul(out, lhsT=lhsT, rhs=rhs, start=start, stop=stop, **kw), reads, writes)

    def load_w_bf16(self, dst, dst_ap, src_ap, shape, eng="pool"):
        st = self.stage[self.stage_i % len(self.stage)]
        self.stage_i += 1
        n = 1
        for s_ in shape[1:]:
            n *= s_
        assert n <= self.stage_n, (n, self.stage_n)
        flat = st[:, 0:n]
        if len(shape) == 3:
            sv = flat.rearrange("p (a b) -> p a b", a=shape[1])
        elif len(shape) == 4:
            sv = flat.rearrange("p (a b c) -> p a b c", a=shape[1], b=shape[2])
        else:
            sv = flat
        self.kb.dma(sv, src_ap, writes=[st])
        if self.stage_i % 2 == 0:
            self.kb.dve(lambda e: e.tensor_copy(out=dst_ap, in_=sv), reads=[st], writes=[dst])
        else:
            self.kb.act(lambda e: e.activation(out=dst_ap, in_=sv, func=AF.Copy), reads=[st], writes=[dst])

    def setup_stage(self, n_elems, nbuf=2):
        self.stage = [self.tile([128, n_elems], F32, "stage") for _ in range(nbuf)]
        self.stage_n = n_elems
        self.stage_i = 0

    def build(self):
        kb = self.kb
        self.consts()
        base = self.mark()
        for l in range(self.nl):
            for ph, fn in (("norm", self.ph_norm), ("nsa", self.ph_nsa), ("swa", self.ph_swa),
                           ("s5", self.ph_s5), ("merge", self.ph_merge), ("mlp", self.ph_mlp)):
                if self.phases is not None and ph not in self.phases:
                    continue
                kb.barrier()
                self.release(base)
                fn(l)
        kb.barrier()
        self.release(base)
        if self.phases is None or "final" in self.phases:
            self.ph_final()
        kb.barrier()
        n = kb.emit()
        return self.nc, n

    def consts(self):
        kb = self.kb
        nc = self.nc
        for k in range(8):
            kb.dma(self.d_x[:, k, :], self.i_x[:, k, :])
        self.ident = self.tile([128, 128], BF16, "ident")
        self.identf = self.tile([128, 128], F32, "identf")
        self.ones = self.tile([128, 128], BF16, "ones")
        self.zeros = self.tile([128, 512], BF16, "zeros")
        self.maskD = self.tile([128, 4, 128], BF16, "maskD")
        self.maskU = self.tile([128, 4, 128], BF16, "maskU")
        self.ewide = self.tile([128, S], BF16, "ewide")
        self.rc = self.tile([128, 2], F32, "rc")
        self.eps = self.tile([128, 1], F32, "eps")
        kb.pool(lambda e: e.memset(self.ident[:], 1.0), writes=[self.ident])
        kb.pool(lambda e: e.affine_select(out=self.ident[:], in_=self.ident[:], pattern=[[1, 128]],
                                          compare_op=ALU.is_equal, fill=0.0, base=0, channel_multiplier=-1),
                reads=[self.ident], writes=[self.ident])
        kb.pool(lambda e: e.memset(self.identf[:], 1.0), writes=[self.identf])
        kb.pool(lambda e: e.affine_select(out=self.identf[:], in_=self.identf[:], pattern=[[1, 128]],
                                          compare_op=ALU.is_equal, fill=0.0, base=0, channel_multiplier=-1),
                reads=[self.identf], writes=[self.identf])
        self.permf = self.tile([128, 128], F32, "permf")
        for q4 in range(4):
            src = q4 ^ 1
            kb.pool(lambda e, q4=q4, src=src: e.tensor_copy(out=self.permf[:, 32 * q4:32 * q4 + 32], in_=self.identf[:, 32 * src:32 * src + 32]),
                    reads=[self.identf], writes=[self.permf])
        kb.pool(lambda e: e.memset(self.ones[:], 1.0), writes=[self.ones])
        kb.pool(lambda e: e.memset(self.zeros[:], 0.0), writes=[self.zeros])
        kb.pool(lambda e: e.memset(self.eps[:], 1e-6), writes=[self.eps])
        kb.pool(lambda e: e.memset(self.maskD[:], 0.0), writes=[self.maskD])
        kb.pool(lambda e: e.affine_select(out=self.maskD[:], in_=self.maskD[:], pattern=[[0, 4], [1, 128]],
                                          compare_op=ALU.is_ge, fill=NEG, base=0, channel_multiplier=-1),
                reads=[self.maskD], writes=[self.maskD])
        kb.pool(lambda e: e.memset(self.maskU[:], 0.0), writes=[self.maskU])
        kb.pool(lambda e: e.affine_select(out=self.maskU[:], in_=self.maskU[:], pattern=[[0, 4], [-1, 128]],
                                          compare_op=ALU.is_ge, fill=NEG, base=-1, channel_multiplier=1),
                reads=[self.maskU], writes=[self.maskU])
        for hf in range(2):
            sl = slice(64 * hf, 64 * hf + 64)
            kb.pool(lambda e, sl=sl: e.memset(self.ewide[sl, :], 1.0), writes=[self.ewide])
            kb.pool(lambda e, sl=sl: e.affine_select(out=self.ewide[sl, :], in_=self.ewide[sl, :], pattern=[[1, S]],
                                                     compare_op=ALU.is_ge, fill=0.0, base=0, channel_multiplier=-64),
                    reads=[self.ewide], writes=[self.ewide])
            kb.pool(lambda e, sl=sl: e.affine_select(out=self.ewide[sl, :], in_=self.ewide[sl, :], pattern=[[-1, S]],
                                                     compare_op=ALU.is_ge, fill=0.0, base=63, channel_multiplier=64),
                    reads=[self.ewide], writes=[self.ewide])
        kb.dma(self.rc[:], self.i_rc[:, :], writes=[self.rc])
        self.csel = self.tile([128, 2, 64], BF16, "csel")
        m = self.mark()
        cs = self.tile([128, 2, 64], F32, "csst")
        kb.dma(cs[:], self.i_csel.ap().rearrange("(t p) j -> p t j", p=128), writes=[cs])
        kb.dve(lambda e: e.tensor_copy(out=self.csel[:], in_=cs[:]), reads=[cs], writes=[self.csel])
        self.rope_tables()
        kb.barrier()
        self.release(m)

    def rope_tables(self):
        kb = self.kb
        HI = 6.28125
        LO = 2.0 * math.pi - HI
        for half in range(2):
            m = self.mark()
            n = 2048
            sl = slice(half * n, (half + 1) * n)
            pi_ = self.tile([128, n], I32, "posi")
            pf_ = self.tile([128, n], F32, "posf")
            ang = self.tile([128, n], F32, "ang")
            kf = self.tile([128, n], F32, "kf")
            ki = self.tile([128, n], I32, "ki")
            r = self.tile([128, n], F32, "r")
            msk = self.tile([128, n], F32, "msk")
            kb.dma(pi_[:], self.i_pos[:, sl].partition_broadcast(128) if False else self.i_pos[0:1, sl].broadcast_to([128, n]), writes=[pi_])
            kb.dve(lambda e: e.tensor_copy(out=pf_[:], in_=pi_[:]), reads=[pi_], writes=[pf_])
            kb.dve(lambda e: e.tensor_scalar(out=ang[:], in0=pf_[:], scalar1=self.rc[:, 0:1], scalar2=None, op0=ALU.mult),
                   reads=[pf_, self.rc], writes=[ang])
            kb.dve(lambda e: e.tensor_scalar(out=ki[:], in0=ang[:], scalar1=1.0 / (2 * math.pi), scalar2=None, op0=ALU.mult),
                   reads=[ang], writes=[ki])
            kb.dve(lambda e: e.tensor_copy(out=kf[:], in_=ki[:]), reads=[ki], writes=[kf])
            kb.dve(lambda e: e.scalar_tensor_tensor(out=ang[:], in0=kf[:], scalar=-HI, in1=ang[:], op0=ALU.mult, op1=ALU.add),
                   reads=[kf, ang], writes=[ang])
            kb.dve(lambda e: e.scalar_tensor_tensor(out=ang[:], in0=kf[:], scalar=-LO, in1=ang[:], op0=ALU.mult, op1=ALU.add),
                   reads=[kf, ang], writes=[ang])
            for which in range(2):
                kb.dve(lambda e, which=which: e.tensor_scalar(out=r[:], in0=ang[:], scalar1=(math.pi / 2) * which, scalar2=None, op0=ALU.add),
                       reads=[ang], writes=[r])
                for rep in range(2):
                    kb.dve(lambda e: e.tensor_scalar(out=msk[:], in0=r[:], scalar1=math.pi, scalar2=None, op0=ALU.is_gt),
                           reads=[r], writes=[msk])
                    kb.dve(lambda e: e.scalar_tensor_tensor(out=r[:], in0=msk[:], scalar=-2 * math.pi, in1=r[:], op0=ALU.mult, op1=ALU.add),
                           reads=[msk, r], writes=[r])
                    kb.dve(lambda e: e.tensor_scalar(out=msk[:], in0=r[:], scalar1=-math.pi, scalar2=None, op0=ALU.is_lt),
                           reads=[r], writes=[msk])
                    kb.dve(lambda e: e.scalar_tensor_tensor(out=r[:], in0=msk[:], scalar=2 * math.pi, in1=r[:], op0=ALU.mult, op1=ALU.add),
                           reads=[msk, r], writes=[r])
                kb.dve(lambda e: e.tensor_scalar(out=r[:], in0=r[:], scalar1=3.1415925, scalar2=-3.1415925, op0=ALU.min, op1=ALU.max),
                       reads=[r], writes=[r])
                kb.act(lambda e: e.activation(out=msk[:], in_=r[:], func=AF.Sin), reads=[r], writes=[msk])
                if which == 0:
                    kb.dve(lambda e: e.tensor_scalar(out=msk[:], in0=msk[:], scalar1=self.rc[:, 1:2], scalar2=None, op0=ALU.mult),
                           reads=[msk, self.rc], writes=[msk])
                    kb.dma(self.d_sin[:, sl], msk[:], reads=[msk])
                else:
                    kb.dma(self.d_cos[:, sl], msk[:], reads=[msk])
            kb.barrier()
            self.release(m)

    def rmsnorm_group(self, xg, hg, gain, sq, rstd):
        kb = self.kb
        ps = self.bank()
        for k in range(8):
            kb.act(lambda e, k=k: e.activation(out=sq[:, k, :], in_=xg[:, k, :], func=AF.Square), reads=[xg], writes=[(sq, k)])
        for k in range(8):
            self.mm(ps[:], self.ones[:], sq[:, k, :], k == 0, k == 7, reads=[(sq, k), self.ones], writes=[ps])
        kb.act(lambda e: e.activation(out=rstd[:], in_=ps[:], func=AF.Sqrt, scale=1.0 / D, bias=self.eps[:]),
               reads=[ps, self.eps], writes=[rstd])
        kb.dve(lambda e: e.reciprocal(out=rstd[:], in_=rstd[:]), reads=[rstd], writes=[rstd])
        for k in range(8):
            kb.dve(lambda e, k=k: e.scalar_tensor_tensor(out=hg[:, k, :], in0=xg[:, k, :], scalar=gain[:, k:k + 1], in1=rstd[:],
                                                         op0=ALU.mult, op1=ALU.mult),
                   reads=[xg, gain, rstd], writes=[(hg, k)])

    def ph_norm(self, l):
        kb = self.kb
        gain = self.tile([128, 8], F32, "gain")
        kb.dma(gain[:], self.i_gains[l], writes=[gain])
        xs = [self.tile([128, 8, 512], F32, "xg") for _ in range(2)]
        hs = [self.tile([128, 8, 512], BF16, "hg") for _ in range(2)]
        sqs = [self.tile([128, 8, 512], BF16, "sq") for _ in range(2)]
        rstds = [self.tile([128, 512], F32, "rstd") for _ in range(2)]
        for g in range(NG):
            xg, hg = xs[g % 2], hs[g % 2]
            sl = slice(g * 512, (g + 1) * 512)
            kb.dma(xg[:], self.d_x[:, :, sl], writes=[xg])
            self.rmsnorm_group(xg, hg, gain, sqs[g % 2], rstds[g % 2])
            kb.dma(self.d_h[:, :, sl], hg[:], reads=[hg])

    def rope_chunk(self, ps, dst_ap, dst, cs, sn, xf, rot, t1):
        kb = self.kb
        kb.act(lambda e: e.activation(out=xf[:], in_=ps[:], func=AF.Copy), reads=[ps], writes=[xf])
        pr = self.bank()
        self.mm(pr[:], self.permf[:], xf[:], True, True, reads=[xf, self.permf], writes=[pr])
        kb.dve(lambda e: e.tensor_tensor(out=t1[:], in0=xf[:], in1=cs[:], op=ALU.mult), reads=[xf, cs], writes=[t1])
        kb.dve(lambda e: e.tensor_tensor(out=rot[:], in0=pr[:], in1=sn[:], op=ALU.mult), reads=[pr, sn], writes=[rot])
        kb.dve(lambda e: e.tensor_tensor(out=dst_ap, in0=t1[:], in1=rot[:], op=ALU.add), reads=[t1, rot], writes=[dst])

    def proj_feature(self, wsb, ncol_chunks, hg, consume):
        for c in range(ncol_chunks):
            ps = self.bank()
            for k in range(8):
                self.mm(ps[:], wsb[:, k, c * 128:(c + 1) * 128], hg[:, k, :], k == 0, k == 7, reads=[hg], writes=[ps])
            consume(c, ps)

    def ph_nsa(self, l):
        kb = self.kb
        wf = self.tile([128, 8, 1024], BF16, "wnf")
        wt = self.tile([128, 8, 280], BF16, "wnt")
        w1 = self.tile([128, 2, 32, 128], BF16, "cw1")
        w2k = self.tile([128, 128], BF16, "cw2k")
        w2v = self.tile([128, 64], BF16, "cw2v")
        cpos = self.tile([128, 2, 32], BF16, "cpos")
        qT = self.tile([128, 4, S], BF16, "qTa")
        kT = self.tile([128, 3, S], BF16, "kTa")
        v0T = self.tile([128, S], BF16, "v0T")
        vtok = self.tile([128, NT, 2, 2, 65], BF16, "vtok")
        gates = self.tile([128, NT, 24], F32, "gates")
        kc = self.tile([128, 256], BF16, "kc")
        vcaug = self.tile([128, 2, 2, 128], BF16, "vcaug")
        m_w = self.mark()
        self.setup_stage(4096)
        for k in range(8):
            self.load_w_bf16(wf, wf[:, k, :], self.i_wnf[l, k * 128:(k + 1) * 128, :], [128, 1024])
        self.load_w_bf16(wt, wt[:, :, :], self.i_wnt[l].rearrange("(k p) c -> p k c", p=128), [128, 8, 280])
        for kv in range(2):
            for hf in range(2):
                st = self.stage[self.stage_i % 2]
                self.stage_i += 1
                sv = st[64 * hf:64 * hf + 64, 0:4096].rearrange("p (a b) -> p a b", a=32)
                kb.dma(sv, self.i_cw1[l, kv], writes=[st])
                kb.pool(lambda e, sv=sv, kv=kv, hf=hf: e.tensor_copy(out=w1[64 * hf:64 * hf + 64, kv, :, :], in_=sv), reads=[st], writes=[w1])
        st = self.stage[self.stage_i % 2]
        self.stage_i += 1
        kb.dma(st[:, 0:64], self.i_cw2[l, 0], writes=[st])
        kb.dma(st[:, 64:128], self.i_cw2[l, 1], writes=[st])
        kb.pool(lambda e, st=st: e.tensor_copy(out=w2k[:, 0:64], in_=st[:, 0:64]), reads=[st], writes=[w2k])
        kb.pool(lambda e, st=st: e.tensor_copy(out=w2k[:, 64:128], in_=st[:, 0:64]), reads=[st], writes=[w2k])
        kb.pool(lambda e, st=st: e.tensor_copy(out=w2v[:], in_=st[:, 64:128]), reads=[st], writes=[w2v])
        st = self.stage[self.stage_i % 2]
        self.stage_i += 1
        for hf in range(2):
            kb.dma(st[64 * hf:64 * hf + 64, 0:64].rearrange("p (a b) -> p a b", a=2), self.i_cpos[l], writes=[st])
        kb.pool(lambda e, st=st: e.tensor_copy(out=cpos[:], in_=st[:, 0:64].rearrange("p (a b) -> p a b", a=2)), reads=[st], writes=[cpos])
        kb.pool(lambda e: e.memset(vtok[:, :, :, :, 64:65], 1.0), writes=[vtok])
        for nt in range(2):
            for hk in range(2):
                kb.pool(lambda e, nt=nt, hk=hk: e.tensor_copy(out=vcaug[:, nt, hk, 64:128], in_=self.csel[:, nt, :]),
                        reads=[self.csel], writes=[vcaug])
        kb.barrier()
        hs = [self.tile([128, 8, 512], BF16, "hg") for _ in range(2)]
        css = [self.tile([128, 512], F32, "cs") for _ in range(2)]
        sns = [self.tile([128, 512], F32, "sn") for _ in range(2)]
        xf = [self.tile([128, 512], F32, "xf") for _ in range(2)]
        rot = [self.tile([128, 512], F32, "rot") for _ in range(2)]
        t1 = [self.tile([128, 512], F32, "t1") for _ in range(2)]
        cnt = [0]
        for g in range(NG):
            hg, cs, sn = hs[g % 2], css[g % 2], sns[g % 2]
            sl = slice(g * 512, (g + 1) * 512)
            kb.dma(hg[:], self.d_h[:, :, sl], writes=[hg])
            kb.dma(cs[:], self.d_cos[:, sl], writes=[cs])
            kb.dma(sn[:], self.d_sin[:, sl], writes=[sn])

            def consume(c, ps, sl=sl, cs=cs, sn=sn):
                i = cnt[0] % 2
                cnt[0] += 1
                if c < 4:
                    self.rope_chunk(ps, qT[:, c, sl], (qT, ("w", c, sl.start)), cs, sn, xf[i], rot[i], t1[i])
                elif c < 7:
                    self.rope_chunk(ps, kT[:, c - 4, sl], (kT, ("w", c, sl.start)), cs, sn, xf[i], rot[i], t1[i])
                else:
                    kb.act(lambda e: e.activation(out=v0T[:, sl], in_=ps[:], func=AF.Copy), reads=[ps], writes=[(v0T, sl.start)])
            self.proj_feature(wf, 8, hg, consume)
            for tt in range(4):
                ti = g * 4 + tt
                ps = self.bank()
                for k in range(8):
                    self.mm(ps[:, 0:280], hg[:, k, tt * 128:(tt + 1) * 128], wt[:, k, :], k == 0, k == 7, reads=[hg], writes=[ps])
                kb.act(lambda e, ti=ti, ps=ps: e.activation(out=vtok[:, ti, :, :, 0:64],
                                                            in_=ps[:, 0:256].rearrange("p (a b c) -> p a b c", a=2, b=2),
                                                            func=AF.Copy), reads=[ps], writes=[(vtok, ti)])
                kb.act(lambda e, ti=ti, ps=ps: e.activation(out=gates[:, ti, :], in_=ps[:, 256:280], func=AF.Sigmoid),
                       reads=[ps], writes=[(gates, ti)])
        kb.barrier()
        gel = self.tile([128, 256], BF16, "gel")
        hpre = self.tile([128, 256], F32, "hpre")
        hsq = self.tile([128, 256], F32, "hsq")
        bias = self.tile([128, 1], F32, "cbias")
        kb.pool(lambda e: e.memset(gel[:], 0.0), writes=[gel])
        for kv in range(2):
            psb = self.bank()
            for ll in range(32):
                self.mm(psb[:, 0:1], w1[0:64, kv, ll, :], cpos[0:64, kv, ll:ll + 1], ll == 0, ll == 31, reads=[w1, cpos], writes=[psb])
            kb.act(lambda e, psb=psb: e.activation(out=bias[:], in_=psb[:, 0:1], func=AF.Copy), reads=[psb], writes=[bias])
            src = kT if kv == 0 else v0T
            for hd in range(2):
                ps = self.bank()
                lo = 64 * hd
                for ll in range(32):
                    if kv == 0:
                        rhs = kT[lo:lo + 64, 0, ll:ll + 16 * 254 + 1:16]
                    else:
                        rhs = v0T[lo:lo + 64, ll:ll + 16 * 254 + 1:16]
                    self.mm(ps[:, 0:255], w1[lo:lo + 64, kv, ll, :], rhs, ll == 0, ll == 31, reads=[w1], writes=[ps])
                kb.act(lambda e, ps=ps: e.activation(out=hpre[:, 0:255], in_=ps[:, 0:255], func=AF.Identity, bias=bias[:]),
                       reads=[ps, bias], writes=[hpre])
                kb.dve(lambda e: e.tensor_tensor(out=hsq[:, 0:255], in0=hpre[:, 0:255], in1=hpre[:, 0:255], op=ALU.mult), reads=[hpre], writes=[hsq])
                kb.dve(lambda e: e.tensor_scalar(out=hsq[:, 0:255], in0=hsq[:, 0:255], scalar1=0.044715, scalar2=1.0, op0=ALU.mult, op1=ALU.add),
                       reads=[hsq], writes=[hsq])
                kb.dve(lambda e: e.tensor_tensor(out=hsq[:, 0:255], in0=hsq[:, 0:255], in1=hpre[:, 0:255], op=ALU.mult), reads=[hsq, hpre], writes=[hsq])
                kb.act(lambda e: e.activation(out=hsq[:, 0:255], in_=hsq[:, 0:255], func=AF.Sigmoid, scale=1.5957691216), reads=[hsq], writes=[hsq])
                kb.dve(lambda e: e.tensor_tensor(out=gel[:, 0:255], in0=hsq[:, 0:255], in1=hpre[:, 0:255], op=ALU.mult), reads=[hsq, hpre], writes=[gel])
                if kv == 0:
                    p2 = self.bank()
                    self.mm(p2[:, 0:256], w2k[:], gel[:], True, True, reads=[w2k, gel], writes=[p2])
                    kb.act(lambda e, p2=p2, lo=lo: e.activation(out=kc[lo:lo + 64, :], in_=p2[lo:lo + 64, 0:256], func=AF.Copy),
                           reads=[p2], writes=[kc])
                else:
                    for nt in range(2):
                        p2 = self.bank()
                        self.mm(p2[:, 0:64], gel[:, nt * 128:(nt + 1) * 128], w2v[:], True, True, reads=[w2v, gel], writes=[p2])
                        kb.act(lambda e, p2=p2, nt=nt, hd=hd: e.activation(out=vcaug[:, nt, hd, 0:64], in_=p2[:, 0:64], func=AF.Copy),
                               reads=[p2], writes=[vcaug])
        kb.barrier()
        self.release(m_w)
        kst = [self.tile([128, S], BF16, "kst") for _ in range(2)]
        kb.pool(lambda e: e.tensor_copy(out=kst[0][0:64, :], in_=kT[0:64, 1, :]), writes=[kst[0]])
        kb.act(lambda e: e.activation(out=kst[0][64:128, :], in_=self.ewide[64:128, :], func=AF.Copy), writes=[kst[0]])
        kb.act(lambda e: e.activation(out=kst[1][0:64, :], in_=self.ewide[0:64, :], func=AF.Copy), writes=[kst[1]])
        kb.pool(lambda e: e.tensor_copy(out=kst[1][64:128, :], in_=kT[64:128, 1, :]), writes=[kst[1]])
        NE = 8
        LA = 3
        nfv = [self.tile([128, 64], F32, "nfv") for _ in range(2)]
        add = [self.tile([128, 64], F32, "add") for _ in range(2)]
        eT = [self.tile([128, 512], BF16, "eT") for _ in range(NE)]
        oc = [self.tile([128, 2, 4, 128], F32, "oc") for _ in range(2)]
        osw = [[self.tile([128, 2, 4, 65], F32, "osw") for _ in range(2)] for _ in range(2)]
        den = [self.tile([128, 3, 8], F32, "den") for _ in range(2)]
        rden = [self.tile([128, 3, 8], F32, "rden") for _ in range(2)]
        imp = self.tile([128, 2, 64], F32, "imp")
        sc = self.tile([128, 2, 64], F32, "sc")
        wk = self.tile([128, 2, 64], F32, "wk")
        m8 = self.tile([128, 8], F32, "m8")
        nmk = self.tile([128, 2, 64], BF16, "nmk")
        qst = [[self.tile([128, 4, 128], BF16, "qst") for _ in range(2)] for _ in range(2)]
        y = self.tile([128, 8, 64], F32, "y")
        ytmp = self.tile([128, 8, 64], F32, "ytmp")
        yb = self.tile([128, 512], BF16, "yb16")
        yT = [self.tile([128, 4, 128], BF16, "yT") for _ in range(2)]
        fac = self.tile([128, 3, 8], F32, "fac")
        sbanks = self.pf[0:4]
        abanks = self.pf[4:7]
        cnt = {"s": 0, "e": 0, "a": 0}
        cur_po = {}
        units = []
        def add_units(i, kind):
            if kind == "cmp":
                kts = [0] if i < 16 else [0, 1]
            elif kind == "win":
                kts = list(range(max(0, i - 4), i + 1))
            else:
                kts = list(range(i + 1))
            for hk in range(2):
                for n_i, kt in enumerate(kts):
                    units.append(dict(i=i, kind=kind, hk=hk, kt=kt, first=(n_i == 0), last=(n_i == len(kts) - 1)))
        add_units(0, "cmp")
        add_units(0, "win")
        for i in range(NT):
            if i + 1 < NT:
                add_units(i + 1, "cmp")
            add_units(i, "sel")
            if i + 1 < NT:
                add_units(i + 1, "win")

        def score_fn(u):
            i, kind, hk, kt = u["i"], u["kind"], u["hk"], u["kt"]
            lo = 64 * hk
            qs = slice(i * 128, (i + 1) * 128)
            ps = sbanks[cnt["s"] % 4]
            cnt["s"] += 1
            if kind == "cmp":
                if hk == 0 and u["first"]:
                    kb.dma(nfv[i % 2][:], self.i_nfv[i], writes=[nfv[i % 2]])
                    kb.dma(add[i % 2][:], self.i_add[i], writes=[add[i % 2]])
                self.mm(ps[:], kc[lo:lo + 64, kt * 128:(kt + 1) * 128], qT[lo:lo + 64, :, qs], True, True, writes=[ps])
            else:
                br = 0 if kind == "sel" else 1
                ks = slice(kt * 128, (kt + 1) * 128)
                extra = []
                if br == 0:
                    extra.append("sel")
                if kt == i:
                    extra.append("D")
                if br == 1 and kt == i - 4:
                    extra.append("U")
                if br == 0:
                    extra.remove("sel")
                    qs_ = qst[i % 2][hk]
                    self.mm(ps[:], kst[hk][:, ks], qs_[:], True, len(extra) == 0, reads=[qs_], writes=[ps])
                else:
                    self.mm(ps[:], kT[lo:lo + 64, 1 + br, ks], qT[lo:lo + 64, :, qs], True, len(extra) == 0, writes=[ps])
                for xi, kd in enumerate(extra):
                    last = xi == len(extra) - 1
                    if kd == "D":
                        self.mm(ps[:], self.ident[:], self.maskD[:], False, last, writes=[ps])
                    else:
                        self.mm(ps[:], self.ident[:], self.maskU[:], False, last, writes=[ps])
            e_ = eT[cnt["e"] % NE]
            cnt["e"] += 1
            kb.act(lambda e: e.activation(out=e_[:], in_=ps[:], func=AF.Exp, scale=0.125), reads=[ps], writes=[e_])
            if kind == "cmp":
                kb.pool(lambda e: e.affine_select(
                    out=e_[:].rearrange("p (a n) -> p a n", a=4), in_=e_[:].rearrange("p (a n) -> p a n", a=4),
                    pattern=[[0, 4], [1, 128]], compare_op=ALU.is_ge, fill=0.0,
                    base=128 * i - 31 - 16 * 128 * kt, channel_multiplier=-16), reads=[e_], writes=[e_])
            u["e"] = e_

        def chain(i):
            oc_, den_, rden_ = oc[i % 2], den[i % 2], rden[i % 2]
            nf, ad = nfv[i % 2], add[i % 2]
            kb.dve(lambda e: e.tensor_reduce(out=den_[:, 0, :], in_=oc_[:, :, :, 64:128].rearrange("p a b c -> p (a b) c"), axis=AX.X, op=ALU.add),
                   reads=[oc_], writes=[den_])
            kb.dve(lambda e: e.tensor_scalar(out=rden_[:, 0, :], in0=den_[:, 0, :], scalar1=1e-30, scalar2=None, op0=ALU.max), reads=[den_], writes=[rden_])
            kb.dve(lambda e: e.reciprocal(out=rden_[:, 0, :], in_=rden_[:, 0, :]), reads=[rden_], writes=[rden_])
            for hk in range(2):
                for g4 in range(4):
                    hd = hk * 4 + g4
                    if g4 == 0:
                        kb.dve(lambda e, hk=hk, hd=hd: e.tensor_scalar(out=imp[:, hk, :], in0=oc_[:, hk, 0, 64:128], scalar1=rden_[:, 0, hd:hd + 1],
                                                                       scalar2=None, op0=ALU.mult), reads=[oc_, rden_], writes=[imp])
                    else:
                        kb.dve(lambda e, hk=hk, hd=hd, g4=g4: e.scalar_tensor_tensor(out=imp[:, hk, :], in0=oc_[:, hk, g4, 64:128],
                                                                                     scalar=rden_[:, 0, hd:hd + 1], in1=imp[:, hk, :],
                                                                                     op0=ALU.mult, op1=ALU.add), reads=[oc_, rden_, imp], writes=[imp])
            for hk in range(2):
                kb.dve(lambda e, hk=hk: e.tensor_tensor(out=sc[:, hk, :], in0=imp[:, hk, :], in1=nf[:], op=ALU.mult), reads=[imp, nf], writes=[sc])
                kb.dve(lambda e, hk=hk: e.tensor_tensor(out=sc[:, hk, :], in0=sc[:, hk, :], in1=ad[:], op=ALU.add), reads=[sc, ad], writes=[sc])
                kb.dve(lambda e, hk=hk: e.max(out=m8[:], in_=sc[:, hk, :]), reads=[sc], writes=[m8])
                kb.dve(lambda e, hk=hk: e.match_replace(out=wk[:, hk, :], in_to_replace=m8[:], in_values=sc[:, hk, :], imm_value=-3e38),
                       reads=[sc, m8], writes=[wk])
                kb.dve(lambda e, hk=hk: e.max(out=m8[:], in_=wk[:, hk, :]), reads=[wk], writes=[m8])
                kb.dve(lambda e, hk=hk: e.match_replace(out=wk[:, hk, :], in_to_replace=m8[:], in_values=wk[:, hk, :], imm_value=-3e38),
                       reads=[wk, m8], writes=[wk])
                kb.dve(lambda e, hk=hk: e.tensor_tensor(out=wk[:, hk, :], in0=wk[:, hk, :], in1=sc[:, hk, :], op=ALU.is_equal), reads=[wk, sc], writes=[wk])
                kb.dve(lambda e, hk=hk: e.tensor_scalar(out=nmk[:, 1 - hk, :], in0=wk[:, hk, :], scalar1=NEG, scalar2=None, op0=ALU.mult), reads=[wk], writes=[nmk])
            kb.pe(lambda e: e.transpose(self.pb[:, 0:128], nmk[:].rearrange("p a b -> p (a b)"), self.ident[:]), reads=[nmk, self.ident], writes=[self.pb])
            qs = slice(i * 128, (i + 1) * 128)
            for hk in range(2):
                lo = 64 * hk
                mo = 64 * (1 - hk)
                qs_ = qst[i % 2][hk]
                kb.pool(lambda e, lo=lo, qs_=qs_: e.tensor_copy(out=qs_[lo:lo + 64, :, :], in_=qT[lo:lo + 64, :, qs]), writes=[qs_])
                kb.act(lambda e, mo=mo, qs_=qs_: e.activation(out=qs_[mo:mo + 64, :, :],
                                                              in_=self.pb[mo:mo + 64, 0:128].rearrange("p (o n) -> p o n", o=1).broadcast_to([64, 4, 128]),
                                                              func=AF.Copy), reads=[self.pb], writes=[qs_])

        def combine(i):
            oc_, den_, rden_, osw_ = oc[i % 2], den[i % 2], rden[i % 2], osw[i % 2]
            qs = slice(i * 128, (i + 1) * 128)
            for br in range(2):
                kb.dve(lambda e, br=br: e.tensor_copy(out=den_[:, 1 + br, :], in_=osw_[br][:, :, :, 64:65].rearrange("p a b c -> p (a b c)")),
                       reads=[osw_[br]], writes=[den_])
            kb.dve(lambda e: e.reciprocal(out=rden_[:, 1:3, :], in_=den_[:, 1:3, :]), reads=[den_], writes=[rden_])
            kb.dve(lambda e: e.tensor_tensor(out=fac[:], in0=rden_[:], in1=gates[:, i, :].rearrange("p (a b) -> p a b", a=3), op=ALU.mult),
                   reads=[rden_, gates], writes=[fac])
            kb.dve(lambda e: e.tensor_tensor(out=y[:], in0=oc_[:, :, :, 0:64].rearrange("p a b c -> p (a b) c"),
                                             in1=fac[:, 0, :].rearrange("p (a o) -> p a o", o=1).broadcast_to([128, 8, 64]), op=ALU.mult),
                   reads=[oc_, fac], writes=[y])
            for br in range(2):
                kb.dve(lambda e, br=br: e.tensor_tensor(out=ytmp[:], in0=osw_[br][:, :, :, 0:64].rearrange("p a b c -> p (a b) c"),
                                                        in1=fac[:, 1 + br, :].rearrange("p (a o) -> p a o", o=1).broadcast_to([128, 8, 64]), op=ALU.mult),
                       reads=[osw_[br], fac], writes=[ytmp])
                if br == 0:
                    kb.dve(lambda e: e.tensor_tensor(out=y[:], in0=y[:], in1=ytmp[:], op=ALU.add), reads=[y, ytmp], writes=[y])
                else:
                    kb.dve(lambda e: e.tensor_tensor(out=yb[:].rearrange("p (a b) -> p a b", a=8), in0=y[:], in1=ytmp[:], op=ALU.add),
                           reads=[y, ytmp], writes=[yb])
            self.emit_yT(yb, yT[i % 2], self.d_ya, qs)

        def pv_fn(u):
            i, kind, hk, kt = u["i"], u["kind"], u["hk"], u["kt"]
            e_ = u["e"]
            key = (kind, hk)
            if u["first"]:
                po = abanks[cnt["a"] % 3]
                cnt["a"] += 1
                cur_po[key] = po
                self.mm(po[:], self.zeros[:, 0:128], self.zeros[:], True, False, reads=[self.zeros], writes=[po])
            po = cur_po[key]
            fin = u["last"]
            if kind == "cmp":
                for g4 in range(4):
                    self.mm(po[:, g4 * 128:(g4 + 1) * 128], e_[:, g4 * 128:(g4 + 1) * 128], vcaug[:, kt, hk, :], False, (fin and g4 == 3),
                            reads=[e_], writes=[po])
                if fin:
                    oc_ = oc[i % 2]
                    kb.act(lambda e: e.activation(out=oc_[:, hk, :, :], in_=po[:].rearrange("p (a b) -> p a b", a=4), func=AF.Copy),
                           reads=[po], writes=[oc_])
                    if hk == 1:
                        chain(i)
            else:
                br = 0 if kind == "sel" else 1
                for g4 in range(4):
                    self.mm(po[:, g4 * 65:(g4 + 1) * 65], e_[:, g4 * 128:(g4 + 1) * 128], vtok[:, kt, br, hk, :], False, (fin and g4 == 3),
                            reads=[e_], writes=[po])
                if fin:
                    o_ = osw[i % 2][br]
                    kb.act(lambda e: e.activation(out=o_[:, hk, :, :], in_=po[:, 0:260].rearrange("p (a b) -> p a b", a=4), func=AF.Copy),
                           reads=[po], writes=[o_])
                    if kind == "sel" and hk == 1:
                        combine(i)

        for idx in range(len(units) + LA):
            if idx - LA >= 0:
                pv_fn(units[idx - LA])
            if idx < len(units):
                score_fn(units[idx])

    def emit_yT(self, yb, yT, dst, qs):
        kb = self.kb
        pbh = self.pb
        for c in range(4):
            kb.pe(lambda e, c=c: e.transpose(self.pb[:, 512 + c * 128:512 + (c + 1) * 128], yb[:, c * 128:(c + 1) * 128], self.ident[:]),
                  reads=[yb, self.ident], writes=[pbh])
        kb.act(lambda e: e.activation(out=yT[:], in_=self.pb[:, 512:1024].rearrange("p (a b) -> p a b", a=4), func=AF.Copy), reads=[pbh], writes=[yT])
        kb.dma(dst[:, :, qs], yT[:], reads=[yT])

    def ph_swa(self, l):
        kb = self.kb
        wf = self.tile([128, 8, 640], BF16, "wsf")
        wt = self.tile([128, 8, 64], BF16, "wst")
        qT = self.tile([128, 4, S], BF16, "qTb")
        kT = self.tile([128, S], BF16, "kTb")
        vtok = self.tile([128, NT, 65], BF16, "vtokb")
        esink = self.tile([128, 8], F32, "esink")
        m_w = self.mark()
        self.setup_stage(5120)
        self.load_w_bf16(wf, wf[:, :, :], self.i_wsf[l].rearrange("(k p) c -> p k c", p=128), [128, 8, 640])
        self.load_w_bf16(wt, wt[:, :, :], self.i_wst[l].rearrange("(k p) c -> p k c", p=128), [128, 8, 64])
        kb.dma(esink[:], self.i_sinks[l, 0:1, :].broadcast_to([128, 8]), writes=[esink])
        kb.act(lambda e: e.activation(out=esink[:], in_=esink[:], func=AF.Exp), reads=[esink], writes=[esink])
        kb.pool(lambda e: e.memset(vtok[:, :, 64:65], 1.0), writes=[vtok])
        kb.barrier()
        hs = [self.tile([128, 8, 512], BF16, "hg") for _ in range(2)]
        css = [self.tile([128, 512], F32, "cs") for _ in range(2)]
        sns = [self.tile([128, 512], F32, "sn") for _ in range(2)]
        xf = [self.tile([128, 512], F32, "xf") for _ in range(2)]
        rot = [self.tile([128, 512], F32, "rot") for _ in range(2)]
        t1 = [self.tile([128, 512], F32, "t1") for _ in range(2)]
        cnt = [0]
        for g in range(NG):
            hg, cs, sn = hs[g % 2], css[g % 2], sns[g % 2]
            sl = slice(g * 512, (g + 1) * 512)
            kb.dma(hg[:], self.d_h[:, :, sl], writes=[hg])
            kb.dma(cs[:], self.d_cos[:, sl], writes=[cs])
            kb.dma(sn[:], self.d_sin[:, sl], writes=[sn])

            def consume(c, ps, sl=sl, cs=cs, sn=sn):
                i = cnt[0] % 2
                cnt[0] += 1
                if c < 4:
                    self.rope_chunk(ps, qT[:, c, sl], (qT, ("w", c, sl.start)), cs, sn, xf[i], rot[i], t1[i])
                else:
                    self.rope_chunk(ps, kT[:, sl], (kT, ("w", sl.start)), cs, sn, xf[i], rot[i], t1[i])
            self.proj_feature(wf, 5, hg, consume)
            for tt in range(4):
                ti = g * 4 + tt
                ps = self.bank()
                for k in range(8):
                    self.mm(ps[:, 0:64], hg[:, k, tt * 128:(tt + 1) * 128], wt[:, k, :], k == 0, k == 7, reads=[hg], writes=[ps])
                kb.act(lambda e, ti=ti, ps=ps: e.activation(out=vtok[:, ti, 0:64], in_=ps[:, 0:64], func=AF.Copy), reads=[ps], writes=[(vtok, ti)])
        kb.barrier()
        self.release(m_w)
        NE = 8
        LA = 3
        eT = [self.tile([128, 512], BF16, "eT") for _ in range(NE)]
        ob = [self.tile([128, 2, 4, 65], F32, "ob") for _ in range(2)]
        den = self.tile([128, 2, 4], F32, "denb")
        yb = self.tile([128, 512], BF16, "yb16")
        yT = [self.tile([128, 4, 128], BF16, "yT") for _ in range(2)]
        sbanks = self.pf[0:4]
        abanks = self.pf[4:7]
        cnt = {"s": 0, "e": 0, "a": 0}
        cur_po = {}
        units = []
        for i in range(NT):
            kts = [kt for kt in (i - 1, i) if kt >= 0]
            for hf in range(2):
                for n_i, kt in enumerate(kts):
                    units.append(dict(i=i, hf=hf, kt=kt, first=(n_i == 0), last=(n_i == len(kts) - 1)))

        def score_fn(u):
            i, hf, kt = u["i"], u["hf"], u["kt"]
            lo = 64 * hf
            qs = slice(i * 128, (i + 1) * 128)
            ks = slice(kt * 128, (kt + 1) * 128)
            ps = sbanks[cnt["s"] % 4]
            cnt["s"] += 1
            self.mm(ps[:], kT[lo:lo + 64, ks], qT[lo:lo + 64, :, qs], True, False, writes=[ps])
            self.mm(ps[:], self.ident[:], (self.maskD if kt == i else self.maskU)[:], False, True, writes=[ps])
            e_ = eT[cnt["e"] % NE]
            cnt["e"] += 1
            kb.act(lambda e: e.activation(out=e_[:], in_=ps[:], func=AF.Exp, scale=0.125), reads=[ps], writes=[e_])
            u["e"] = e_

        def finish(i):
            ob_ = ob[i % 2]
            qs = slice(i * 128, (i + 1) * 128)
            kb.dve(lambda e: e.tensor_tensor(out=den[:], in0=ob_[:, :, :, 64:65].rearrange("p a b c -> p a (b c)"),
                                             in1=esink[:].rearrange("p (c h) -> p h c", h=2), op=ALU.add), reads=[ob_, esink], writes=[den])
            kb.dve(lambda e: e.reciprocal(out=den[:], in_=den[:]), reads=[den], writes=[den])
            kb.dve(lambda e: e.tensor_tensor(out=yb[:].rearrange("p (c h d) -> p h c d", c=4, h=2), in0=ob_[:, :, :, 0:64],
                                             in1=den[:].rearrange("p a (b o) -> p a b o", o=1).broadcast_to([128, 2, 4, 64]), op=ALU.mult),
                   reads=[ob_, den], writes=[yb])
            self.emit_yT(yb, yT[i % 2], self.d_yb, qs)

        def pv_fn(u):
            i, hf, kt = u["i"], u["hf"], u["kt"]
            e_ = u["e"]
            if u["first"]:
                po = abanks[cnt["a"] % 3]
                cnt["a"] += 1
                cur_po[hf] = po
                self.mm(po[:], self.zeros[:, 0:128], self.zeros[:], True, False, reads=[self.zeros], writes=[po])
            po = cur_po[hf]
            fin = u["last"]
            for c in range(4):
                self.mm(po[:, c * 65:(c + 1) * 65], e_[:, c * 128:(c + 1) * 128], vtok[:, kt, :], False, (fin and c == 3),
                        reads=[e_], writes=[po])
            if fin:
                ob_ = ob[i % 2]
                kb.act(lambda e: e.activation(out=ob_[:, hf, :, :], in_=po[:, 0:260].rearrange("p (a b) -> p a b", a=4), func=AF.Copy),
                       reads=[po], writes=[ob_])
                if hf == 1:
                    finish(i)

        for idx in range(len(units) + LA):
            if idx - LA >= 0:
                pv_fn(units[idx - LA])
            if idx < len(units):
                score_fn(units[idx])

    def ph_s5(self, l):
        kb = self.kb
        NP = 16
        uT = self.tile([128, 4, S], BF16, "uT")
        bbar = self.tile([128, NP, 2, 128], BF16, "bbar")
        pwT = self.tile([128, 17, 3, NP], F32, "pwT")
        par = self.tile([128, 3, NP], F32, "s5par")
        dsk = self.tile([128, 2, 4], F32, "s5d")
        bT = self.tile([128, NP, 2, 128], BF16, "bT")
        cw = self.tile([128, NP, 2, 128], BF16, "cw")
        glu = self.tile([128, 4, 512], BF16, "glu")
        pw = self.tile([128, 12, 3, NP], F32, "pw")
        m_w = self.mark()
        bw = self.tile([128, NP, 2, 128], F32, "bw")
        wu = self.tile([128, 8, 512], BF16, "wu")
        self.setup_stage(4096)
        self.load_w_bf16(wu, wu[:, :, :], self.i_wu[l].rearrange("(k p) c -> p k c", p=128), [128, 8, 512])
        self.load_w_bf16(glu, glu[:, :, :], self.i_glu[l].rearrange("(k p) c -> p k c", p=128), [128, 4, 512])
        kb.dma(par[:], self.i_s5a[l], writes=[par])
        kb.dma(dsk[:], self.i_s5d[l], writes=[dsk])
        kb.dma(bw[:], self.i_s5b[l], writes=[bw])
        st = self.stage[self.stage_i % 2]
        self.stage_i += 1
        sv = st[:, 0:4096].rearrange("p (a b c) -> p a b c", a=NP, b=2)
        kb.dma(sv, self.i_s5c[l], writes=[st])
        kb.pool(lambda e: e.tensor_copy(out=cw[:, :, 0, :], in_=sv[:, :, 0, :]), reads=[st], writes=[cw])
        kb.pool(lambda e: e.tensor_scalar(out=cw[:, :, 1, :], in0=sv[:, :, 1, :], scalar1=-1.0, scalar2=None, op0=ALU.mult), reads=[st], writes=[cw])
        import os as _os
        sub = int(_os.environ.get("S5SUB", "99"))
        if sub <= 0:
            return
        def t16(name):
            return self.tile([128, NP], F32, name)
        dt, mag, phi, ar, ai, kf, r, msk, cr, ci, den, nr, t_a, t_b = [t16("s5_%d" % j) for j in range(14)]
        ki = self.tile([128, NP], I32, "s5ki")
        A_re, A_im = par[:, 0, :], par[:, 1, :]
        V = kb.dve
        V(lambda e: e.tensor_copy(out=dt[:], in_=par[:, 2, :]), reads=[par], writes=[dt])
        kb.act(lambda e: e.activation(out=dt[:], in_=dt[:], func=AF.Exp), reads=[dt], writes=[dt])
        V(lambda e: e.tensor_tensor(out=mag[:], in0=dt[:], in1=A_re, op=ALU.mult), reads=[dt, par], writes=[mag])
        kb.act(lambda e: e.activation(out=mag[:], in_=mag[:], func=AF.Exp), reads=[mag], writes=[mag])
        V(lambda e: e.tensor_tensor(out=phi[:], in0=dt[:], in1=A_im, op=ALU.mult), reads=[dt, par], writes=[phi])
        HI = 6.28125
        LO = 2.0 * math.pi - HI

        def sin_of(dst, src_t, shift):
            V(lambda e: e.tensor_scalar(out=t_a[:], in0=src_t[:], scalar1=shift, scalar2=None, op0=ALU.add), reads=[src_t], writes=[t_a])
            V(lambda e: e.tensor_scalar(out=ki[:], in0=t_a[:], scalar1=1.0 / (2 * math.pi), scalar2=None, op0=ALU.mult), reads=[t_a], writes=[ki])
            V(lambda e: e.tensor_copy(out=kf[:], in_=ki[:]), reads=[ki], writes=[kf])
            V(lambda e: e.scalar_tensor_tensor(out=r[:], in0=kf[:], scalar=-HI, in1=t_a[:], op0=ALU.mult, op1=ALU.add), reads=[kf, t_a], writes=[r])
            V(lambda e: e.scalar_tensor_tensor(out=r[:], in0=kf[:], scalar=-LO, in1=r[:], op0=ALU.mult, op1=ALU.add), reads=[kf, r], writes=[r])
            V(lambda e: e.tensor_scalar(out=msk[:], in0=r[:], scalar1=math.pi, scalar2=None, op0=ALU.is_gt), reads=[r], writes=[msk])
            V(lambda e: e.scalar_tensor_tensor(out=r[:], in0=msk[:], scalar=-2 * math.pi, in1=r[:], op0=ALU.mult, op1=ALU.add), reads=[msk, r], writes=[r])
            V(lambda e: e.tensor_scalar(out=msk[:], in0=r[:], scalar1=-math.pi, scalar2=None, op0=ALU.is_lt), reads=[r], writes=[msk])
            V(lambda e: e.scalar_tensor_tensor(out=r[:], in0=msk[:], scalar=2 * math.pi, in1=r[:], op0=ALU.mult, op1=ALU.add), reads=[msk, r], writes=[r])
            V(lambda e: e.tensor_scalar(out=r[:], in0=r[:], scalar1=3.1415925, scalar2=-3.1415925, op0=ALU.min, op1=ALU.max), reads=[r], writes=[r])
            kb.act(lambda e: e.activation(out=dst[:], in_=r[:], func=AF.Sin), reads=[r], writes=[dst])
        if sub <= 1:
            return
        sin_of(ai, phi, 0.0)
        sin_of(ar, phi, math.pi / 2)
        if sub <= 2:
            return
        V(lambda e: e.tensor_tensor(out=ar[:], in0=ar[:], in1=mag[:], op=ALU.mult), reads=[ar, mag], writes=[ar])
        V(lambda e: e.tensor_tensor(out=ai[:], in0=ai[:], in1=mag[:], op=ALU.mult), reads=[ai, mag], writes=[ai])
        V(lambda e: e.tensor_scalar(out=nr[:], in0=ar[:], scalar1=-1.0, scalar2=None, op0=ALU.add), reads=[ar], writes=[nr])
        V(lambda e: e.tensor_tensor(out=den[:], in0=A_re, in1=A_re, op=ALU.mult), reads=[par], writes=[den])
        V(lambda e: e.tensor_tensor(out=t_a[:], in0=A_im, in1=A_im, op=ALU.mult), reads=[par], writes=[t_a])
        V(lambda e: e.tensor_tensor(out=den[:], in0=den[:], in1=t_a[:], op=ALU.add), reads=[den, t_a], writes=[den])
        V(lambda e: e.reciprocal(out=den[:], in_=den[:]), reads=[den], writes=[den])
        V(lambda e: e.tensor_tensor(out=cr[:], in0=nr[:], in1=A_re, op=ALU.mult), reads=[nr, par], writes=[cr])
        V(lambda e: e.tensor_tensor(out=t_a[:], in0=ai[:], in1=A_im, op=ALU.mult), reads=[ai, par], writes=[t_a])
        V(lambda e: e.tensor_tensor(out=cr[:], in0=cr[:], in1=t_a[:], op=ALU.add), reads=[cr, t_a], writes=[cr])
        V(lambda e: e.tensor_tensor(out=cr[:], in0=cr[:], in1=den[:], op=ALU.mult), reads=[cr, den], writes=[cr])
        V(lambda e: e.tensor_tensor(out=ci[:], in0=ai[:], in1=A_re, op=ALU.mult), reads=[ai, par], writes=[ci])
        V(lambda e: e.tensor_tensor(out=t_a[:], in0=nr[:], in1=A_im, op=ALU.mult), reads=[nr, par], writes=[t_a])
        V(lambda e: e.tensor_tensor(out=ci[:], in0=ci[:], in1=t_a[:], op=ALU.subtract), reads=[ci, t_a], writes=[ci])
        V(lambda e: e.tensor_tensor(out=ci[:], in0=ci[:], in1=den[:], op=ALU.mult), reads=[ci, den], writes=[ci])
        if sub <= 3:
            return
        tb1 = self.tile([128, NP, 128], F32, "tb1")
        tb2 = self.tile([128, NP, 128], F32, "tb2")

        def bc(tl):
            return tl[:].rearrange("p (a o) -> p a o", o=1).broadcast_to([128, NP, 128])
        V(lambda e: e.tensor_tensor(out=tb1[:], in0=bw[:, :, 0, :], in1=bc(cr), op=ALU.mult), reads=[bw, cr], writes=[tb1])
        V(lambda e: e.tensor_tensor(out=tb2[:], in0=bw[:, :, 1, :], in1=bc(ci), op=ALU.mult), reads=[bw, ci], writes=[tb2])
        V(lambda e: e.tensor_tensor(out=bbar[:, :, 0, :], in0=tb1[:], in1=tb2[:], op=ALU.subtract), reads=[tb1, tb2], writes=[bbar])
        V(lambda e: e.tensor_tensor(out=tb1[:], in0=bw[:, :, 1, :], in1=bc(cr), op=ALU.mult), reads=[bw, cr], writes=[tb1])
        V(lambda e: e.tensor_tensor(out=tb2[:], in0=bw[:, :, 0, :], in1=bc(ci), op=ALU.mult), reads=[bw, ci], writes=[tb2])
        V(lambda e: e.tensor_tensor(out=bbar[:, :, 1, :], in0=tb1[:], in1=tb2[:], op=ALU.add), reads=[tb1, tb2], writes=[bbar])
        if sub <= 4:
            return
        for P in range(NP):
            for c2 in range(2):
                var = int(_os.environ.get("S5VAR", "0"))
                j = (P * 2 + c2) % 4
                if var == 1:
                    j = 0
                if var == 2:
                    j = 4 + (P * 2 + c2) % 4
                pk = self.pb
                kb.pe(lambda e, P=P, c2=c2, j=j: e.transpose(self.pb[:, j * 128:(j + 1) * 128], bbar[:, P, c2, :], self.ident[:]),
                      reads=[bbar, self.ident], writes=[pk])
                kb.act(lambda e, P=P, c2=c2, j=j: e.activation(out=bT[:, P, c2, :], in_=self.pb[:, j * 128:(j + 1) * 128], func=AF.Copy),
                       reads=[pk], writes=[bT])
        if sub <= 5:
            return
        V(lambda e: e.tensor_copy(out=pw[:, 0, 0, :], in_=ar[:]), reads=[ar], writes=[pw])
        V(lambda e: e.tensor_copy(out=pw[:, 0, 1, :], in_=ai[:]), reads=[ai], writes=[pw])
        for d_ in range(1, 12):
            pr, pi_ = pw[:, d_ - 1, 0, :], pw[:, d_ - 1, 1, :]
            V(lambda e, pr=pr: e.tensor_tensor(out=t_a[:], in0=pr, in1=pr, op=ALU.mult), reads=[pw], writes=[t_a])
            V(lambda e, pi_=pi_: e.tensor_tensor(out=t_b[:], in0=pi_, in1=pi_, op=ALU.mult), reads=[pw], writes=[t_b])
            V(lambda e, d_=d_: e.tensor_tensor(out=pw[:, d_, 0, :], in0=t_a[:], in1=t_b[:], op=ALU.subtract), reads=[t_a, t_b], writes=[pw])
            V(lambda e, pr=pr, pi_=pi_: e.tensor_tensor(out=t_a[:], in0=pr, in1=pi_, op=ALU.mult), reads=[pw], writes=[t_a])
            V(lambda e, d_=d_: e.tensor_scalar(out=pw[:, d_, 1, :], in0=t_a[:], scalar1=2.0, scalar2=None, op0=ALU.mult), reads=[t_a], writes=[pw])
        V(lambda e: e.tensor_scalar(out=pw[:, :, 2, :], in0=pw[:, :, 1, :], scalar1=-1.0, scalar2=None, op0=ALU.mult), reads=[pw], writes=[pw])
        V(lambda e: e.memset(pwT[:, 0, 0, :], 1.0), writes=[pwT])
        V(lambda e: e.memset(pwT[:, 0, 1, :], 0.0), writes=[pwT])
        for tau in range(1, 17):
            pr, pi_ = pwT[:, tau - 1, 0, :], pwT[:, tau - 1, 1, :]
            V(lambda e, pr=pr: e.tensor_tensor(out=t_a[:], in0=pr, in1=ar[:], op=ALU.mult), reads=[pwT, ar], writes=[t_a])
            V(lambda e, pi_=pi_: e.tensor_tensor(out=t_b[:], in0=pi_, in1=ai[:], op=ALU.mult), reads=[pwT, ai], writes=[t_b])
            V(lambda e, tau=tau: e.tensor_tensor(out=pwT[:, tau, 0, :], in0=t_a[:], in1=t_b[:], op=ALU.subtract), reads=[t_a, t_b], writes=[pwT])
            V(lambda e, pr=pr: e.tensor_tensor(out=t_a[:], in0=pr, in1=ai[:], op=ALU.mult), reads=[pwT, ai], writes=[t_a])
            V(lambda e, pi_=pi_: e.tensor_tensor(out=t_b[:], in0=pi_, in1=ar[:], op=ALU.mult), reads=[pwT, ar], writes=[t_b])
            V(lambda e, tau=tau: e.tensor_tensor(out=pwT[:, tau, 1, :], in0=t_a[:], in1=t_b[:], op=ALU.add), reads=[t_a, t_b], writes=[pwT])
        V(lambda e: e.tensor_scalar(out=pwT[:, :, 2, :], in0=pwT[:, :, 1, :], scalar1=-1.0, scalar2=None, op0=ALU.mult), reads=[pwT], writes=[pwT])
        kb.barrier()
        hs = [self.tile([128, 8, 512], BF16, "hg") for _ in range(2)]
        for g in range(NG):
            hg = hs[g % 2]
            sl = slice(g * 512, (g + 1) * 512)
            kb.dma(hg[:], self.d_h[:, :, sl], writes=[hg])

            def consume(c, ps, sl=sl):
                kb.act(lambda e: e.activation(out=uT[:, c, sl], in_=ps[:], func=AF.Copy), reads=[ps], writes=[(uT, (c, sl.start))])
            self.proj_feature(wu, 4, hg, consume)
        kb.barrier()
        self.release(m_w)
        TC = 16
        NC = S // TC
        Bu = [self.tile([128, TC, NC], F32, "Bur"), self.tile([128, TC, NC], F32, "Bui")]
        upm = self.tile([128, TC, NC], BF16, "upm")
        SA = [self.tile([128, NC], F32, "SAr"), self.tile([128, NC], F32, "SAi")]
        SB = [self.tile([128, NC], F32, "SBr"), self.tile([128, NC], F32, "SBi")]
        Xb = [self.tile([128, 2, NC + 1], BF16, "Xb") for _ in range(2)]
        obs = [self.tile([128, 17, 2, 128], BF16, "obs") for _ in range(2)]
        tm1 = self.tile([128, 17, 128], F32, "tm1")
        tm2 = self.tile([128, 17, 128], F32, "tm2")
        Kacc = self.tile([128, 16, 128], F32, "Kacc")
        Ksb = self.tile([128, 16, 128], BF16, "Ksb")
        yc = self.tile([128, S], F32, "ycf")
        ztc = self.tile([128, S], BF16, "ztc")
        tgl = Bu[0]
        tglf = Bu[0][:].rearrange("p t c -> p (t c)")
        for xb_ in Xb:
            kb.pool(lambda e, xb_=xb_: e.memset(xb_[:, :, 0:1], 0.0), writes=[xb_])

        def bc_c(ap2):
            return ap2.rearrange("p (o c) -> p o c", o=1).broadcast_to([128, 17, 128])

        def bc_t(ap2):
            return ap2.rearrange("p (t o) -> p t o", o=1).broadcast_to([128, 17, 128])
        ycv = yc[:].rearrange("p (t c) -> p t c", t=TC)
        ztv = ztc[:].rearrange("p (c t) -> p t c", t=TC)
        for ch in range(4):
            kb.pool(lambda e, ch=ch: e.tensor_copy(out=upm[:], in_=uT[:, ch, :].rearrange("p (c t) -> p t c", t=TC)), writes=[upm])
            kb.act(lambda e, ch=ch: e.activation(out=ycv, in_=upm[:], func=AF.Copy, scale=dsk[:, 0, ch:ch + 1]),
                   reads=[dsk, upm], writes=[yc])
            for pq in range(4):
                P = ch * 4 + pq
                xb_ = Xb[P % 2]
                ob_ = obs[P % 2]
                for c2 in range(2):
                    for g in range(NG):
                        ps = self.bank()
                        self.mm(ps[:], bT[:, P, c2, :], upm[:, 2 * g:2 * g + 2, :], True, True, reads=[upm], writes=[ps])
                        kb.act(lambda e, ps=ps, c2=c2, g=g: e.activation(out=Bu[c2][:, 2 * g:2 * g + 2, :], in_=ps[:].rearrange("p (a b) -> p a b", a=2),
                                                                         func=AF.Copy), reads=[ps], writes=[(Bu[c2], g)])
                for j in range(TC):
                    tau = TC - 1 - j
                    s_r = pwT[:, tau, 0, P:P + 1]
                    s_i = pwT[:, tau, 1, P:P + 1]
                    xr, xi = Bu[0][:, j, :], Bu[1][:, j, :]
                    if j == 0:
                        V(lambda e, s_r=s_r, xr=xr: e.tensor_scalar(out=SA[0][:], in0=xr, scalar1=s_r, scalar2=None, op0=ALU.mult), reads=[Bu[0], pwT], writes=[SA[0]])
                        V(lambda e, s_r=s_r, xi=xi: e.tensor_scalar(out=SA[1][:], in0=xi, scalar1=s_r, scalar2=None, op0=ALU.mult), reads=[Bu[1], pwT], writes=[SA[1]])
                    else:
                        V(lambda e, s_r=s_r, xr=xr: e.scalar_tensor_tensor(out=SA[0][:], in0=xr, scalar=s_r, in1=SA[0][:], op0=ALU.mult, op1=ALU.add),
                          reads=[Bu[0], pwT, SA[0]], writes=[SA[0]])
                        V(lambda e, s_r=s_r, xi=xi: e.scalar_tensor_tensor(out=SA[1][:], in0=xi, scalar=s_r, in1=SA[1][:], op0=ALU.mult, op1=ALU.add),
                          reads=[Bu[1], pwT, SA[1]], writes=[SA[1]])
                    s_ni = pwT[:, tau, 2, P:P + 1]
                    V(lambda e, s_ni=s_ni, xi=xi: e.scalar_tensor_tensor(out=SA[0][:], in0=xi, scalar=s_ni, in1=SA[0][:], op0=ALU.mult, op1=ALU.add),
                      reads=[Bu[1], pwT, SA[0]], writes=[SA[0]])
                    V(lambda e, s_i=s_i, xr=xr: e.scalar_tensor_tensor(out=SA[1][:], in0=xr, scalar=s_i, in1=SA[1][:], op0=ALU.mult, op1=ALU.add),
                      reads=[Bu[0], pwT, SA[1]], writes=[SA[1]])
                A, B = SA, SB
                for d_ in range(8):
                    sh = 1 << d_
                    s_ar, s_ai, s_nai = pw[:, 4 + d_, 0, P:P + 1], pw[:, 4 + d_, 1, P:P + 1], pw[:, 4 + d_, 2, P:P + 1]
                    V(lambda e, A=A, B=B, sh=sh, s=s_ar: e.scalar_tensor_tensor(out=B[0][:, sh:], in0=A[0][:, 0:NC - sh], scalar=s, in1=A[0][:, sh:],
                                                                              op0=ALU.mult, op1=ALU.add), reads=[A[0], pw], writes=[B[0]])
                    V(lambda e, A=A, B=B, sh=sh, s=s_nai: e.scalar_tensor_tensor(out=B[0][:, sh:], in0=A[1][:, 0:NC - sh], scalar=s, in1=B[0][:, sh:],
                                                                               op0=ALU.mult, op1=ALU.add), reads=[A[1], B[0], pw], writes=[B[0]])
                    V(lambda e, A=A, B=B, sh=sh, s=s_ar: e.scalar_tensor_tensor(out=B[1][:, sh:], in0=A[1][:, 0:NC - sh], scalar=s, in1=A[1][:, sh:],
                                                                              op0=ALU.mult, op1=ALU.add), reads=[A[1], pw], writes=[B[1]])
                    V(lambda e, A=A, B=B, sh=sh, s=s_ai: e.scalar_tensor_tensor(out=B[1][:, sh:], in0=A[0][:, 0:NC - sh], scalar=s, in1=B[1][:, sh:],
                                                                              op0=ALU.mult, op1=ALU.add), reads=[A[0], B[1], pw], writes=[B[1]])
                    kb.pool(lambda e, A=A, B=B, sh=sh: e.tensor_copy(out=B[0][:, 0:sh], in_=A[0][:, 0:sh]), reads=[A[0]], writes=[B[0]])
                    kb.pool(lambda e, A=A, B=B, sh=sh: e.tensor_copy(out=B[1][:, 0:sh], in_=A[1][:, 0:sh]), reads=[A[1]], writes=[B[1]])
                    A, B = B, A
                for c2 in range(2):
                    kb.pool(lambda e, A=A, c2=c2, xb_=xb_: e.tensor_copy(out=xb_[:, c2, 1:NC + 1], in_=A[c2][:]), reads=[A[c2]], writes=[xb_])
                c0, c1 = bc_c(cw[:, P, 0, :]), bc_c(cw[:, P, 1, :])
                p_r, p_i = bc_t(pwT[:, :, 0, P]), bc_t(pwT[:, :, 1, P])
                V(lambda e, c0=c0, p_r=p_r: e.tensor_tensor(out=tm1[:], in0=c0, in1=p_r, op=ALU.mult), reads=[cw, pwT], writes=[tm1])
                V(lambda e, c1=c1, p_i=p_i: e.tensor_tensor(out=tm2[:], in0=c1, in1=p_i, op=ALU.mult), reads=[cw, pwT], writes=[tm2])
                V(lambda e, ob_=ob_: e.tensor_tensor(out=ob_[:, :, 0, :], in0=tm1[:], in1=tm2[:], op=ALU.add), reads=[tm1, tm2], writes=[ob_])
                V(lambda e, c1=c1, p_r=p_r: e.tensor_tensor(out=tm1[:], in0=c1, in1=p_r, op=ALU.mult), reads=[cw, pwT], writes=[tm1])
                V(lambda e, c0=c0, p_i=p_i: e.tensor_tensor(out=tm2[:], in0=c0, in1=p_i, op=ALU.mult), reads=[cw, pwT], writes=[tm2])
                V(lambda e, ob_=ob_: e.tensor_tensor(out=ob_[:, :, 1, :], in0=tm1[:], in1=tm2[:], op=ALU.subtract), reads=[tm1, tm2], writes=[ob_])
                for i in range(TC):
                    ps = self.bank()
                    self.mm(ps[:, 0:NC], ob_[:, i + 1, 0, :], xb_[:, 0, 0:NC], True, False, reads=[ob_, xb_], writes=[ps])
                    self.mm(ps[:, 0:NC], ob_[:, i + 1, 1, :], xb_[:, 1, 0:NC], False, True, reads=[ob_, xb_], writes=[ps])
                    V(lambda e, ps=ps, i=i: e.tensor_tensor(out=ycv[:, i, :], in0=ycv[:, i, :], in1=ps[:, 0:NC], op=ALU.add), reads=[ps, yc], writes=[yc])
                for t4 in range(4):
                    ps = self.bank()
                    self.mm(ps[:], bbar[:, P, 0, :], ob_[:, t4 * 4:(t4 + 1) * 4, 0, :], True, False, reads=[ob_, bbar], writes=[ps])
                    self.mm(ps[:], bbar[:, P, 1, :], ob_[:, t4 * 4:(t4 + 1) * 4, 1, :], False, True, reads=[ob_, bbar], writes=[ps])
                    kv = Kacc[:, t4 * 4:(t4 + 1) * 4, :]
                    if pq == 0:
                        kb.act(lambda e, ps=ps, kv=kv: e.activation(out=kv, in_=ps[:].rearrange("p (a b) -> p a b", a=4), func=AF.Copy),
                               reads=[ps], writes=[(Kacc, t4)])
                    else:
                        V(lambda e, ps=ps, kv=kv: e.tensor_tensor(out=kv, in0=kv, in1=ps[:].rearrange("p (a b) -> p a b", a=4), op=ALU.add),
                          reads=[ps, (Kacc, t4)], writes=[(Kacc, t4)])
            kb.pool(lambda e: e.tensor_copy(out=Ksb[:], in_=Kacc[:]), reads=[Kacc], writes=[Ksb])
            for i in range(TC):
                ps = self.bank()
                for j in range(i + 1):
                    self.mm(ps[:, 0:NC], Ksb[:, i - j, :], upm[:, j, :], j == 0, j == i, reads=[Ksb, upm], writes=[ps])
                V(lambda e, ps=ps, i=i: e.tensor_tensor(out=ycv[:, i, :], in0=ycv[:, i, :], in1=ps[:, 0:NC], op=ALU.add), reads=[ps, yc], writes=[yc])
            for g in range(NG):
                sl = slice(g * 512, (g + 1) * 512)
                kb.pool(lambda e, sl=sl, g=g: e.tensor_tensor(out=tglf[:, sl], in0=yc[:, sl], in1=yc[:, sl], op=ALU.mult), reads=[yc], writes=[(tgl, g)])
                kb.pool(lambda e, sl=sl, g=g: e.tensor_scalar(out=tglf[:, sl], in0=tglf[:, sl], scalar1=0.044715, scalar2=1.0, op0=ALU.mult, op1=ALU.add),
                        reads=[(tgl, g)], writes=[(tgl, g)])
                kb.pool(lambda e, sl=sl, g=g: e.tensor_tensor(out=tglf[:, sl], in0=tglf[:, sl], in1=yc[:, sl], op=ALU.mult), reads=[(tgl, g), yc],
                        writes=[(tgl, g)])
                kb.act(lambda e, sl=sl, g=g: e.activation(out=tglf[:, sl], in_=tglf[:, sl], func=AF.Sigmoid, scale=1.5957691216), reads=[(tgl, g)],
                       writes=[(tgl, g)])
                kb.pool(lambda e, sl=sl, g=g: e.tensor_tensor(out=ztv[:, 2 * g:2 * g + 2, :], in0=tglf[:, sl].rearrange("p (a b) -> p a b", a=2),
                                                              in1=yc[:, sl].rearrange("p (a b) -> p a b", a=2), op=ALU.mult),
                        reads=[(tgl, g), yc], writes=[(ztc, sl.start)])
            kb.dma(self.d_z[:, ch, :], ztc[:], reads=[ztc])
        kb.barrier()
        self.release(m_w)
        sg = [self.tile([128, 512], F32, "sg") for _ in range(2)]
        og = [self.tile([128, 4, 512], BF16, "og") for _ in range(2)]
        zg = [self.tile([128, 4, 512], BF16, "zg") for _ in range(2)]
        for g in range(NG):
            sl = slice(g * 512, (g + 1) * 512)
            o_ = og[g % 2]
            z_ = zg[g % 2]
            kb.dma(z_[:], self.d_z[:, :, sl], writes=[z_])
            for co in range(4):
                ps = self.bank()
                for k4 in range(4):
                    self.mm(ps[:], glu[:, k4, co * 128:(co + 1) * 128], z_[:, k4, :], k4 == 0, k4 == 3, reads=[z_], writes=[ps])
                s_ = sg[co % 2]
                kb.act(lambda e, ps=ps, s_=s_, co=co: e.activation(out=s_[:], in_=ps[:], func=AF.Sigmoid, bias=dsk[:, 1, co:co + 1]),
                       reads=[ps, dsk], writes=[s_])
                V(lambda e, s_=s_, co=co, o_=o_, z_=z_: e.tensor_tensor(out=o_[:, co, :], in0=s_[:], in1=z_[:, co, :], op=ALU.mult),
                  reads=[s_, z_], writes=[(o_, co)])
            kb.dma(self.d_yc[:, :, sl], o_[:], reads=[o_])

    def ph_merge(self, l):
        kb = self.kb
        wgm = self.tile([128, 8, 3072], BF16, "wgm")
        wbr = self.tile([128, 3, 4, D], BF16, "wbr")
        wo = self.tile([128, 8, D], BF16, "wo")
        m_st = self.mark()
        self.setup_stage(4096)
        for k in range(8):
            self.load_w_bf16(wgm, wgm[:, k, :], self.i_wgm[l, k * 128:(k + 1) * 128, :], [128, 3072])
        for b in range(3):
            self.load_w_bf16(wbr, wbr[:, b, :, :], self.i_wbr[l, b].rearrange("(k p) c -> p k c", p=128), [128, 4, D])
        for k2 in range(2):
            self.load_w_bf16(wo, wo[:, k2 * 4:(k2 + 1) * 4, :], self.i_wout[l, k2 * 512:(k2 + 1) * 512, :].rearrange("(k p) c -> p k c", p=128), [128, 4, D])
        kb.barrier()
        self.release(m_st)
        hs = [self.tile([128, 8, 512], BF16, "hg") for _ in range(2)]
        ys = [[self.tile([128, 4, 512], BF16, "yg") for _ in range(3)] for _ in range(2)]
        xs = [self.tile([128, 8, 512], F32, "xg") for _ in range(2)]
        mg = self.tile([128, 8, 512], BF16, "mg")
        sgt = [self.tile([128, 512], F32, "sgt") for _ in range(3)]
        acc = self.tile([128, 512], F32, "acc")
        tmp = self.tile([128, 512], F32, "tmp")
        ysrc = [self.d_ya, self.d_yb, self.d_yc]
        for g in range(NG):
            sl = slice(g * 512, (g + 1) * 512)
            hg, xg, yg = hs[g % 2], xs[g % 2], ys[g % 2]
            kb.dma(hg[:], self.d_h[:, :, sl], writes=[hg])
            kb.dma(xg[:], self.d_x[:, :, sl], writes=[xg])
            for b in range(3):
                kb.dma(yg[b][:], ysrc[b][:, :, sl], writes=[yg[b]])
            for dc in range(8):
                cs_ = slice(dc * 128, (dc + 1) * 128)
                for b in range(3):
                    ps = self.bank()
                    for k in range(8):
                        self.mm(ps[:], wgm[:, k, b * D + dc * 128:b * D + (dc + 1) * 128], hg[:, k, :], k == 0, k == 7, reads=[hg], writes=[ps])
                    kb.act(lambda e, ps=ps, b=b: e.activation(out=sgt[b][:], in_=ps[:], func=AF.Sigmoid), reads=[ps], writes=[sgt[b]])
                for b in range(3):
                    ps = self.bank()
                    for k in range(4):
                        self.mm(ps[:], wbr[:, b, k, cs_], yg[b][:, k, :], k == 0, k == 3, reads=[yg[b]], writes=[ps])
                    if b == 0:
                        kb.dve(lambda e, ps=ps: e.tensor_tensor(out=acc[:], in0=sgt[0][:], in1=ps[:], op=ALU.mult), reads=[sgt[0], ps], writes=[acc])
                    elif b == 1:
                        kb.dve(lambda e, ps=ps: e.tensor_tensor(out=tmp[:], in0=sgt[1][:], in1=ps[:], op=ALU.mult), reads=[sgt[1], ps], writes=[tmp])
                        kb.dve(lambda e: e.tensor_tensor(out=acc[:], in0=acc[:], in1=tmp[:], op=ALU.add), reads=[acc, tmp], writes=[acc])
                    else:
                        kb.dve(lambda e, ps=ps: e.tensor_tensor(out=tmp[:], in0=sgt[2][:], in1=ps[:], op=ALU.mult), reads=[sgt[2], ps], writes=[tmp])
                        kb.dve(lambda e, dc=dc: e.tensor_tensor(out=mg[:, dc, :], in0=acc[:], in1=tmp[:], op=ALU.add), reads=[acc, tmp], writes=[(mg, dc)])
            for do in range(8):
                ps = self.bank()
                for k in range(8):
                    self.mm(ps[:], wo[:, k, do * 128:(do + 1) * 128], mg[:, k, :], k == 0, k == 7, reads=[(mg, k)], writes=[ps])
                kb.dve(lambda e, ps=ps, do=do, xg=xg: e.tensor_tensor(out=xg[:, do, :], in0=xg[:, do, :], in1=ps[:], op=ALU.add),
                       reads=[ps, (xg, do)], writes=[(xg, do)])
            kb.dma(self.d_x[:, :, sl], xg[:], reads=[xg])

    def ph_mlp(self, l):
        kb = self.kb
        wup = self.tile([128, 8, 4 * D], BF16, "wup")
        wdn = self.tile([128, 32, D], BF16, "wdn")
        gain = self.tile([128, 8], F32, "gain")
        m_st = self.mark()
        self.setup_stage(4096)
        for k in range(8):
            self.load_w_bf16(wup, wup[:, k, :], self.i_wup[l, k * 128:(k + 1) * 128, :], [128, 4096])
        for k4 in range(8):
            self.load_w_bf16(wdn, wdn[:, k4 * 4:(k4 + 1) * 4, :], self.i_wdn[l, k4 * 512:(k4 + 1) * 512, :].rearrange("(k p) c -> p k c", p=128), [128, 4, D])
        kb.dma(gain[:], self.i_gains[L + l], writes=[gain])
        kb.barrier()
        self.release(m_st)
        xs = [self.tile([128, 8, 512], F32, "xg") for _ in range(1)]
        hg = self.tile([128, 8, 512], BF16, "hg")
        rstd = self.tile([128, 512], F32, "rstd")
        a = self.tile([128, 32, 512], BF16, "aT")
        sq = a
        rl = [self.tile([128, 512], F32, "rl") for _ in range(2)]
        for g in range(NG):
            sl = slice(g * 512, (g + 1) * 512)
            xg = xs[0]
            kb.dma(xg[:], self.d_x[:, :, sl], writes=[xg])
            self.rmsnorm_group(xg, hg, gain, sq, rstd)
            for f in range(32):
                ps = self.bank()
                for k in range(8):
                    self.mm(ps[:], wup[:, k, f * 128:(f + 1) * 128], hg[:, k, :], k == 0, k == 7, reads=[(hg, k)], writes=[ps])
                r_ = rl[f % 2]
                kb.act(lambda e, ps=ps, r_=r_: e.activation(out=r_[:], in_=ps[:], func=AF.Relu), reads=[ps], writes=[r_])
                kb.pool(lambda e, r_=r_, f=f: e.tensor_tensor(out=a[:, f, :], in0=r_[:], in1=r_[:], op=ALU.mult), reads=[r_], writes=[(a, f)])
            for do in range(8):
                ps = self.bank()
                for f in range(32):
                    self.mm(ps[:], wdn[:, f, do * 128:(do + 1) * 128], a[:, f, :], f == 0, f == 31, reads=[(a, f)], writes=[ps])
                kb.dve(lambda e, ps=ps, do=do, xg=xg: e.tensor_tensor(out=xg[:, do, :], in0=xg[:, do, :], in1=ps[:], op=ALU.add),
                       reads=[ps, (xg, do)], writes=[(xg, do)])
            kb.dma(self.d_x[:, :, sl], xg[:], reads=[xg])

    def ph_final(self):
        kb = self.kb
        gain = self.tile([128, 8], F32, "gain")
        kb.dma(gain[:], self.i_gains[2 * L], writes=[gain])
        xs = [self.tile([128, 8, 512], F32, "xg") for _ in range(2)]
        os_ = [self.tile([128, 8, 512], F32, "og") for _ in range(2)]
        sqs = [self.tile([128, 8, 512], BF16, "sq") for _ in range(2)]
        rstds = [self.tile([128, 512], F32, "rstd") for _ in range(2)]
        for g in range(NG):
            sl = slice(g * 512, (g + 1) * 512)
            xg, og = xs[g % 2], os_[g % 2]
            kb.dma(xg[:], self.d_x[:, :, sl], writes=[xg])
            self.rmsnorm_group(xg, og, gain, sqs[g % 2], rstds[g % 2])
            kb.dma(self.o_out[:, :, sl], og[:], reads=[og])


def prep_shared(inp):
    f = np.float32
    w_in = np.asarray(inp["w_in"], f)
    o = {}
    o["gains"] = np.ascontiguousarray(np.concatenate([
        np.asarray(inp["norm_mix"], f).reshape(L, 8, 128).transpose(0, 2, 1),
        np.asarray(inp["norm_mlp"], f).reshape(L, 8, 128).transpose(0, 2, 1),
        np.asarray(inp["norm_final"], f).reshape(1, 8, 128).transpose(0, 2, 1)], 0))
    qa = w_in[:, :, 0:512].reshape(L, D, 8, 64)
    qa_perm = np.stack([np.concatenate([qa[:, :, c], qa[:, :, 4 + c]], -1) for c in range(4)], 2).reshape(L, D, 512)
    kv = w_in[:, :, 512:1280]
    k0, v0 = kv[:, :, 0:128], kv[:, :, 128:256]
    k1, v1 = kv[:, :, 256:384], kv[:, :, 384:512]
    k2, v2 = kv[:, :, 512:640], kv[:, :, 640:768]
    ga = w_in[:, :, 1280:1304]
    o["w_nsa_f"] = np.ascontiguousarray(np.concatenate([qa_perm, k0, k1, k2, v0], -1))
    o["w_nsa_t"] = np.ascontiguousarray(np.concatenate([v1, v2, ga], -1))
    qb = w_in[:, :, 1304:1816]
    kvb = w_in[:, :, 1816:1944]
    o["w_swa_f"] = np.ascontiguousarray(np.concatenate([qb, kvb[:, :, 0:64], kvb[:, :, 0:64]], -1))
    o["w_swa_t"] = np.ascontiguousarray(kvb[:, :, 64:128])
    o["w_u"] = np.ascontiguousarray(w_in[:, :, 1944:2456])
    o["w_gm"] = np.ascontiguousarray(w_in[:, :, 2456:5528])
    o["cw1"] = np.ascontiguousarray(np.asarray(inp["nsa_cmp_w1"], f).reshape(L, 2, 32, 64, 128).transpose(0, 1, 3, 2, 4))
    o["cw2"] = np.ascontiguousarray(np.asarray(inp["nsa_cmp_w2"], f))
    o["cposT"] = np.ascontiguousarray(np.asarray(inp["nsa_cmp_pos"], f).transpose(0, 3, 1, 2))
    o["sinks"] = np.ascontiguousarray(np.asarray(inp["swa_sinks"], f).reshape(L, 1, 8))
    def pl(a):
        return np.asarray(a, f).reshape(L, 16, 2, 64).transpose(0, 2, 3, 1).reshape(L, 128, 16)
    ldt = np.repeat(np.asarray(inp["s5_log_dt"], f)[:, :, None], 64, 2)
    o["s5a"] = np.ascontiguousarray(np.stack([pl(inp["s5_a_re"]), pl(inp["s5_a_im"]), pl(ldt)], 2))
    def bl(a):
        a = np.asarray(a, f).reshape(L, 16, 2, 64, 16)
        out = np.zeros((L, 2, 64, 16, 4, 2, 16), f)
        for P in range(16):
            for g2 in range(2):
                out[:, g2, :, P, P % 4, g2, :] = a[:, P, g2]
        return out.reshape(L, 128, 16, 128)
    o["s5b"] = np.ascontiguousarray(np.stack([bl(inp["s5_b_re"]), bl(inp["s5_b_im"])], 3))
    cre = np.asarray(inp["s5_c_re"], f).transpose(0, 1, 3, 2)
    cim = np.asarray(inp["s5_c_im"], f).transpose(0, 1, 3, 2)
    o["s5c"] = np.ascontiguousarray(np.stack([bl(cre), bl(cim)], 3))
    o["s5d"] = np.ascontiguousarray(np.stack([np.asarray(inp["s5_d"], f).reshape(L, 4, 128).transpose(0, 2, 1),
                                              np.asarray(inp["s5_glu_b"], f).reshape(L, 4, 128).transpose(0, 2, 1)], 2))
    o["glu_w"] = np.ascontiguousarray(np.asarray(inp["s5_glu_w"], f))
    o["w_br"] = np.ascontiguousarray(np.stack([np.asarray(inp["w_branch_a"], f), np.asarray(inp["w_branch_b"], f),
                                               np.asarray(inp["w_branch_c"], f)], 1))
    o["w_out"] = np.ascontiguousarray(np.asarray(inp["w_out"], f))
    o["w_up"] = np.ascontiguousarray(np.asarray(inp["w_mlp_up"], f))
    o["w_dn"] = np.ascontiguousarray(np.asarray(inp["w_mlp_down"], f))
    n = np.arange(256)[:, None]
    j = np.arange(64)[None, :]
    ov = np.minimum(16 * n + 32, 64 * j + 64) - np.maximum(16 * n, 64 * j)
    cs = np.clip(ov, 0, None).astype(f) / 32.0
    cs[255] = 0
    o["csel"] = cs
    qpos = np.arange(S)
    cur = (qpos // 64)[:, None]
    jj = np.arange(64)[None, :]
    valid = jj <= cur
    forced = (jj == 0) | (jj == cur) | (jj == cur - 1)
    o["nfv"] = np.ascontiguousarray((valid & ~forced).astype(f).reshape(NT, 128, 64))
    o["addend"] = np.ascontiguousarray(np.where(forced & valid, 1e4, np.where(valid, 0.0, -1e30)).astype(f).reshape(NT, 128, 64))
    p = np.arange(128)
    invf = (10000.0 ** (-(np.arange(32, dtype=f)) / 32.0)).astype(f)
    rc = np.zeros((128, 2), f)
    rc[:, 0] = invf[p % 32]
    rc[:, 1] = np.where((p % 64) < 32, -1.0, 1.0)
    o["ropec"] = rc
    return o


_CACHE = {}


def get_prog(key, **kw):
    if key not in _CACHE:
        pr = Prog(**kw)
        nc, n = pr.build()
        _CACHE[key] = (pr, nc)
    return _CACHE[key]


def kernel(**inputs):
    shared = prep_shared(inputs)
    x = np.asarray(inputs["x"], np.float32)
    pos = np.asarray(inputs["positions"], np.int32)
    pr, nc = get_prog("full")
    in_maps = []
    for b in range(8):
        m = dict(shared)
        m["xT_in"] = np.ascontiguousarray(x[b].T.reshape(8, 128, S).transpose(1, 0, 2))
        m["pos"] = np.ascontiguousarray(pos[b].reshape(1, S))
        in_maps.append(m)
    res = run_bass_kernel_spmd(nc, in_maps, core_ids=list(range(8)))
    out = np.empty((8, S, D), np.float32)
    for b in range(8):
        o = np.asarray(res.results[b]["outT"])
        out[b] = o.transpose(2, 1, 0).reshape(S, D)
    return out
```

```python
import math
import numpy as np
import concourse.bass as bass
import concourse.mybir as mybir
from concourse.bass_utils import run_bass_kernel_spmd

F32 = mybir.dt.float32
BF16 = mybir.dt.bfloat16
I32 = mybir.dt.int32
AF = mybir.ActivationFunctionType
ALU = mybir.AluOpType
AX = mybir.AxisListType

S = 4096
D = 1024
L = 4
NT = 32
NG = 8
NEG = -30000.0
SB_BASE = 16640
SB_TOP = 229312

NSLOT = 8
SEM_CHUNK = 2000


class _St:
    __slots__ = ("w", "r")

    def __init__(self):
        self.w = None
        self.r = []


class T:
    def __init__(self, h, name=""):
        self.h = h
        self.name = name
        self.whole = _St()
        self.parts = {}

    def __getitem__(self, k):
        return self.h[k]


class Op:
    __slots__ = ("stream", "vq", "vidx", "fn", "deps", "waits", "signaled", "sem", "val", "isdma")


class KB:
    def __init__(self, nc):
        self.nc = nc
        self.ops = []
        self.vq_count = {}
        self.vq_last = {}
        self.dma_n = {"sp": 0, "act": 0, "pool": 0}
        self.pending = {}
        self.streams = ["pe", "act", "dve", "pool", "sp"]

    def _acc(self, lst):
        out = []
        for a in lst:
            if isinstance(a, T):
                out.append((a, None))
            else:
                out.append(a)
        return out

    def op(self, stream, fn, reads=(), writes=(), dma=False):
        o = Op()
        o.stream = stream
        o.fn = fn
        o.isdma = dma
        if dma:
            n = self.dma_n[stream]
            self.dma_n[stream] = n + 1
            o.vq = "%s_d%d" % (stream, n % NSLOT)
        else:
            o.vq = stream
        o.vidx = self.vq_count.get(o.vq, 0)
        self.vq_count[o.vq] = o.vidx + 1
        deps = set()
        if dma and o.vq in self.vq_last:
            deps.add(self.vq_last[o.vq])
        self.vq_last[o.vq] = o
        pb = self.pending.pop(stream, None)
        if pb:
            deps.update(pb)
        for (t, k) in self._acc(reads):
            sts = [t.whole]
            if k is None:
                sts += list(t.parts.values())
            else:
                if k not in t.parts:
                    t.parts[k] = _St()
                sts.append(t.parts[k])
            for s in sts:
                if s.w is not None:
                    deps.add(s.w)
            (t.whole if k is None else t.parts[k]).r.append(o)
        for (t, k) in self._acc(writes):
            sts = [t.whole]
            if k is None:
                sts += list(t.parts.values())
            else:
                if k not in t.parts:
                    t.parts[k] = _St()
                sts.append(t.parts[k])
            for s in sts:
                if s.w is not None:
                    deps.add(s.w)
                deps.update(s.r)
            if k is None:
                t.parts = {}
                t.whole.w = o
                t.whole.r = []
            else:
                st = t.parts[k]
                st.w = o
                st.r = []
        deps.discard(o)
        o.deps = deps
        o.waits = []
        o.signaled = False
        self.ops.append(o)
        return o

    def barrier(self):
        lasts = list(self.vq_last.values())
        for s in self.streams:
            self.pending[s] = set(lasts) | self.pending.get(s, set())

    def pe(self, fn, reads=(), writes=()):
        return self.op("pe", fn, reads, writes)

    def act(self, fn, reads=(), writes=()):
        return self.op("act", fn, reads, writes)

    def dve(self, fn, reads=(), writes=()):
        return self.op("dve", fn, reads, writes)

    def pool(self, fn, reads=(), writes=()):
        return self.op("pool", fn, reads, writes)

    def dma(self, out, in_, reads=(), writes=(), q="sp", **kw):
        return self.op(q, lambda e: e.dma_start(out=out, in_=in_, **kw), reads, writes, dma=True)

    def emit(self):
        from contextlib import ExitStack
        nc = self.nc
        waited = {}
        for c in self.ops:
            for p in sorted(c.deps, key=lambda x: x.vidx):
                if p.stream == c.stream and c.stream == "pe" and not p.isdma:
                    continue
                key = (c.stream, p.vq)
                if waited.get(key, -1) >= p.vidx:
                    continue
                waited[key] = p.vidx
                p.signaled = True
                c.waits.append(p)
        lasts = list(self.vq_last.values())
        for p in lasts:
            p.signaled = True
        cnt = {}
        need = {}
        for o in self.ops:
            if o.signaled:
                n = cnt.get(o.vq, 0)
                cnt[o.vq] = n + 1
                o.sem = (o.vq, n // SEM_CHUNK)
                o.val = (n % SEM_CHUNK + 1) * (16 if o.isdma else 1)
                need[o.sem] = True
        with ExitStack() as es:
            sems = {}
            for s in sorted(need.keys()):
                sems[s] = es.enter_context(nc.semaphore("s_%s_%d" % s))
            per = {s: [o for o in self.ops if o.stream == s] for s in self.streams}
            block = es.enter_context(nc.Block())

            def run(stream, eng):
                for o in per[stream]:
                    w = {}
                    for p in o.waits:
                        w[p.sem] = max(w.get(p.sem, 0), p.val)
                    for s, v in w.items():
                        eng.wait_ge(sems[s], v)
                    ins = o.fn(eng)
                    if o.signaled:
                        ins.then_inc(sems[o.sem], 16 if o.isdma else 1)
                if stream == "sp":
                    for p in lasts:
                        eng.wait_ge(sems[p.sem], p.val)

            @block.tensor
            def _(e):
                run("pe", e)

            @block.scalar
            def _(e):
                run("act", e)

            @block.vector
            def _(e):
                run("dve", e)

            @block.gpsimd
            def _(e):
                run("pool", e)

            @block.sync
            def _(e):
                run("sp", e)
        return len(self.ops)


class Prog:
    def __init__(self, n_layers=L, debug=False, phases=None):
        self.nl = n_layers
        self.debug = debug
        self.phases = phases
        nc = bass.Bass("TRN2", target_bir_lowering=False)
        self.nc = nc
        self.kb = KB(nc)
        self.sb_off = SB_BASE
        self.uid = 0
        self.arena = nc.alloc_sbuf_tensor_at("arena", [128, (SB_TOP - SB_BASE) // 2], BF16, offset=SB_BASE)
        self.decl_io()
        self.alloc_psum()

    def tile(self, shape, dt, name="t"):
        self.uid += 1
        nb = 2 if dt == BF16 else 4
        n = 1
        for s in shape[1:]:
            n *= s
        size = (n * nb + 63) // 64 * 64
        off = self.sb_off
        assert off + size <= SB_TOP, ("SBUF overflow", name, off, size)
        self.sb_off += size
        lo = (off - SB_BASE) // 2
        flat = self.arena[:, lo:lo + size // 2]
        if dt != BF16:
            flat = flat.bitcast(dt)
        v = flat[:, 0:n]
        sh = list(shape)
        if len(sh) == 3:
            v = v.rearrange("p (a b) -> p a b", a=sh[1])
        elif len(sh) == 4:
            v = v.rearrange("p (a b c) -> p a b c", a=sh[1], b=sh[2])
        elif len(sh) == 5:
            v = v.rearrange("p (a b c d) -> p a b c d", a=sh[1], b=sh[2], c=sh[3])
        assert sh[0] == 128
        return T(v, name)

    def mark(self):
        return self.sb_off

    def release(self, m):
        self.sb_off = m

    def dram_in(self, name, shape, dt=F32):
        return self.nc.dram_tensor(name, list(shape), dt, kind="ExternalInput")

    def dram_scr(self, name, shape, dt, out=False):
        kind = "ExternalOutput" if (out or self.debug) else "Internal"
        return self.nc.dram_tensor(name, list(shape), dt, kind=kind)

    def decl_io(self):
        nl = self.nl
        d = self.dram_in
        self.i_x = d("xT_in", [128, 8, S])
        self.i_pos = d("pos", [1, S], I32)
        self.i_gains = d("gains", [2 * L + 1, 128, 8])
        self.i_wnf = d("w_nsa_f", [L, D, 1024])
        self.i_wnt = d("w_nsa_t", [L, D, 280])
        self.i_wsf = d("w_swa_f", [L, D, 640])
        self.i_wst = d("w_swa_t", [L, D, 64])
        self.i_wu = d("w_u", [L, D, 512])
        self.i_wgm = d("w_gm", [L, D, 3072])
        self.i_cw1 = d("cw1", [L, 2, 64, 32, 128])
        self.i_cw2 = d("cw2", [L, 2, 128, 64])
        self.i_cpos = d("cposT", [L, 64, 2, 32])
        self.i_sinks = d("sinks", [L, 1, 8])
        self.i_s5a = d("s5a", [L, 128, 3, 16])
        self.i_s5b = d("s5b", [L, 128, 16, 2, 128])
        self.i_s5c = d("s5c", [L, 128, 16, 2, 128])
        self.i_s5d = d("s5d", [L, 128, 2, 4])
        self.i_glu = d("glu_w", [L, 512, 512])
        self.i_wbr = d("w_br", [L, 3, 512, D])
        self.i_wout = d("w_out", [L, D, D])
        self.i_wup = d("w_up", [L, D, 4 * D])
        self.i_wdn = d("w_dn", [L, 4 * D, D])
        self.i_csel = d("csel", [256, 64])
        self.i_nfv = d("nfv", [NT, 128, 64])
        self.i_add = d("addend", [NT, 128, 64])
        self.i_rc = d("ropec", [128, 2])
        s = self.dram_scr
        self.o_out = self.nc.dram_tensor("outT", [128, 8, S], F32, kind="ExternalOutput")
        self.d_x = s("xT", [128, 8, S], F32)
        self.d_h = s("hT", [128, 8, S], BF16)
        self.d_cos = s("cosT", [128, S], F32)
        self.d_sin = s("sinT", [128, S], F32)
        self.d_ya = s("yaT", [128, 4, S], BF16)
        self.d_yb = s("ybT", [128, 4, S], BF16)
        self.d_yc = s("ycT", [128, 4, S], BF16)
        self.d_dbg = s("dbg", [128, 8192], F32)
        self.d_z = s("zT", [128, 4, S], BF16)

    def alloc_psum(self):
        nc = self.nc
        self.pf = [T(nc.alloc_psum_tensor("pf%d" % i, [128, 512], F32), "pf%d" % i) for i in range(7)]
        self.pb = T(nc.alloc_psum_tensor("pb", [128, 1024], BF16), "pb")
        self.pf_rr = 0

    def bank(self, lo=0, hi=7):
        n = hi - lo
        b = self.pf[lo + (self.pf_rr % n)]
        self.pf_rr += 1
        return b

    def mm(self, out, lhsT, rhs, start, stop, reads=(), writes=(), **kw):
        return self.kb.pe(lambda e: e.matmul(out, lhsT=lhsT, rhs=rhs, start=start, stop=stop, **kw), reads, writes)

    def load_w_bf16(self, dst, dst_ap, src_ap, shape, eng="pool"):
        st = self.stage[self.stage_i % len(self.stage)]
        self.stage_i += 1
        n = 1
        for s_ in shape[1:]:
            n *= s_
        assert n <= self.stage_n, (n, self.stage_n)
        flat = st[:, 0:n]
        if len(shape) == 3:
            sv = flat.rearrange("p (a b) -> p a b", a=shape[1])
        elif len(shape) == 4:
            sv = flat.rearrange("p (a b c) -> p a b c", a=shape[1], b=shape[2])
        else:
            sv = flat
        self.kb.dma(sv, src_ap, writes=[st])
        if self.stage_i % 2 == 0:
            self.kb.dve(lambda e: e.tensor_copy(out=dst_ap, in_=sv), reads=[st], writes=[dst])
        else:
            self.kb.act(lambda e: e.activation(out=dst_ap, in_=sv, func=AF.Copy), reads=[st], writes=[dst])

    def setup_stage(self, n_elems, nbuf=2):
        self.stage = [self.tile([128, n_elems], F32, "stage") for _ in range(nbuf)]
        self.stage_n = n_elems
        self.stage_i = 0

    def build(self):
        kb = self.kb
        self.consts()
        base = self.mark()
        for l in range(self.nl):
            for ph, fn in (("norm", self.ph_norm), ("nsa", self.ph_nsa), ("swa", self.ph_swa),
                           ("s5", self.ph_s5), ("merge", self.ph_merge), ("mlp", self.ph_mlp)):
                if self.phases is not None and ph not in self.phases:
                    continue
                if ph == "norm" and l > 0 and self.phases is None:
                    continue
                kb.barrier()
                self.release(base)
                fn(l)
        kb.barrier()
        self.release(base)
        if self.phases is None or "final" in self.phases:
            self.ph_final()
        kb.barrier()
        n = kb.emit()
        return self.nc, n

    def consts(self):
        kb = self.kb
        nc = self.nc
        for k in range(8):
            kb.dma(self.d_x[:, k, :], self.i_x[:, k, :])
        self.ident = self.tile([128, 128], BF16, "ident")
        self.identf = self.tile([128, 128], F32, "identf")
        self.ones = self.tile([128, 128], BF16, "ones")
        self.zeros = self.tile([128, 512], BF16, "zeros")
        self.maskD = self.tile([128, 4, 128], BF16, "maskD")
        self.maskU = self.tile([128, 4, 128], BF16, "maskU")
        self.ewide = self.tile([128, S], BF16, "ewide")
        self.rc = self.tile([128, 2], F32, "rc")
        self.eps = self.tile([128, 1], F32, "eps")
        kb.pool(lambda e: e.memset(self.ident[:], 1.0), writes=[self.ident])
        kb.pool(lambda e: e.affine_select(out=self.ident[:], in_=self.ident[:], pattern=[[1, 128]],
                                          compare_op=ALU.is_equal, fill=0.0, base=0, channel_multiplier=-1),
                reads=[self.ident], writes=[self.ident])
        kb.pool(lambda e: e.memset(self.identf[:], 1.0), writes=[self.identf])
        kb.pool(lambda e: e.affine_select(out=self.identf[:], in_=self.identf[:], pattern=[[1, 128]],
                                          compare_op=ALU.is_equal, fill=0.0, base=0, channel_multiplier=-1),
                reads=[self.identf], writes=[self.identf])
        self.permf = self.tile([128, 128], F32, "permf")
        for q4 in range(4):
            src = q4 ^ 1
            kb.pool(lambda e, q4=q4, src=src: e.tensor_copy(out=self.permf[:, 32 * q4:32 * q4 + 32], in_=self.identf[:, 32 * src:32 * src + 32]),
                    reads=[self.identf], writes=[self.permf])
        kb.pool(lambda e: e.memset(self.ones[:], 1.0), writes=[self.ones])
        kb.pool(lambda e: e.memset(self.zeros[:], 0.0), writes=[self.zeros])
        kb.pool(lambda e: e.memset(self.eps[:], 1e-6), writes=[self.eps])
        kb.pool(lambda e: e.memset(self.maskD[:], 0.0), writes=[self.maskD])
        kb.pool(lambda e: e.affine_select(out=self.maskD[:], in_=self.maskD[:], pattern=[[0, 4], [1, 128]],
                                          compare_op=ALU.is_ge, fill=NEG, base=0, channel_multiplier=-1),
                reads=[self.maskD], writes=[self.maskD])
        kb.pool(lambda e: e.memset(self.maskU[:], 0.0), writes=[self.maskU])
        kb.pool(lambda e: e.affine_select(out=self.maskU[:], in_=self.maskU[:], pattern=[[0, 4], [-1, 128]],
                                          compare_op=ALU.is_ge, fill=NEG, base=-1, channel_multiplier=1),
                reads=[self.maskU], writes=[self.maskU])
        for hf in range(2):
            sl = slice(64 * hf, 64 * hf + 64)
            kb.pool(lambda e, sl=sl: e.memset(self.ewide[sl, :], 1.0), writes=[self.ewide])
            kb.pool(lambda e, sl=sl: e.affine_select(out=self.ewide[sl, :], in_=self.ewide[sl, :], pattern=[[1, S]],
                                                     compare_op=ALU.is_ge, fill=0.0, base=0, channel_multiplier=-64),
                    reads=[self.ewide], writes=[self.ewide])
            kb.pool(lambda e, sl=sl: e.affine_select(out=self.ewide[sl, :], in_=self.ewide[sl, :], pattern=[[-1, S]],
                                                     compare_op=ALU.is_ge, fill=0.0, base=63, channel_multiplier=64),
                    reads=[self.ewide], writes=[self.ewide])
        kb.dma(self.rc[:], self.i_rc[:, :], writes=[self.rc])
        self.csel = self.tile([128, 2, 64], BF16, "csel")
        m = self.mark()
        cs = self.tile([128, 2, 64], F32, "csst")
        kb.dma(cs[:], self.i_csel.ap().rearrange("(t p) j -> p t j", p=128), writes=[cs])
        kb.dve(lambda e: e.tensor_copy(out=self.csel[:], in_=cs[:]), reads=[cs], writes=[self.csel])
        self.rope_tables()
        kb.barrier()
        self.release(m)

    def rope_tables(self):
        kb = self.kb
        HI = 6.28125
        LO = 2.0 * math.pi - HI
        for half in range(2):
            m = self.mark()
            n = 2048
            sl = slice(half * n, (half + 1) * n)
            pi_ = self.tile([128, n], I32, "posi")
            pf_ = self.tile([128, n], F32, "posf")
            ang = self.tile([128, n], F32, "ang")
            kf = self.tile([128, n], F32, "kf")
            ki = self.tile([128, n], I32, "ki")
            r = self.tile([128, n], F32, "r")
            msk = self.tile([128, n], F32, "msk")
            kb.dma(pi_[:], self.i_pos[:, sl].partition_broadcast(128) if False else self.i_pos[0:1, sl].broadcast_to([128, n]), writes=[pi_])
            kb.dve(lambda e: e.tensor_copy(out=pf_[:], in_=pi_[:]), reads=[pi_], writes=[pf_])
            kb.dve(lambda e: e.tensor_scalar(out=ang[:], in0=pf_[:], scalar1=self.rc[:, 0:1], scalar2=None, op0=ALU.mult),
                   reads=[pf_, self.rc], writes=[ang])
            kb.dve(lambda e: e.tensor_scalar(out=ki[:], in0=ang[:], scalar1=1.0 / (2 * math.pi), scalar2=None, op0=ALU.mult),
                   reads=[ang], writes=[ki])
            kb.dve(lambda e: e.tensor_copy(out=kf[:], in_=ki[:]), reads=[ki], writes=[kf])
            kb.dve(lambda e: e.scalar_tensor_tensor(out=ang[:], in0=kf[:], scalar=-HI, in1=ang[:], op0=ALU.mult, op1=ALU.add),
                   reads=[kf, ang], writes=[ang])
            kb.dve(lambda e: e.scalar_tensor_tensor(out=ang[:], in0=kf[:], scalar=-LO, in1=ang[:], op0=ALU.mult, op1=ALU.add),
                   reads=[kf, ang], writes=[ang])
            for which in range(2):
                kb.dve(lambda e, which=which: e.tensor_scalar(out=r[:], in0=ang[:], scalar1=(math.pi / 2) * which, scalar2=None, op0=ALU.add),
                       reads=[ang], writes=[r])
                for rep in range(2):
                    kb.dve(lambda e: e.tensor_scalar(out=msk[:], in0=r[:], scalar1=math.pi, scalar2=None, op0=ALU.is_gt),
                           reads=[r], writes=[msk])
                    kb.dve(lambda e: e.scalar_tensor_tensor(out=r[:], in0=msk[:], scalar=-2 * math.pi, in1=r[:], op0=ALU.mult, op1=ALU.add),
                           reads=[msk, r], writes=[r])
                    kb.dve(lambda e: e.tensor_scalar(out=msk[:], in0=r[:], scalar1=-math.pi, scalar2=None, op0=ALU.is_lt),
                           reads=[r], writes=[msk])
                    kb.dve(lambda e: e.scalar_tensor_tensor(out=r[:], in0=msk[:], scalar=2 * math.pi, in1=r[:], op0=ALU.mult, op1=ALU.add),
                           reads=[msk, r], writes=[r])
                kb.dve(lambda e: e.tensor_scalar(out=r[:], in0=r[:], scalar1=3.1415925, scalar2=-3.1415925, op0=ALU.min, op1=ALU.max),
                       reads=[r], writes=[r])
                kb.act(lambda e: e.activation(out=msk[:], in_=r[:], func=AF.Sin), reads=[r], writes=[msk])
                if which == 0:
                    kb.dve(lambda e: e.tensor_scalar(out=msk[:], in0=msk[:], scalar1=self.rc[:, 1:2], scalar2=None, op0=ALU.mult),
                           reads=[msk, self.rc], writes=[msk])
                    kb.dma(self.d_sin[:, sl], msk[:], reads=[msk])
                else:
                    kb.dma(self.d_cos[:, sl], msk[:], reads=[msk])
            kb.barrier()
            self.release(m)

    def rmsnorm_group(self, xg, hg, gain, sq, rstd):
        kb = self.kb
        ps = self.bank()
        for k in range(8):
            kb.act(lambda e, k=k: e.activation(out=sq[:, k, :], in_=xg[:, k, :], func=AF.Square), reads=[xg], writes=[(sq, k)])
        for k in range(8):
            self.mm(ps[:], self.ones[:], sq[:, k, :], k == 0, k == 7, reads=[(sq, k), self.ones], writes=[ps])
        kb.act(lambda e: e.activation(out=rstd[:], in_=ps[:], func=AF.Sqrt, scale=1.0 / D, bias=self.eps[:]),
               reads=[ps, self.eps], writes=[rstd])
        kb.dve(lambda e: e.reciprocal(out=rstd[:], in_=rstd[:]), reads=[rstd], writes=[rstd])
        for k in range(8):
            kb.dve(lambda e, k=k: e.scalar_tensor_tensor(out=hg[:, k, :], in0=xg[:, k, :], scalar=gain[:, k:k + 1], in1=rstd[:],
                                                         op0=ALU.mult, op1=ALU.mult),
                   reads=[xg, gain, rstd], writes=[(hg, k)])

    def ph_norm(self, l):
        kb = self.kb
        gain = self.tile([128, 8], F32, "gain")
        kb.dma(gain[:], self.i_gains[l], writes=[gain])
        xs = [self.tile([128, 8, 512], F32, "xg") for _ in range(2)]
        hs = [self.tile([128, 8, 512], BF16, "hg") for _ in range(2)]
        sqs = [self.tile([128, 8, 512], BF16, "sq") for _ in range(2)]
        rstds = [self.tile([128, 512], F32, "rstd") for _ in range(2)]
        for g in range(NG):
            xg, hg = xs[g % 2], hs[g % 2]
            sl = slice(g * 512, (g + 1) * 512)
            kb.dma(xg[:], self.d_x[:, :, sl], writes=[xg])
            self.rmsnorm_group(xg, hg, gain, sqs[g % 2], rstds[g % 2])
            kb.dma(self.d_h[:, :, sl], hg[:], reads=[hg])

    def rope_chunk(self, ps, dst_ap, dst, cs, sn, xf, rot, t1):
        kb = self.kb
        kb.act(lambda e: e.activation(out=xf[:], in_=ps[:], func=AF.Copy), reads=[ps], writes=[xf])
        pr = self.bank()
        self.mm(pr[:], self.permf[:], xf[:], True, True, reads=[xf, self.permf], writes=[pr])
        kb.dve(lambda e: e.tensor_tensor(out=t1[:], in0=xf[:], in1=cs[:], op=ALU.mult), reads=[xf, cs], writes=[t1])
        kb.dve(lambda e: e.tensor_tensor(out=rot[:], in0=pr[:], in1=sn[:], op=ALU.mult), reads=[pr, sn], writes=[rot])
        kb.dve(lambda e: e.tensor_tensor(out=dst_ap, in0=t1[:], in1=rot[:], op=ALU.add), reads=[t1, rot], writes=[dst])

    def proj_feature(self, wsb, ncol_chunks, hg, consume):
        for c in range(ncol_chunks):
            ps = self.bank()
            for k in range(8):
                self.mm(ps[:], wsb[:, k, c * 128:(c + 1) * 128], hg[:, k, :], k == 0, k == 7, reads=[hg], writes=[ps])
            consume(c, ps)

    def ph_nsa(self, l):
        kb = self.kb
        wf = self.tile([128, 8, 1024], BF16, "wnf")
        wt = self.tile([128, 8, 280], BF16, "wnt")
        w1 = self.tile([128, 2, 32, 128], BF16, "cw1")
        w2k = self.tile([128, 128], BF16, "cw2k")
        w2v = self.tile([128, 64], BF16, "cw2v")
        cpos = self.tile([128, 2, 32], BF16, "cpos")
        qT = self.tile([128, 4, S], BF16, "qTa")
        kT = self.tile([128, 3, S], BF16, "kTa")
        v0T = self.tile([128, S], BF16, "v0T")
        vtok = self.tile([128, NT, 2, 2, 65], BF16, "vtok")
        gates = self.tile([128, NT, 24], F32, "gates")
        kc = self.tile([128, 256], BF16, "kc")
        vcaug = self.tile([128, 2, 2, 128], BF16, "vcaug")
        m_w = self.mark()
        self.setup_stage(4096)
        for k in range(8):
            self.load_w_bf16(wf, wf[:, k, :], self.i_wnf[l, k * 128:(k + 1) * 128, :], [128, 1024])
        self.load_w_bf16(wt, wt[:, :, :], self.i_wnt[l].rearrange("(k p) c -> p k c", p=128), [128, 8, 280])
        for kv in range(2):
            for hf in range(2):
                st = self.stage[self.stage_i % 2]
                self.stage_i += 1
                sv = st[64 * hf:64 * hf + 64, 0:4096].rearrange("p (a b) -> p a b", a=32)
                kb.dma(sv, self.i_cw1[l, kv], writes=[st])
                kb.pool(lambda e, sv=sv, kv=kv, hf=hf: e.tensor_copy(out=w1[64 * hf:64 * hf + 64, kv, :, :], in_=sv), reads=[st], writes=[w1])
        st = self.stage[self.stage_i % 2]
        self.stage_i += 1
        kb.dma(st[:, 0:64], self.i_cw2[l, 0], writes=[st])
        kb.dma(st[:, 64:128], self.i_cw2[l, 1], writes=[st])
        kb.pool(lambda e, st=st: e.tensor_copy(out=w2k[:, 0:64], in_=st[:, 0:64]), reads=[st], writes=[w2k])
        kb.pool(lambda e, st=st: e.tensor_copy(out=w2k[:, 64:128], in_=st[:, 0:64]), reads=[st], writes=[w2k])
        kb.pool(lambda e, st=st: e.tensor_copy(out=w2v[:], in_=st[:, 64:128]), reads=[st], writes=[w2v])
        st = self.stage[self.stage_i % 2]
        self.stage_i += 1
        for hf in range(2):
            kb.dma(st[64 * hf:64 * hf + 64, 0:64].rearrange("p (a b) -> p a b", a=2), self.i_cpos[l], writes=[st])
        kb.pool(lambda e, st=st: e.tensor_copy(out=cpos[:], in_=st[:, 0:64].rearrange("p (a b) -> p a b", a=2)), reads=[st], writes=[cpos])
        kb.pool(lambda e: e.memset(vtok[:, :, :, :, 64:65], 1.0), writes=[vtok])
        for nt in range(2):
            for hk in range(2):
                kb.pool(lambda e, nt=nt, hk=hk: e.tensor_copy(out=vcaug[:, nt, hk, 64:128], in_=self.csel[:, nt, :]),
                        reads=[self.csel], writes=[vcaug])
        kb.barrier()
        hs = [self.tile([128, 8, 512], BF16, "hg") for _ in range(2)]
        css = [self.tile([128, 512], F32, "cs") for _ in range(2)]
        sns = [self.tile([128, 512], F32, "sn") for _ in range(2)]
        xf = [self.tile([128, 512], F32, "xf") for _ in range(2)]
        rot = [self.tile([128, 512], F32, "rot") for _ in range(2)]
        t1 = [self.tile([128, 512], F32, "t1") for _ in range(2)]
        cnt = [0]
        for g in range(NG):
            hg, cs, sn = hs[g % 2], css[g % 2], sns[g % 2]
            sl = slice(g * 512, (g + 1) * 512)
            kb.dma(hg[:], self.d_h[:, :, sl], writes=[hg])
            kb.dma(cs[:], self.d_cos[:, sl], writes=[cs])
            kb.dma(sn[:], self.d_sin[:, sl], writes=[sn])

            def consume(c, ps, sl=sl, cs=cs, sn=sn):
                i = cnt[0] % 2
                cnt[0] += 1
                if c < 4:
                    self.rope_chunk(ps, qT[:, c, sl], (qT, ("w", c, sl.start)), cs, sn, xf[i], rot[i], t1[i])
                elif c < 7:
                    self.rope_chunk(ps, kT[:, c - 4, sl], (kT, ("w", c, sl.start)), cs, sn, xf[i], rot[i], t1[i])
                else:
                    kb.act(lambda e: e.activation(out=v0T[:, sl], in_=ps[:], func=AF.Copy), reads=[ps], writes=[(v0T, sl.start)])
            self.proj_feature(wf, 8, hg, consume)
            for tt in range(4):
                ti = g * 4 + tt
                ps = self.bank()
                for k in range(8):
                    self.mm(ps[:, 0:280], hg[:, k, tt * 128:(tt + 1) * 128], wt[:, k, :], k == 0, k == 7, reads=[hg], writes=[ps])
                kb.act(lambda e, ti=ti, ps=ps: e.activation(out=vtok[:, ti, :, :, 0:64],
                                                            in_=ps[:, 0:256].rearrange("p (a b c) -> p a b c", a=2, b=2),
                                                            func=AF.Copy), reads=[ps], writes=[(vtok, ti)])
                kb.act(lambda e, ti=ti, ps=ps: e.activation(out=gates[:, ti, :], in_=ps[:, 256:280], func=AF.Sigmoid),
                       reads=[ps], writes=[(gates, ti)])
        kb.barrier()
        gel = self.tile([128, 256], BF16, "gel")
        hpre = self.tile([128, 256], F32, "hpre")
        hsq = self.tile([128, 256], F32, "hsq")
        bias = self.tile([128, 1], F32, "cbias")
        kb.pool(lambda e: e.memset(gel[:], 0.0), writes=[gel])
        for kv in range(2):
            psb = self.bank()
            for ll in range(32):
                self.mm(psb[:, 0:1], w1[0:64, kv, ll, :], cpos[0:64, kv, ll:ll + 1], ll == 0, ll == 31, reads=[w1, cpos], writes=[psb])
            kb.act(lambda e, psb=psb: e.activation(out=bias[:], in_=psb[:, 0:1], func=AF.Copy), reads=[psb], writes=[bias])
            src = kT if kv == 0 else v0T
            for hd in range(2):
                ps = self.bank()
                lo = 64 * hd
                for ll in range(32):
                    if kv == 0:
                        rhs = kT[lo:lo + 64, 0, ll:ll + 16 * 254 + 1:16]
                    else:
                        rhs = v0T[lo:lo + 64, ll:ll + 16 * 254 + 1:16]
                    self.mm(ps[:, 0:255], w1[lo:lo + 64, kv, ll, :], rhs, ll == 0, ll == 31, reads=[w1], writes=[ps])
                kb.act(lambda e, ps=ps: e.activation(out=hpre[:, 0:255], in_=ps[:, 0:255], func=AF.Identity, bias=bias[:]),
                       reads=[ps, bias], writes=[hpre])
                kb.dve(lambda e: e.tensor_tensor(out=hsq[:, 0:255], in0=hpre[:, 0:255], in1=hpre[:, 0:255], op=ALU.mult), reads=[hpre], writes=[hsq])
                kb.dve(lambda e: e.tensor_scalar(out=hsq[:, 0:255], in0=hsq[:, 0:255], scalar1=0.044715, scalar2=1.0, op0=ALU.mult, op1=ALU.add),
                       reads=[hsq], writes=[hsq])
                kb.dve(lambda e: e.tensor_tensor(out=hsq[:, 0:255], in0=hsq[:, 0:255], in1=hpre[:, 0:255], op=ALU.mult), reads=[hsq, hpre], writes=[hsq])
                kb.act(lambda e: e.activation(out=hsq[:, 0:255], in_=hsq[:, 0:255], func=AF.Sigmoid, scale=1.5957691216), reads=[hsq], writes=[hsq])
                kb.dve(lambda e: e.tensor_tensor(out=gel[:, 0:255], in0=hsq[:, 0:255], in1=hpre[:, 0:255], op=ALU.mult), reads=[hsq, hpre], writes=[gel])
                if kv == 0:
                    p2 = self.bank()
                    self.mm(p2[:, 0:256], w2k[:], gel[:], True, True, reads=[w2k, gel], writes=[p2])
                    kb.act(lambda e, p2=p2, lo=lo: e.activation(out=kc[lo:lo + 64, :], in_=p2[lo:lo + 64, 0:256], func=AF.Copy),
                           reads=[p2], writes=[kc])
                else:
                    for nt in range(2):
                        p2 = self.bank()
                        self.mm(p2[:, 0:64], gel[:, nt * 128:(nt + 1) * 128], w2v[:], True, True, reads=[w2v, gel], writes=[p2])
                        kb.act(lambda e, p2=p2, nt=nt, hd=hd: e.activation(out=vcaug[:, nt, hd, 0:64], in_=p2[:, 0:64], func=AF.Copy),
                               reads=[p2], writes=[vcaug])
        kb.barrier()
        self.release(m_w)
        kst = [self.tile([128, S], BF16, "kst") for _ in range(2)]
        kb.pool(lambda e: e.tensor_copy(out=kst[0][0:64, :], in_=kT[0:64, 1, :]), writes=[kst[0]])
        kb.act(lambda e: e.activation(out=kst[0][64:128, :], in_=self.ewide[64:128, :], func=AF.Copy), writes=[kst[0]])
        kb.act(lambda e: e.activation(out=kst[1][0:64, :], in_=self.ewide[0:64, :], func=AF.Copy), writes=[kst[1]])
        kb.pool(lambda e: e.tensor_copy(out=kst[1][64:128, :], in_=kT[64:128, 1, :]), writes=[kst[1]])
        NE = 8
        LA = 3
        nfv = [self.tile([128, 64], F32, "nfv") for _ in range(2)]
        add = [self.tile([128, 64], F32, "add") for _ in range(2)]
        eT = [self.tile([128, 512], BF16, "eT") for _ in range(NE)]
        oc = [self.tile([128, 2, 4, 128], F32, "oc") for _ in range(2)]
        osw = [[self.tile([128, 2, 4, 65], F32, "osw") for _ in range(2)] for _ in range(2)]
        den = [self.tile([128, 3, 8], F32, "den") for _ in range(2)]
        rden = [self.tile([128, 3, 8], F32, "rden") for _ in range(2)]
        imp = self.tile([128, 2, 64], F32, "imp")
        sc = self.tile([128, 2, 64], F32, "sc")
        wk = self.tile([128, 2, 64], F32, "wk")
        m8 = self.tile([128, 8], F32, "m8")
        nmk = self.tile([128, 2, 64], BF16, "nmk")
        qst = [[self.tile([128, 4, 128], BF16, "qst") for _ in range(2)] for _ in range(2)]
        y = self.tile([128, 8, 64], F32, "y")
        ytmp = self.tile([128, 8, 64], F32, "ytmp")
        yb = self.tile([128, 512], BF16, "yb16")
        yT = [self.tile([128, 4, 128], BF16, "yT") for _ in range(2)]
        fac = self.tile([128, 3, 8], F32, "fac")
        sbanks = self.pf[0:4]
        abanks = self.pf[4:7]
        cnt = {"s": 0, "e": 0, "a": 0}
        cur_po = {}
        units = []
        def add_units(i, kind):
            if kind == "cmp":
                kts = [0] if i < 16 else [0, 1]
            elif kind == "win":
                kts = list(range(max(0, i - 4), i + 1))
            else:
                kts = list(range(i + 1))
            for hk in range(2):
                for n_i, kt in enumerate(kts):
                    units.append(dict(i=i, kind=kind, hk=hk, kt=kt, first=(n_i == 0), last=(n_i == len(kts) - 1)))
        add_units(0, "cmp")
        add_units(0, "win")
        for i in range(NT):
            if i + 1 < NT:
                add_units(i + 1, "cmp")
            add_units(i, "sel")
            if i + 1 < NT:
                add_units(i + 1, "win")

        def score_fn(u):
            i, kind, hk, kt = u["i"], u["kind"], u["hk"], u["kt"]
            lo = 64 * hk
            qs = slice(i * 128, (i + 1) * 128)
            ps = sbanks[cnt["s"] % 4]
            cnt["s"] += 1
            if kind == "cmp":
                if hk == 0 and u["first"]:
                    kb.dma(nfv[i % 2][:], self.i_nfv[i], writes=[nfv[i % 2]])
                    kb.dma(add[i % 2][:], self.i_add[i], writes=[add[i % 2]])
                self.mm(ps[:], kc[lo:lo + 64, kt * 128:(kt + 1) * 128], qT[lo:lo + 64, :, qs], True, True, writes=[ps])
            else:
                br = 0 if kind == "sel" else 1
                ks = slice(kt * 128, (kt + 1) * 128)
                extra = []
                if br == 0:
                    extra.append("sel")
                if kt == i:
                    extra.append("D")
                if br == 1 and kt == i - 4:
                    extra.append("U")
                if br == 0:
                    extra.remove("sel")
                    qs_ = qst[i % 2][hk]
                    self.mm(ps[:], kst[hk][:, ks], qs_[:], True, len(extra) == 0, reads=[qs_], writes=[ps])
                else:
                    self.mm(ps[:], kT[lo:lo + 64, 1 + br, ks], qT[lo:lo + 64, :, qs], True, len(extra) == 0, writes=[ps])
                for xi, kd in enumerate(extra):
                    last = xi == len(extra) - 1
                    if kd == "D":
                        self.mm(ps[:], self.ident[:], self.maskD[:], False, last, writes=[ps])
                    else:
                        self.mm(ps[:], self.ident[:], self.maskU[:], False, last, writes=[ps])
            e_ = eT[cnt["e"] % NE]
            cnt["e"] += 1
            kb.act(lambda e: e.activation(out=e_[:], in_=ps[:], func=AF.Exp, scale=0.125), reads=[ps], writes=[e_])
            if kind == "cmp":
                kb.pool(lambda e: e.affine_select(
                    out=e_[:].rearrange("p (a n) -> p a n", a=4), in_=e_[:].rearrange("p (a n) -> p a n", a=4),
                    pattern=[[0, 4], [1, 128]], compare_op=ALU.is_ge, fill=0.0,
                    base=128 * i - 31 - 16 * 128 * kt, channel_multiplier=-16), reads=[e_], writes=[e_])
            u["e"] = e_

        def chain(i):
            oc_, den_, rden_ = oc[i % 2], den[i % 2], rden[i % 2]
            nf, ad = nfv[i % 2], add[i % 2]
            kb.dve(lambda e: e.tensor_reduce(out=den_[:, 0, :], in_=oc_[:, :, :, 64:128].rearrange("p a b c -> p (a b) c"), axis=AX.X, op=ALU.add),
                   reads=[oc_], writes=[den_])
            kb.dve(lambda e: e.tensor_scalar(out=rden_[:, 0, :], in0=den_[:, 0, :], scalar1=1e-30, scalar2=None, op0=ALU.max), reads=[den_], writes=[rden_])
            kb.dve(lambda e: e.reciprocal(out=rden_[:, 0, :], in_=rden_[:, 0, :]), reads=[rden_], writes=[rden_])
            for hk in range(2):
                for g4 in range(4):
                    hd = hk * 4 + g4
                    if g4 == 0:
                        kb.dve(lambda e, hk=hk, hd=hd: e.tensor_scalar(out=imp[:, hk, :], in0=oc_[:, hk, 0, 64:128], scalar1=rden_[:, 0, hd:hd + 1],
                                                                       scalar2=None, op0=ALU.mult), reads=[oc_, rden_], writes=[imp])
                    else:
                        kb.dve(lambda e, hk=hk, hd=hd, g4=g4: e.scalar_tensor_tensor(out=imp[:, hk, :], in0=oc_[:, hk, g4, 64:128],
                                                                                     scalar=rden_[:, 0, hd:hd + 1], in1=imp[:, hk, :],
                                                                                     op0=ALU.mult, op1=ALU.add), reads=[oc_, rden_, imp], writes=[imp])
            for hk in range(2):
                kb.dve(lambda e, hk=hk: e.tensor_tensor(out=sc[:, hk, :], in0=imp[:, hk, :], in1=nf[:], op=ALU.mult), reads=[imp, nf], writes=[sc])
                kb.dve(lambda e, hk=hk: e.tensor_tensor(out=sc[:, hk, :], in0=sc[:, hk, :], in1=ad[:], op=ALU.add), reads=[sc, ad], writes=[sc])
                kb.dve(lambda e, hk=hk: e.max(out=m8[:], in_=sc[:, hk, :]), reads=[sc], writes=[m8])
                kb.dve(lambda e, hk=hk: e.match_replace(out=wk[:, hk, :], in_to_replace=m8[:], in_values=sc[:, hk, :], imm_value=-3e38),
                       reads=[sc, m8], writes=[wk])
                kb.dve(lambda e, hk=hk: e.max(out=m8[:], in_=wk[:, hk, :]), reads=[wk], writes=[m8])
                kb.dve(lambda e, hk=hk: e.match_replace(out=wk[:, hk, :], in_to_replace=m8[:], in_values=wk[:, hk, :], imm_value=-3e38),
                       reads=[wk, m8], writes=[wk])
                kb.dve(lambda e, hk=hk: e.tensor_tensor(out=wk[:, hk, :], in0=wk[:, hk, :], in1=sc[:, hk, :], op=ALU.is_equal), reads=[wk, sc], writes=[wk])
                kb.dve(lambda e, hk=hk: e.tensor_scalar(out=nmk[:, 1 - hk, :], in0=wk[:, hk, :], scalar1=NEG, scalar2=None, op0=ALU.mult), reads=[wk], writes=[nmk])
            kb.pe(lambda e: e.transpose(self.pb[:, 0:128], nmk[:].rearrange("p a b -> p (a b)"), self.ident[:]), reads=[nmk, self.ident], writes=[self.pb])
            qs = slice(i * 128, (i + 1) * 128)
            for hk in range(2):
                lo = 64 * hk
                mo = 64 * (1 - hk)
                qs_ = qst[i % 2][hk]
                kb.pool(lambda e, lo=lo, qs_=qs_: e.tensor_copy(out=qs_[lo:lo + 64, :, :], in_=qT[lo:lo + 64, :, qs]), writes=[qs_])
                kb.act(lambda e, mo=mo, qs_=qs_: e.activation(out=qs_[mo:mo + 64, :, :],
                                                              in_=self.pb[mo:mo + 64, 0:128].rearrange("p (o n) -> p o n", o=1).broadcast_to([64, 4, 128]),
                                                              func=AF.Copy), reads=[self.pb], writes=[qs_])

        def combine(i):
            oc_, den_, rden_, osw_ = oc[i % 2], den[i % 2], rden[i % 2], osw[i % 2]
            qs = slice(i * 128, (i + 1) * 128)
            for br in range(2):
                kb.dve(lambda e, br=br: e.tensor_copy(out=den_[:, 1 + br, :], in_=osw_[br][:, :, :, 64:65].rearrange("p a b c -> p (a b c)")),
                       reads=[osw_[br]], writes=[den_])
            kb.dve(lambda e: e.reciprocal(out=rden_[:, 1:3, :], in_=den_[:, 1:3, :]), reads=[den_], writes=[rden_])
            kb.dve(lambda e: e.tensor_tensor(out=fac[:], in0=rden_[:], in1=gates[:, i, :].rearrange("p (a b) -> p a b", a=3), op=ALU.mult),
                   reads=[rden_, gates], writes=[fac])
            kb.dve(lambda e: e.tensor_tensor(out=y[:], in0=oc_[:, :, :, 0:64].rearrange("p a b c -> p (a b) c"),
                                             in1=fac[:, 0, :].rearrange("p (a o) -> p a o", o=1).broadcast_to([128, 8, 64]), op=ALU.mult),
                   reads=[oc_, fac], writes=[y])
            for br in range(2):
                kb.dve(lambda e, br=br: e.tensor_tensor(out=ytmp[:], in0=osw_[br][:, :, :, 0:64].rearrange("p a b c -> p (a b) c"),
                                                        in1=fac[:, 1 + br, :].rearrange("p (a o) -> p a o", o=1).broadcast_to([128, 8, 64]), op=ALU.mult),
                       reads=[osw_[br], fac], writes=[ytmp])
                if br == 0:
                    kb.dve(lambda e: e.tensor_tensor(out=y[:], in0=y[:], in1=ytmp[:], op=ALU.add), reads=[y, ytmp], writes=[y])
                else:
                    kb.dve(lambda e: e.tensor_tensor(out=yb[:].rearrange("p (a b) -> p a b", a=8), in0=y[:], in1=ytmp[:], op=ALU.add),
                           reads=[y, ytmp], writes=[yb])
            self.emit_yT(yb, yT[i % 2], self.d_ya, qs)

        def pv_fn(u):
            i, kind, hk, kt = u["i"], u["kind"], u["hk"], u["kt"]
            e_ = u["e"]
            key = (kind, hk)
            if u["first"]:
                po = abanks[cnt["a"] % 3]
                cnt["a"] += 1
                cur_po[key] = po
                self.mm(po[:], self.zeros[:, 0:128], self.zeros[:], True, False, reads=[self.zeros], writes=[po])
            po = cur_po[key]
            fin = u["last"]
            if kind == "cmp":
                for g4 in range(4):
                    self.mm(po[:, g4 * 128:(g4 + 1) * 128], e_[:, g4 * 128:(g4 + 1) * 128], vcaug[:, kt, hk, :], False, (fin and g4 == 3),
                            reads=[e_], writes=[po])
                if fin:
                    oc_ = oc[i % 2]
                    kb.act(lambda e: e.activation(out=oc_[:, hk, :, :], in_=po[:].rearrange("p (a b) -> p a b", a=4), func=AF.Copy),
                           reads=[po], writes=[oc_])
                    if hk == 1:
                        chain(i)
            else:
                br = 0 if kind == "sel" else 1
                for g4 in range(4):
                    self.mm(po[:, g4 * 65:(g4 + 1) * 65], e_[:, g4 * 128:(g4 + 1) * 128], vtok[:, kt, br, hk, :], False, (fin and g4 == 3),
                            reads=[e_], writes=[po])
                if fin:
                    o_ = osw[i % 2][br]
                    kb.act(lambda e: e.activation(out=o_[:, hk, :, :], in_=po[:, 0:260].rearrange("p (a b) -> p a b", a=4), func=AF.Copy),
                           reads=[po], writes=[o_])
                    if kind == "sel" and hk == 1:
                        combine(i)

        for idx in range(len(units) + LA):
            if idx - LA >= 0:
                pv_fn(units[idx - LA])
            if idx < len(units):
                score_fn(units[idx])

    def emit_yT(self, yb, yT, dst, qs):
        kb = self.kb
        pbh = self.pb
        for c in range(4):
            kb.pe(lambda e, c=c: e.transpose(self.pb[:, 512 + c * 128:512 + (c + 1) * 128], yb[:, c * 128:(c + 1) * 128], self.ident[:]),
                  reads=[yb, self.ident], writes=[pbh])
        kb.act(lambda e: e.activation(out=yT[:], in_=self.pb[:, 512:1024].rearrange("p (a b) -> p a b", a=4), func=AF.Copy), reads=[pbh], writes=[yT])
        kb.dma(dst[:, :, qs], yT[:], reads=[yT])

    def ph_swa(self, l):
        kb = self.kb
        wf = self.tile([128, 8, 640], BF16, "wsf")
        wt = self.tile([128, 8, 64], BF16, "wst")
        qT = self.tile([128, 4, S], BF16, "qTb")
        kT = self.tile([128, S], BF16, "kTb")
        vtok = self.tile([128, NT, 65], BF16, "vtokb")
        esink = self.tile([128, 8], F32, "esink")
        m_w = self.mark()
        self.setup_stage(5120)
        self.load_w_bf16(wf, wf[:, :, :], self.i_wsf[l].rearrange("(k p) c -> p k c", p=128), [128, 8, 640])
        self.load_w_bf16(wt, wt[:, :, :], self.i_wst[l].rearrange("(k p) c -> p k c", p=128), [128, 8, 64])
        kb.dma(esink[:], self.i_sinks[l, 0:1, :].broadcast_to([128, 8]), writes=[esink])
        kb.act(lambda e: e.activation(out=esink[:], in_=esink[:], func=AF.Exp), reads=[esink], writes=[esink])
        kb.pool(lambda e: e.memset(vtok[:, :, 64:65], 1.0), writes=[vtok])
        kb.barrier()
        hs = [self.tile([128, 8, 512], BF16, "hg") for _ in range(2)]
        css = [self.tile([128, 512], F32, "cs") for _ in range(2)]
        sns = [self.tile([128, 512], F32, "sn") for _ in range(2)]
        xf = [self.tile([128, 512], F32, "xf") for _ in range(2)]
        rot = [self.tile([128, 512], F32, "rot") for _ in range(2)]
        t1 = [self.tile([128, 512], F32, "t1") for _ in range(2)]
        cnt = [0]
        for g in range(NG):
            hg, cs, sn = hs[g % 2], css[g % 2], sns[g % 2]
            sl = slice(g * 512, (g + 1) * 512)
            kb.dma(hg[:], self.d_h[:, :, sl], writes=[hg])
            kb.dma(cs[:], self.d_cos[:, sl], writes=[cs])
            kb.dma(sn[:], self.d_sin[:, sl], writes=[sn])

            def consume(c, ps, sl=sl, cs=cs, sn=sn):
                i = cnt[0] % 2
                cnt[0] += 1
                if c < 4:
                    self.rope_chunk(ps, qT[:, c, sl], (qT, ("w", c, sl.start)), cs, sn, xf[i], rot[i], t1[i])
                else:
                    self.rope_chunk(ps, kT[:, sl], (kT, ("w", sl.start)), cs, sn, xf[i], rot[i], t1[i])
            self.proj_feature(wf, 5, hg, consume)
            for tt in range(4):
                ti = g * 4 + tt
                ps = self.bank()
                for k in range(8):
                    self.mm(ps[:, 0:64], hg[:, k, tt * 128:(tt + 1) * 128], wt[:, k, :], k == 0, k == 7, reads=[hg], writes=[ps])
                kb.act(lambda e, ti=ti, ps=ps: e.activation(out=vtok[:, ti, 0:64], in_=ps[:, 0:64], func=AF.Copy), reads=[ps], writes=[(vtok, ti)])
        kb.barrier()
        self.release(m_w)
        NE = 8
        LA = 3
        eT = [self.tile([128, 512], BF16, "eT") for _ in range(NE)]
        ob = [self.tile([128, 2, 4, 65], F32, "ob") for _ in range(2)]
        den = self.tile([128, 2, 4], F32, "denb")
        yb = self.tile([128, 512], BF16, "yb16")
        yT = [self.tile([128, 4, 128], BF16, "yT") for _ in range(2)]
        sbanks = self.pf[0:4]
        abanks = self.pf[4:7]
        cnt = {"s": 0, "e": 0, "a": 0}
        cur_po = {}
        units = []
        for i in range(NT):
            kts = [kt for kt in (i - 1, i) if kt >= 0]
            for hf in range(2):
                for n_i, kt in enumerate(kts):
                    units.append(dict(i=i, hf=hf, kt=kt, first=(n_i == 0), last=(n_i == len(kts) - 1)))

        def score_fn(u):
            i, hf, kt = u["i"], u["hf"], u["kt"]
            lo = 64 * hf
            qs = slice(i * 128, (i + 1) * 128)
            ks = slice(kt * 128, (kt + 1) * 128)
            ps = sbanks[cnt["s"] % 4]
            cnt["s"] += 1
            self.mm(ps[:], kT[lo:lo + 64, ks], qT[lo:lo + 64, :, qs], True, False, writes=[ps])
            self.mm(ps[:], self.ident[:], (self.maskD if kt == i else self.maskU)[:], False, True, writes=[ps])
            e_ = eT[cnt["e"] % NE]
            cnt["e"] += 1
            kb.act(lambda e: e.activation(out=e_[:], in_=ps[:], func=AF.Exp, scale=0.125), reads=[ps], writes=[e_])
            u["e"] = e_

        def finish(i):
            ob_ = ob[i % 2]
            qs = slice(i * 128, (i + 1) * 128)
            kb.dve(lambda e: e.tensor_tensor(out=den[:], in0=ob_[:, :, :, 64:65].rearrange("p a b c -> p a (b c)"),
                                             in1=esink[:].rearrange("p (c h) -> p h c", h=2), op=ALU.add), reads=[ob_, esink], writes=[den])
            kb.dve(lambda e: e.reciprocal(out=den[:], in_=den[:]), reads=[den], writes=[den])
            kb.dve(lambda e: e.tensor_tensor(out=yb[:].rearrange("p (c h d) -> p h c d", c=4, h=2), in0=ob_[:, :, :, 0:64],
                                             in1=den[:].rearrange("p a (b o) -> p a b o", o=1).broadcast_to([128, 2, 4, 64]), op=ALU.mult),
                   reads=[ob_, den], writes=[yb])
            self.emit_yT(yb, yT[i % 2], self.d_yb, qs)

        def pv_fn(u):
            i, hf, kt = u["i"], u["hf"], u["kt"]
            e_ = u["e"]
            if u["first"]:
                po = abanks[cnt["a"] % 3]
                cnt["a"] += 1
                cur_po[hf] = po
                self.mm(po[:], self.zeros[:, 0:128], self.zeros[:], True, False, reads=[self.zeros], writes=[po])
            po = cur_po[hf]
            fin = u["last"]
            for c in range(4):
                self.mm(po[:, c * 65:(c + 1) * 65], e_[:, c * 128:(c + 1) * 128], vtok[:, kt, :], False, (fin and c == 3),
                        reads=[e_], writes=[po])
            if fin:
                ob_ = ob[i % 2]
                kb.act(lambda e: e.activation(out=ob_[:, hf, :, :], in_=po[:, 0:260].rearrange("p (a b) -> p a b", a=4), func=AF.Copy),
                       reads=[po], writes=[ob_])
                if hf == 1:
                    finish(i)

        for idx in range(len(units) + LA):
            if idx - LA >= 0:
                pv_fn(units[idx - LA])
            if idx < len(units):
                score_fn(units[idx])

    def ph_s5(self, l):
        kb = self.kb
        NP = 16
        uT = self.tile([128, 4, S], BF16, "uT")
        bbar = self.tile([128, NP, 2, 128], BF16, "bbar")
        pwT = self.tile([128, 17, 3, NP], F32, "pwT")
        par = self.tile([128, 3, NP], F32, "s5par")
        dsk = self.tile([128, 2, 4], F32, "s5d")
        bT = self.tile([128, NP, 2, 128], BF16, "bT")
        cw = self.tile([128, NP, 2, 128], BF16, "cw")
        glu = self.tile([128, 4, 512], BF16, "glu")
        pw = self.tile([128, 12, 3, NP], F32, "pw")
        m_w = self.mark()
        bw = self.tile([128, NP, 2, 128], F32, "bw")
        wu = self.tile([128, 8, 512], BF16, "wu")
        self.setup_stage(4096)
        self.load_w_bf16(wu, wu[:, :, :], self.i_wu[l].rearrange("(k p) c -> p k c", p=128), [128, 8, 512])
        self.load_w_bf16(glu, glu[:, :, :], self.i_glu[l].rearrange("(k p) c -> p k c", p=128), [128, 4, 512])
        kb.dma(par[:], self.i_s5a[l], writes=[par])
        kb.dma(dsk[:], self.i_s5d[l], writes=[dsk])
        kb.dma(bw[:], self.i_s5b[l], writes=[bw])
        st = self.stage[self.stage_i % 2]
        self.stage_i += 1
        sv = st[:, 0:4096].rearrange("p (a b c) -> p a b c", a=NP, b=2)
        kb.dma(sv, self.i_s5c[l], writes=[st])
        kb.pool(lambda e: e.tensor_copy(out=cw[:, :, 0, :], in_=sv[:, :, 0, :]), reads=[st], writes=[cw])
        kb.pool(lambda e: e.tensor_scalar(out=cw[:, :, 1, :], in0=sv[:, :, 1, :], scalar1=-1.0, scalar2=None, op0=ALU.mult), reads=[st], writes=[cw])
        import os as _os
        sub = int(_os.environ.get("S5SUB", "99"))
        if sub <= 0:
            return
        def t16(name):
            return self.tile([128, NP], F32, name)
        dt, mag, phi, ar, ai, kf, r, msk, cr, ci, den, nr, t_a, t_b = [t16("s5_%d" % j) for j in range(14)]
        ki = self.tile([128, NP], I32, "s5ki")
        A_re, A_im = par[:, 0, :], par[:, 1, :]
        V = kb.dve
        V(lambda e: e.tensor_copy(out=dt[:], in_=par[:, 2, :]), reads=[par], writes=[dt])
        kb.act(lambda e: e.activation(out=dt[:], in_=dt[:], func=AF.Exp), reads=[dt], writes=[dt])
        V(lambda e: e.tensor_tensor(out=mag[:], in0=dt[:], in1=A_re, op=ALU.mult), reads=[dt, par], writes=[mag])
        kb.act(lambda e: e.activation(out=mag[:], in_=mag[:], func=AF.Exp), reads=[mag], writes=[mag])
        V(lambda e: e.tensor_tensor(out=phi[:], in0=dt[:], in1=A_im, op=ALU.mult), reads=[dt, par], writes=[phi])
        HI = 6.28125
        LO = 2.0 * math.pi - HI

        def sin_of(dst, src_t, shift):
            V(lambda e: e.tensor_scalar(out=t_a[:], in0=src_t[:], scalar1=shift, scalar2=None, op0=ALU.add), reads=[src_t], writes=[t_a])
            V(lambda e: e.tensor_scalar(out=ki[:], in0=t_a[:], scalar1=1.0 / (2 * math.pi), scalar2=None, op0=ALU.mult), reads=[t_a], writes=[ki])
            V(lambda e: e.tensor_copy(out=kf[:], in_=ki[:]), reads=[ki], writes=[kf])
            V(lambda e: e.scalar_tensor_tensor(out=r[:], in0=kf[:], scalar=-HI, in1=t_a[:], op0=ALU.mult, op1=ALU.add), reads=[kf, t_a], writes=[r])
            V(lambda e: e.scalar_tensor_tensor(out=r[:], in0=kf[:], scalar=-LO, in1=r[:], op0=ALU.mult, op1=ALU.add), reads=[kf, r], writes=[r])
            V(lambda e: e.tensor_scalar(out=msk[:], in0=r[:], scalar1=math.pi, scalar2=None, op0=ALU.is_gt), reads=[r], writes=[msk])
            V(lambda e: e.scalar_tensor_tensor(out=r[:], in0=msk[:], scalar=-2 * math.pi, in1=r[:], op0=ALU.mult, op1=ALU.add), reads=[msk, r], writes=[r])
            V(lambda e: e.tensor_scalar(out=msk[:], in0=r[:], scalar1=-math.pi, scalar2=None, op0=ALU.is_lt), reads=[r], writes=[msk])
            V(lambda e: e.scalar_tensor_tensor(out=r[:], in0=msk[:], scalar=2 * math.pi, in1=r[:], op0=ALU.mult, op1=ALU.add), reads=[msk, r], writes=[r])
            V(lambda e: e.tensor_scalar(out=r[:], in0=r[:], scalar1=3.1415925, scalar2=-3.1415925, op0=ALU.min, op1=ALU.max), reads=[r], writes=[r])
            kb.act(lambda e: e.activation(out=dst[:], in_=r[:], func=AF.Sin), reads=[r], writes=[dst])
        if sub <= 1:
            return
        sin_of(ai, phi, 0.0)
        sin_of(ar, phi, math.pi / 2)
        if sub <= 2:
            return
        V(lambda e: e.tensor_tensor(out=ar[:], in0=ar[:], in1=mag[:], op=ALU.mult), reads=[ar, mag], writes=[ar])
        V(lambda e: e.tensor_tensor(out=ai[:], in0=ai[:], in1=mag[:], op=ALU.mult), reads=[ai, mag], writes=[ai])
        V(lambda e: e.tensor_scalar(out=nr[:], in0=ar[:], scalar1=-1.0, scalar2=None, op0=ALU.add), reads=[ar], writes=[nr])
        V(lambda e: e.tensor_tensor(out=den[:], in0=A_re, in1=A_re, op=ALU.mult), reads=[par], writes=[den])
        V(lambda e: e.tensor_tensor(out=t_a[:], in0=A_im, in1=A_im, op=ALU.mult), reads=[par], writes=[t_a])
        V(lambda e: e.tensor_tensor(out=den[:], in0=den[:], in1=t_a[:], op=ALU.add), reads=[den, t_a], writes=[den])
        V(lambda e: e.reciprocal(out=den[:], in_=den[:]), reads=[den], writes=[den])
        V(lambda e: e.tensor_tensor(out=cr[:], in0=nr[:], in1=A_re, op=ALU.mult), reads=[nr, par], writes=[cr])
        V(lambda e: e.tensor_tensor(out=t_a[:], in0=ai[:], in1=A_im, op=ALU.mult), reads=[ai, par], writes=[t_a])
        V(lambda e: e.tensor_tensor(out=cr[:], in0=cr[:], in1=t_a[:], op=ALU.add), reads=[cr, t_a], writes=[cr])
        V(lambda e: e.tensor_tensor(out=cr[:], in0=cr[:], in1=den[:], op=ALU.mult), reads=[cr, den], writes=[cr])
        V(lambda e: e.tensor_tensor(out=ci[:], in0=ai[:], in1=A_re, op=ALU.mult), reads=[ai, par], writes=[ci])
        V(lambda e: e.tensor_tensor(out=t_a[:], in0=nr[:], in1=A_im, op=ALU.mult), reads=[nr, par], writes=[t_a])
        V(lambda e: e.tensor_tensor(out=ci[:], in0=ci[:], in1=t_a[:], op=ALU.subtract), reads=[ci, t_a], writes=[ci])
        V(lambda e: e.tensor_tensor(out=ci[:], in0=ci[:], in1=den[:], op=ALU.mult), reads=[ci, den], writes=[ci])
        if sub <= 3:
            return
        tb1 = self.tile([128, NP, 128], F32, "tb1")
        tb2 = self.tile([128, NP, 128], F32, "tb2")

        def bc(tl):
            return tl[:].rearrange("p (a o) -> p a o", o=1).broadcast_to([128, NP, 128])
        V(lambda e: e.tensor_tensor(out=tb1[:], in0=bw[:, :, 0, :], in1=bc(cr), op=ALU.mult), reads=[bw, cr], writes=[tb1])
        V(lambda e: e.tensor_tensor(out=tb2[:], in0=bw[:, :, 1, :], in1=bc(ci), op=ALU.mult), reads=[bw, ci], writes=[tb2])
        V(lambda e: e.tensor_tensor(out=bbar[:, :, 0, :], in0=tb1[:], in1=tb2[:], op=ALU.subtract), reads=[tb1, tb2], writes=[bbar])
        V(lambda e: e.tensor_tensor(out=tb1[:], in0=bw[:, :, 1, :], in1=bc(cr), op=ALU.mult), reads=[bw, cr], writes=[tb1])
        V(lambda e: e.tensor_tensor(out=tb2[:], in0=bw[:, :, 0, :], in1=bc(ci), op=ALU.mult), reads=[bw, ci], writes=[tb2])
        V(lambda e: e.tensor_tensor(out=bbar[:, :, 1, :], in0=tb1[:], in1=tb2[:], op=ALU.add), reads=[tb1, tb2], writes=[bbar])
        if sub <= 4:
            return
        for P in range(NP):
            for c2 in range(2):
                var = int(_os.environ.get("S5VAR", "0"))
                j = (P * 2 + c2) % 4
                if var == 1:
                    j = 0
                if var == 2:
                    j = 4 + (P * 2 + c2) % 4
                pk = self.pb
                kb.pe(lambda e, P=P, c2=c2, j=j: e.transpose(self.pb[:, j * 128:(j + 1) * 128], bbar[:, P, c2, :], self.ident[:]),
                      reads=[bbar, self.ident], writes=[pk])
                kb.act(lambda e, P=P, c2=c2, j=j: e.activation(out=bT[:, P, c2, :], in_=self.pb[:, j * 128:(j + 1) * 128], func=AF.Copy),
                       reads=[pk], writes=[bT])
        if sub <= 5:
            return
        V(lambda e: e.tensor_copy(out=pw[:, 0, 0, :], in_=ar[:]), reads=[ar], writes=[pw])
        V(lambda e: e.tensor_copy(out=pw[:, 0, 1, :], in_=ai[:]), reads=[ai], writes=[pw])
        for d_ in range(1, 12):
            pr, pi_ = pw[:, d_ - 1, 0, :], pw[:, d_ - 1, 1, :]
            V(lambda e, pr=pr: e.tensor_tensor(out=t_a[:], in0=pr, in1=pr, op=ALU.mult), reads=[pw], writes=[t_a])
            V(lambda e, pi_=pi_: e.tensor_tensor(out=t_b[:], in0=pi_, in1=pi_, op=ALU.mult), reads=[pw], writes=[t_b])
            V(lambda e, d_=d_: e.tensor_tensor(out=pw[:, d_, 0, :], in0=t_a[:], in1=t_b[:], op=ALU.subtract), reads=[t_a, t_b], writes=[pw])
            V(lambda e, pr=pr, pi_=pi_: e.tensor_tensor(out=t_a[:], in0=pr, in1=pi_, op=ALU.mult), reads=[pw], writes=[t_a])
            V(lambda e, d_=d_: e.tensor_scalar(out=pw[:, d_, 1, :], in0=t_a[:], scalar1=2.0, scalar2=None, op0=ALU.mult), reads=[t_a], writes=[pw])
        V(lambda e: e.tensor_scalar(out=pw[:, :, 2, :], in0=pw[:, :, 1, :], scalar1=-1.0, scalar2=None, op0=ALU.mult), reads=[pw], writes=[pw])
        V(lambda e: e.memset(pwT[:, 0, 0, :], 1.0), writes=[pwT])
        V(lambda e: e.memset(pwT[:, 0, 1, :], 0.0), writes=[pwT])
        for tau in range(1, 17):
            pr, pi_ = pwT[:, tau - 1, 0, :], pwT[:, tau - 1, 1, :]
            V(lambda e, pr=pr: e.tensor_tensor(out=t_a[:], in0=pr, in1=ar[:], op=ALU.mult), reads=[pwT, ar], writes=[t_a])
            V(lambda e, pi_=pi_: e.tensor_tensor(out=t_b[:], in0=pi_, in1=ai[:], op=ALU.mult), reads=[pwT, ai], writes=[t_b])
            V(lambda e, tau=tau: e.tensor_tensor(out=pwT[:, tau, 0, :], in0=t_a[:], in1=t_b[:], op=ALU.subtract), reads=[t_a, t_b], writes=[pwT])
            V(lambda e, pr=pr: e.tensor_tensor(out=t_a[:], in0=pr, in1=ai[:], op=ALU.mult), reads=[pwT, ai], writes=[t_a])
            V(lambda e, pi_=pi_: e.tensor_tensor(out=t_b[:], in0=pi_, in1=ar[:], op=ALU.mult), reads=[pwT, ar], writes=[t_b])
            V(lambda e, tau=tau: e.tensor_tensor(out=pwT[:, tau, 1, :], in0=t_a[:], in1=t_b[:], op=ALU.add), reads=[t_a, t_b], writes=[pwT])
        V(lambda e: e.tensor_scalar(out=pwT[:, :, 2, :], in0=pwT[:, :, 1, :], scalar1=-1.0, scalar2=None, op0=ALU.mult), reads=[pwT], writes=[pwT])
        kb.barrier()
        hs = [self.tile([128, 8, 512], BF16, "hg") for _ in range(2)]
        for g in range(NG):
            hg = hs[g % 2]
            sl = slice(g * 512, (g + 1) * 512)
            kb.dma(hg[:], self.d_h[:, :, sl], writes=[hg])

            def consume(c, ps, sl=sl):
                kb.act(lambda e: e.activation(out=uT[:, c, sl], in_=ps[:], func=AF.Copy), reads=[ps], writes=[(uT, (c, sl.start))])
            self.proj_feature(wu, 4, hg, consume)
        kb.barrier()
        self.release(m_w)
        TC = 16
        NC = S // TC
        Bu = [self.tile([128, TC, NC], F32, "Bur"), self.tile([128, TC, NC], F32, "Bui")]
        upm = self.tile([128, TC, NC], BF16, "upm")
        SA = [self.tile([128, NC], F32, "SAr"), self.tile([128, NC], F32, "SAi")]
        SB = [self.tile([128, NC], F32, "SBr"), self.tile([128, NC], F32, "SBi")]
        Xb = [self.tile([128, 2, NC + 1], BF16, "Xb") for _ in range(2)]
        obs = [self.tile([128, 17, 2, 128], BF16, "obs") for _ in range(2)]
        tm1 = self.tile([128, 17, 128], F32, "tm1")
        tm2 = self.tile([128, 17, 128], F32, "tm2")
        Kacc = self.tile([128, 16, 128], F32, "Kacc")
        Ksb = self.tile([128, 16, 128], BF16, "Ksb")
        yc = self.tile([128, S], F32, "ycf")
        ztc = self.tile([128, S], BF16, "ztc")
        tgl = Bu[0]
        tglf = Bu[0][:].rearrange("p t c -> p (t c)")
        for xb_ in Xb:
            kb.pool(lambda e, xb_=xb_: e.memset(xb_[:, :, 0:1], 0.0), writes=[xb_])

        def bc_c(ap2):
            return ap2.rearrange("p (o c) -> p o c", o=1).broadcast_to([128, 17, 128])

        def bc_t(ap2):
            return ap2.rearrange("p (t o) -> p t o", o=1).broadcast_to([128, 17, 128])
        ycv = yc[:].rearrange("p (t c) -> p t c", t=TC)
        ztv = ztc[:].rearrange("p (c t) -> p t c", t=TC)
        for ch in range(4):
            kb.pool(lambda e, ch=ch: e.tensor_copy(out=upm[:], in_=uT[:, ch, :].rearrange("p (c t) -> p t c", t=TC)), writes=[upm])
            kb.act(lambda e, ch=ch: e.activation(out=ycv, in_=upm[:], func=AF.Copy, scale=dsk[:, 0, ch:ch + 1]),
                   reads=[dsk, upm], writes=[yc])
            for pq in range(4):
                P = ch * 4 + pq
                xb_ = Xb[P % 2]
                ob_ = obs[P % 2]
                for c2 in range(2):
                    for g in range(NG):
                        ps = self.bank()
                        self.mm(ps[:], bT[:, P, c2, :], upm[:, 2 * g:2 * g + 2, :], True, True, reads=[upm], writes=[ps])
                        kb.act(lambda e, ps=ps, c2=c2, g=g: e.activation(out=Bu[c2][:, 2 * g:2 * g + 2, :], in_=ps[:].rearrange("p (a b) -> p a b", a=2),
                                                                         func=AF.Copy), reads=[ps], writes=[(Bu[c2], g)])
                for j in range(TC):
                    tau = TC - 1 - j
                    s_r = pwT[:, tau, 0, P:P + 1]
                    s_i = pwT[:, tau, 1, P:P + 1]
                    xr, xi = Bu[0][:, j, :], Bu[1][:, j, :]
                    if j == 0:
                        V(lambda e, s_r=s_r, xr=xr: e.tensor_scalar(out=SA[0][:], in0=xr, scalar1=s_r, scalar2=None, op0=ALU.mult), reads=[Bu[0], pwT], writes=[SA[0]])
                        V(lambda e, s_r=s_r, xi=xi: e.tensor_scalar(out=SA[1][:], in0=xi, scalar1=s_r, scalar2=None, op0=ALU.mult), reads=[Bu[1], pwT], writes=[SA[1]])
                    else:
                        V(lambda e, s_r=s_r, xr=xr: e.scalar_tensor_tensor(out=SA[0][:], in0=xr, scalar=s_r, in1=SA[0][:], op0=ALU.mult, op1=ALU.add),
                          reads=[Bu[0], pwT, SA[0]], writes=[SA[0]])
                        V(lambda e, s_r=s_r, xi=xi: e.scalar_tensor_tensor(out=SA[1][:], in0=xi, scalar=s_r, in1=SA[1][:], op0=ALU.mult, op1=ALU.add),
                          reads=[Bu[1], pwT, SA[1]], writes=[SA[1]])
                    s_ni = pwT[:, tau, 2, P:P + 1]
                    V(lambda e, s_ni=s_ni, xi=xi: e.scalar_tensor_tensor(out=SA[0][:], in0=xi, scalar=s_ni, in1=SA[0][:], op0=ALU.mult, op1=ALU.add),
                      reads=[Bu[1], pwT, SA[0]], writes=[SA[0]])
                    V(lambda e, s_i=s_i, xr=xr: e.scalar_tensor_tensor(out=SA[1][:], in0=xr, scalar=s_i, in1=SA[1][:], op0=ALU.mult, op1=ALU.add),
                      reads=[Bu[0], pwT, SA[1]], writes=[SA[1]])
                A, B = SA, SB
                for d_ in range(8):
                    sh = 1 << d_
                    s_ar, s_ai, s_nai = pw[:, 4 + d_, 0, P:P + 1], pw[:, 4 + d_, 1, P:P + 1], pw[:, 4 + d_, 2, P:P + 1]
                    V(lambda e, A=A, B=B, sh=sh, s=s_ar: e.scalar_tensor_tensor(out=B[0][:, sh:], in0=A[0][:, 0:NC - sh], scalar=s, in1=A[0][:, sh:],
                                                                              op0=ALU.mult, op1=ALU.add), reads=[A[0], pw], writes=[B[0]])
                    V(lambda e, A=A, B=B, sh=sh, s=s_nai: e.scalar_tensor_tensor(out=B[0][:, sh:], in0=A[1][:, 0:NC - sh], scalar=s, in1=B[0][:, sh:],
                                                                               op0=ALU.mult, op1=ALU.add), reads=[A[1], B[0], pw], writes=[B[0]])
                    V(lambda e, A=A, B=B, sh=sh, s=s_ar: e.scalar_tensor_tensor(out=B[1][:, sh:], in0=A[1][:, 0:NC - sh], scalar=s, in1=A[1][:, sh:],
                                                                              op0=ALU.mult, op1=ALU.add), reads=[A[1], pw], writes=[B[1]])
                    V(lambda e, A=A, B=B, sh=sh, s=s_ai: e.scalar_tensor_tensor(out=B[1][:, sh:], in0=A[0][:, 0:NC - sh], scalar=s, in1=B[1][:, sh:],
                                                                              op0=ALU.mult, op1=ALU.add), reads=[A[0], B[1], pw], writes=[B[1]])
                    kb.pool(lambda e, A=A, B=B, sh=sh: e.tensor_copy(out=B[0][:, 0:sh], in_=A[0][:, 0:sh]), reads=[A[0]], writes=[B[0]])
                    kb.pool(lambda e, A=A, B=B, sh=sh: e.tensor_copy(out=B[1][:, 0:sh], in_=A[1][:, 0:sh]), reads=[A[1]], writes=[B[1]])
                    A, B = B, A
                for c2 in range(2):
                    kb.pool(lambda e, A=A, c2=c2, xb_=xb_: e.tensor_copy(out=xb_[:, c2, 1:NC + 1], in_=A[c2][:]), reads=[A[c2]], writes=[xb_])
                c0, c1 = bc_c(cw[:, P, 0, :]), bc_c(cw[:, P, 1, :])
                p_r, p_i = bc_t(pwT[:, :, 0, P]), bc_t(pwT[:, :, 1, P])
                V(lambda e, c0=c0, p_r=p_r: e.tensor_tensor(out=tm1[:], in0=c0, in1=p_r, op=ALU.mult), reads=[cw, pwT], writes=[tm1])
                V(lambda e, c1=c1, p_i=p_i: e.tensor_tensor(out=tm2[:], in0=c1, in1=p_i, op=ALU.mult), reads=[cw, pwT], writes=[tm2])
                V(lambda e, ob_=ob_: e.tensor_tensor(out=ob_[:, :, 0, :], in0=tm1[:], in1=tm2[:], op=ALU.add), reads=[tm1, tm2], writes=[ob_])
                V(lambda e, c1=c1, p_r=p_r: e.tensor_tensor(out=tm1[:], in0=c1, in1=p_r, op=ALU.mult), reads=[cw, pwT], writes=[tm1])
                V(lambda e, c0=c0, p_i=p_i: e.tensor_tensor(out=tm2[:], in0=c0, in1=p_i, op=ALU.mult), reads=[cw, pwT], writes=[tm2])
                V(lambda e, ob_=ob_: e.tensor_tensor(out=ob_[:, :, 1, :], in0=tm1[:], in1=tm2[:], op=ALU.subtract), reads=[tm1, tm2], writes=[ob_])
                for i in range(TC):
                    ps = self.bank()
                    self.mm(ps[:, 0:NC], ob_[:, i + 1, 0, :], xb_[:, 0, 0:NC], True, False, reads=[ob_, xb_], writes=[ps])
                    self.mm(ps[:, 0:NC], ob_[:, i + 1, 1, :], xb_[:, 1, 0:NC], False, True, reads=[ob_, xb_], writes=[ps])
                    V(lambda e, ps=ps, i=i: e.tensor_tensor(out=ycv[:, i, :], in0=ycv[:, i, :], in1=ps[:, 0:NC], op=ALU.add), reads=[ps, yc], writes=[yc])
                for t4 in range(4):
                    ps = self.bank()
                    self.mm(ps[:], bbar[:, P, 0, :], ob_[:, t4 * 4:(t4 + 1) * 4, 0, :], True, False, reads=[ob_, bbar], writes=[ps])
                    self.mm(ps[:], bbar[:, P, 1, :], ob_[:, t4 * 4:(t4 + 1) * 4, 1, :], False, True, reads=[ob_, bbar], writes=[ps])
                    kv = Kacc[:, t4 * 4:(t4 + 1) * 4, :]
                    if pq == 0:
                        kb.act(lambda e, ps=ps, kv=kv: e.activation(out=kv, in_=ps[:].rearrange("p (a b) -> p a b", a=4), func=AF.Copy),
                               reads=[ps], writes=[(Kacc, t4)])
                    else:
                        V(lambda e, ps=ps, kv=kv: e.tensor_tensor(out=kv, in0=kv, in1=ps[:].rearrange("p (a b) -> p a b", a=4), op=ALU.add),
                          reads=[ps, (Kacc, t4)], writes=[(Kacc, t4)])
            kb.pool(lambda e: e.tensor_copy(out=Ksb[:], in_=Kacc[:]), reads=[Kacc], writes=[Ksb])
            for i in range(TC):
                ps = self.bank()
                for j in range(i + 1):
                    self.mm(ps[:, 0:NC], Ksb[:, i - j, :], upm[:, j, :], j == 0, j == i, reads=[Ksb, upm], writes=[ps])
                V(lambda e, ps=ps, i=i: e.tensor_tensor(out=ycv[:, i, :], in0=ycv[:, i, :], in1=ps[:, 0:NC], op=ALU.add), reads=[ps, yc], writes=[yc])
            for g in range(NG):
                sl = slice(g * 512, (g + 1) * 512)
                kb.pool(lambda e, sl=sl, g=g: e.tensor_tensor(out=tglf[:, sl], in0=yc[:, sl], in1=yc[:, sl], op=ALU.mult), reads=[yc], writes=[(tgl, g)])
                kb.pool(lambda e, sl=sl, g=g: e.tensor_scalar(out=tglf[:, sl], in0=tglf[:, sl], scalar1=0.044715, scalar2=1.0, op0=ALU.mult, op1=ALU.add),
                        reads=[(tgl, g)], writes=[(tgl, g)])
                kb.pool(lambda e, sl=sl, g=g: e.tensor_tensor(out=tglf[:, sl], in0=tglf[:, sl], in1=yc[:, sl], op=ALU.mult), reads=[(tgl, g), yc],
                        writes=[(tgl, g)])
                kb.act(lambda e, sl=sl, g=g: e.activation(out=tglf[:, sl], in_=tglf[:, sl], func=AF.Sigmoid, scale=1.5957691216), reads=[(tgl, g)],
                       writes=[(tgl, g)])
                kb.pool(lambda e, sl=sl, g=g: e.tensor_tensor(out=ztv[:, 2 * g:2 * g + 2, :], in0=tglf[:, sl].rearrange("p (a b) -> p a b", a=2),
                                                              in1=yc[:, sl].rearrange("p (a b) -> p a b", a=2), op=ALU.mult),
                        reads=[(tgl, g), yc], writes=[(ztc, sl.start)])
            kb.dma(self.d_z[:, ch, :], ztc[:], reads=[ztc])
        kb.barrier()
        self.release(m_w)
        sg = [self.tile([128, 512], F32, "sg") for _ in range(2)]
        og = [self.tile([128, 4, 512], BF16, "og") for _ in range(2)]
        zg = [self.tile([128, 4, 512], BF16, "zg") for _ in range(2)]
        for g in range(NG):
            sl = slice(g * 512, (g + 1) * 512)
            o_ = og[g % 2]
            z_ = zg[g % 2]
            kb.dma(z_[:], self.d_z[:, :, sl], writes=[z_])
            for co in range(4):
                ps = self.bank()
                for k4 in range(4):
                    self.mm(ps[:], glu[:, k4, co * 128:(co + 1) * 128], z_[:, k4, :], k4 == 0, k4 == 3, reads=[z_], writes=[ps])
                s_ = sg[co % 2]
                kb.act(lambda e, ps=ps, s_=s_, co=co: e.activation(out=s_[:], in_=ps[:], func=AF.Sigmoid, bias=dsk[:, 1, co:co + 1]),
                       reads=[ps, dsk], writes=[s_])
                V(lambda e, s_=s_, co=co, o_=o_, z_=z_: e.tensor_tensor(out=o_[:, co, :], in0=s_[:], in1=z_[:, co, :], op=ALU.mult),
                  reads=[s_, z_], writes=[(o_, co)])
            kb.dma(self.d_yc[:, :, sl], o_[:], reads=[o_])

    def ph_merge(self, l):
        kb = self.kb
        wgm = self.tile([128, 8, 3072], BF16, "wgm")
        wbr = self.tile([128, 3, 4, D], BF16, "wbr")
        wo = self.tile([128, 8, D], BF16, "wo")
        m_st = self.mark()
        self.setup_stage(4096)
        for k in range(8):
            self.load_w_bf16(wgm, wgm[:, k, :], self.i_wgm[l, k * 128:(k + 1) * 128, :], [128, 3072])
        for b in range(3):
            self.load_w_bf16(wbr, wbr[:, b, :, :], self.i_wbr[l, b].rearrange("(k p) c -> p k c", p=128), [128, 4, D])
        for k2 in range(2):
            self.load_w_bf16(wo, wo[:, k2 * 4:(k2 + 1) * 4, :], self.i_wout[l, k2 * 512:(k2 + 1) * 512, :].rearrange("(k p) c -> p k c", p=128), [128, 4, D])
        kb.barrier()
        self.release(m_st)
        hs = [self.tile([128, 8, 512], BF16, "hg") for _ in range(2)]
        ys = [[self.tile([128, 4, 512], BF16, "yg") for _ in range(3)] for _ in range(2)]
        xs = [self.tile([128, 8, 512], F32, "xg") for _ in range(2)]
        mg = self.tile([128, 8, 512], BF16, "mg")
        sgt = [self.tile([128, 512], F32, "sgt") for _ in range(3)]
        acc = self.tile([128, 512], F32, "acc")
        tmp = self.tile([128, 512], F32, "tmp")
        ysrc = [self.d_ya, self.d_yb, self.d_yc]
        for g in range(NG):
            sl = slice(g * 512, (g + 1) * 512)
            hg, xg, yg = hs[g % 2], xs[g % 2], ys[g % 2]
            kb.dma(hg[:], self.d_h[:, :, sl], writes=[hg])
            kb.dma(xg[:], self.d_x[:, :, sl], writes=[xg])
            for b in range(3):
                kb.dma(yg[b][:], ysrc[b][:, :, sl], writes=[yg[b]])
            for dc in range(8):
                cs_ = slice(dc * 128, (dc + 1) * 128)
                for b in range(3):
                    ps = self.bank()
                    for k in range(8):
                        self.mm(ps[:], wgm[:, k, b * D + dc * 128:b * D + (dc + 1) * 128], hg[:, k, :], k == 0, k == 7, reads=[hg], writes=[ps])
                    kb.act(lambda e, ps=ps, b=b: e.activation(out=sgt[b][:], in_=ps[:], func=AF.Sigmoid), reads=[ps], writes=[sgt[b]])
                for b in range(3):
                    ps = self.bank()
                    for k in range(4):
                        self.mm(ps[:], wbr[:, b, k, cs_], yg[b][:, k, :], k == 0, k == 3, reads=[yg[b]], writes=[ps])
                    if b == 0:
                        kb.dve(lambda e, ps=ps: e.tensor_tensor(out=acc[:], in0=sgt[0][:], in1=ps[:], op=ALU.mult), reads=[sgt[0], ps], writes=[acc])
                    elif b == 1:
                        kb.dve(lambda e, ps=ps: e.tensor_tensor(out=tmp[:], in0=sgt[1][:], in1=ps[:], op=ALU.mult), reads=[sgt[1], ps], writes=[tmp])
                        kb.dve(lambda e: e.tensor_tensor(out=acc[:], in0=acc[:], in1=tmp[:], op=ALU.add), reads=[acc, tmp], writes=[acc])
                    else:
                        kb.dve(lambda e, ps=ps: e.tensor_tensor(out=tmp[:], in0=sgt[2][:], in1=ps[:], op=ALU.mult), reads=[sgt[2], ps], writes=[tmp])
                        kb.dve(lambda e, dc=dc: e.tensor_tensor(out=mg[:, dc, :], in0=acc[:], in1=tmp[:], op=ALU.add), reads=[acc, tmp], writes=[(mg, dc)])
            for do in range(8):
                ps = self.bank()
                for k in range(8):
                    self.mm(ps[:], wo[:, k, do * 128:(do + 1) * 128], mg[:, k, :], k == 0, k == 7, reads=[(mg, k)], writes=[ps])
                kb.dve(lambda e, ps=ps, do=do, xg=xg: e.tensor_tensor(out=xg[:, do, :], in0=xg[:, do, :], in1=ps[:], op=ALU.add),
                       reads=[ps, (xg, do)], writes=[(xg, do)])
            kb.dma(self.d_x[:, :, sl], xg[:], reads=[xg])

    def ph_mlp(self, l):
        kb = self.kb
        wup = self.tile([128, 8, 4 * D], BF16, "wup")
        wdn = self.tile([128, 32, D], BF16, "wdn")
        gain = self.tile([128, 8], F32, "gain")
        m_st = self.mark()
        self.setup_stage(4096)
        for k in range(8):
            self.load_w_bf16(wup, wup[:, k, :], self.i_wup[l, k * 128:(k + 1) * 128, :], [128, 4096])
        for k4 in range(8):
            self.load_w_bf16(wdn, wdn[:, k4 * 4:(k4 + 1) * 4, :], self.i_wdn[l, k4 * 512:(k4 + 1) * 512, :].rearrange("(k p) c -> p k c", p=128), [128, 4, D])
        kb.dma(gain[:], self.i_gains[L + l], writes=[gain])
        kb.barrier()
        self.release(m_st)
        gain2 = self.tile([128, 8], F32, "gain2")
        if l + 1 < self.nl:
            kb.dma(gain2[:], self.i_gains[l + 1], writes=[gain2])
        xs = [self.tile([128, 8, 512], F32, "xg") for _ in range(1)]
        hg = self.tile([128, 8, 512], BF16, "hg")
        rstd = self.tile([128, 512], F32, "rstd")
        a = self.tile([128, 32, 512], BF16, "aT")
        sq = a
        rl = [self.tile([128, 512], F32, "rl") for _ in range(2)]
        for g in range(NG):
            sl = slice(g * 512, (g + 1) * 512)
            xg = xs[0]
            kb.dma(xg[:], self.d_x[:, :, sl], writes=[xg])
            self.rmsnorm_group(xg, hg, gain, sq, rstd)
            for f in range(32):
                ps = self.bank()
                for k in range(8):
                    self.mm(ps[:], wup[:, k, f * 128:(f + 1) * 128], hg[:, k, :], k == 0, k == 7, reads=[(hg, k)], writes=[ps])
                r_ = rl[f % 2]
                kb.act(lambda e, ps=ps, r_=r_: e.activation(out=r_[:], in_=ps[:], func=AF.Relu), reads=[ps], writes=[r_])
                kb.pool(lambda e, r_=r_, f=f: e.tensor_tensor(out=a[:, f, :], in0=r_[:], in1=r_[:], op=ALU.mult), reads=[r_], writes=[(a, f)])
            for do in range(8):
                ps = self.bank()
                for f in range(32):
                    self.mm(ps[:], wdn[:, f, do * 128:(do + 1) * 128], a[:, f, :], f == 0, f == 31, reads=[(a, f)], writes=[ps])
                kb.dve(lambda e, ps=ps, do=do, xg=xg: e.tensor_tensor(out=xg[:, do, :], in0=xg[:, do, :], in1=ps[:], op=ALU.add),
                       reads=[ps, (xg, do)], writes=[(xg, do)])
            kb.dma(self.d_x[:, :, sl], xg[:], reads=[xg])
            if l + 1 < self.nl and self.phases is None:
                self.rmsnorm_group(xg, hg, gain2, sq, rstd)
                kb.dma(self.d_h[:, :, sl], hg[:], reads=[hg])

    def ph_final(self):
        kb = self.kb
        gain = self.tile([128, 8], F32, "gain")
        kb.dma(gain[:], self.i_gains[2 * L], writes=[gain])
        xs = [self.tile([128, 8, 512], F32, "xg") for _ in range(2)]
        os_ = [self.tile([128, 8, 512], F32, "og") for _ in range(2)]
        sqs = [self.tile([128, 8, 512], BF16, "sq") for _ in range(2)]
        rstds = [self.tile([128, 512], F32, "rstd") for _ in range(2)]
        for g in range(NG):
            sl = slice(g * 512, (g + 1) * 512)
            xg, og = xs[g % 2], os_[g % 2]
            kb.dma(xg[:], self.d_x[:, :, sl], writes=[xg])
            self.rmsnorm_group(xg, og, gain, sqs[g % 2], rstds[g % 2])
            kb.dma(self.o_out[:, :, sl], og[:], reads=[og])


def prep_shared(inp):
    f = np.float32
    w_in = np.asarray(inp["w_in"], f)
    o = {}
    o["gains"] = np.ascontiguousarray(np.concatenate([
        np.asarray(inp["norm_mix"], f).reshape(L, 8, 128).transpose(0, 2, 1),
        np.asarray(inp["norm_mlp"], f).reshape(L, 8, 128).transpose(0, 2, 1),
        np.asarray(inp["norm_final"], f).reshape(1, 8, 128).transpose(0, 2, 1)], 0))
    qa = w_in[:, :, 0:512].reshape(L, D, 8, 64)
    qa_perm = np.stack([np.concatenate([qa[:, :, c], qa[:, :, 4 + c]], -1) for c in range(4)], 2).reshape(L, D, 512)
    kv = w_in[:, :, 512:1280]
    k0, v0 = kv[:, :, 0:128], kv[:, :, 128:256]
    k1, v1 = kv[:, :, 256:384], kv[:, :, 384:512]
    k2, v2 = kv[:, :, 512:640], kv[:, :, 640:768]
    ga = w_in[:, :, 1280:1304]
    o["w_nsa_f"] = np.ascontiguousarray(np.concatenate([qa_perm, k0, k1, k2, v0], -1))
    o["w_nsa_t"] = np.ascontiguousarray(np.concatenate([v1, v2, ga], -1))
    qb = w_in[:, :, 1304:1816]
    kvb = w_in[:, :, 1816:1944]
    o["w_swa_f"] = np.ascontiguousarray(np.concatenate([qb, kvb[:, :, 0:64], kvb[:, :, 0:64]], -1))
    o["w_swa_t"] = np.ascontiguousarray(kvb[:, :, 64:128])
    o["w_u"] = np.ascontiguousarray(w_in[:, :, 1944:2456])
    o["w_gm"] = np.ascontiguousarray(w_in[:, :, 2456:5528])
    o["cw1"] = np.ascontiguousarray(np.asarray(inp["nsa_cmp_w1"], f).reshape(L, 2, 32, 64, 128).transpose(0, 1, 3, 2, 4))
    o["cw2"] = np.ascontiguousarray(np.asarray(inp["nsa_cmp_w2"], f))
    o["cposT"] = np.ascontiguousarray(np.asarray(inp["nsa_cmp_pos"], f).transpose(0, 3, 1, 2))
    o["sinks"] = np.ascontiguousarray(np.asarray(inp["swa_sinks"], f).reshape(L, 1, 8))
    def pl(a):
        return np.asarray(a, f).reshape(L, 16, 2, 64).transpose(0, 2, 3, 1).reshape(L, 128, 16)
    ldt = np.repeat(np.asarray(inp["s5_log_dt"], f)[:, :, None], 64, 2)
    o["s5a"] = np.ascontiguousarray(np.stack([pl(inp["s5_a_re"]), pl(inp["s5_a_im"]), pl(ldt)], 2))
    def bl(a):
        a = np.asarray(a, f).reshape(L, 16, 2, 64, 16)
        out = np.zeros((L, 2, 64, 16, 4, 2, 16), f)
        for P in range(16):
            for g2 in range(2):
                out[:, g2, :, P, P % 4, g2, :] = a[:, P, g2]
        return out.reshape(L, 128, 16, 128)
    o["s5b"] = np.ascontiguousarray(np.stack([bl(inp["s5_b_re"]), bl(inp["s5_b_im"])], 3))
    cre = np.asarray(inp["s5_c_re"], f).transpose(0, 1, 3, 2)
    cim = np.asarray(inp["s5_c_im"], f).transpose(0, 1, 3, 2)
    o["s5c"] = np.ascontiguousarray(np.stack([bl(cre), bl(cim)], 3))
    o["s5d"] = np.ascontiguousarray(np.stack([np.asarray(inp["s5_d"], f).reshape(L, 4, 128).transpose(0, 2, 1),
                                              np.asarray(inp["s5_glu_b"], f).reshape(L, 4, 128).transpose(0, 2, 1)], 2))
    o["glu_w"] = np.ascontiguousarray(np.asarray(inp["s5_glu_w"], f))
    o["w_br"] = np.ascontiguousarray(np.stack([np.asarray(inp["w_branch_a"], f), np.asarray(inp["w_branch_b"], f),
                                               np.asarray(inp["w_branch_c"], f)], 1))
    o["w_out"] = np.ascontiguousarray(np.asarray(inp["w_out"], f))
    o["w_up"] = np.ascontiguousarray(np.asarray(inp["w_mlp_up"], f))
    o["w_dn"] = np.ascontiguousarray(np.asarray(inp["w_mlp_down"], f))
    n = np.arange(256)[:, None]
    j = np.arange(64)[None, :]
    ov = np.minimum(16 * n + 32, 64 * j + 64) - np.maximum(16 * n, 64 * j)
    cs = np.clip(ov, 0, None).astype(f) / 32.0
    cs[255] = 0
    o["csel"] = cs
    qpos = np.arange(S)
    cur = (qpos // 64)[:, None]
    jj = np.arange(64)[None, :]
    valid = jj <= cur
    forced = (jj == 0) | (jj == cur) | (jj == cur - 1)
    o["nfv"] = np.ascontiguousarray((valid & ~forced).astype(f).reshape(NT, 128, 64))
    o["addend"] = np.ascontiguousarray(np.where(forced & valid, 1e4, np.where(valid, 0.0, -1e30)).astype(f).reshape(NT, 128, 64))
    p = np.arange(128)
    invf = (10000.0 ** (-(np.arange(32, dtype=f)) / 32.0)).astype(f)
    rc = np.zeros((128, 2), f)
    rc[:, 0] = invf[p % 32]
    rc[:, 1] = np.where((p % 64) < 32, -1.0, 1.0)
    o["ropec"] = rc
    return o


_CACHE = {}


def get_prog(key, **kw):
    if key not in _CACHE:
        pr = Prog(**kw)
        nc, n = pr.build()
        _CACHE[key] = (pr, nc)
    return _CACHE[key]


def kernel(**inputs):
    shared = prep_shared(inputs)
    x = np.asarray(inputs["x"], np.float32)
    pos = np.asarray(inputs["positions"], np.int32)
    pr, nc = get_prog("full")
    in_maps = []
    for b in range(8):
        m = dict(shared)
        m["xT_in"] = np.ascontiguousarray(x[b].T.reshape(8, 128, S).transpose(1, 0, 2))
        m["pos"] = np.ascontiguousarray(pos[b].reshape(1, S))
        in_maps.append(m)
    res = run_bass_kernel_spmd(nc, in_maps, core_ids=list(range(8)))
    out = np.empty((8, S, D), np.float32)
    for b in range(8):
        o = np.asarray(res.results[b]["outT"])
        out[b] = o.transpose(2, 1, 0).reshape(S, D)
    return out
```

```python
import math
import numpy as np
import concourse.bass as bass
import concourse.mybir as mybir
from concourse.bass_utils import run_bass_kernel_spmd

F32 = mybir.dt.float32
BF16 = mybir.dt.bfloat16
I32 = mybir.dt.int32
AF = mybir.ActivationFunctionType
ALU = mybir.AluOpType
AX = mybir.AxisListType

S = 4096
D = 1024
L = 4
NT = 32
NG = 8
NEG = -30000.0
SB_BASE = 16640
SB_TOP = 229312

NSLOT = 8
SEM_CHUNK = 2000


class _St:
    __slots__ = ("w", "r")

    def __init__(self):
        self.w = None
        self.r = []


class T:
    def __init__(self, h, name=""):
        self.h = h
        self.name = name
        self.whole = _St()
        self.parts = {}

    def __getitem__(self, k):
        return self.h[k]


class Op:
    __slots__ = ("stream", "vq", "vidx", "fn", "deps", "waits", "signaled", "sem", "val", "isdma")


class KB:
    def __init__(self, nc):
        self.nc = nc
        self.ops = []
        self.vq_count = {}
        self.vq_last = {}
        self.dma_n = {"sp": 0, "act": 0, "pool": 0}
        self.pending = {}
        self.streams = ["pe", "act", "dve", "pool", "sp"]

    def _acc(self, lst):
        out = []
        for a in lst:
            if isinstance(a, T):
                out.append((a, None))
            else:
                out.append(a)
        return out

    def op(self, stream, fn, reads=(), writes=(), dma=False):
        o = Op()
        o.stream = stream
        o.fn = fn
        o.isdma = dma
        if dma:
            n = self.dma_n[stream]
            self.dma_n[stream] = n + 1
            o.vq = "%s_d%d" % (stream, n % NSLOT)
        else:
            o.vq = stream
        o.vidx = self.vq_count.get(o.vq, 0)
        self.vq_count[o.vq] = o.vidx + 1
        deps = set()
        if dma and o.vq in self.vq_last:
            deps.add(self.vq_last[o.vq])
        self.vq_last[o.vq] = o
        pb = self.pending.pop(stream, None)
        if pb:
            deps.update(pb)
        for (t, k) in self._acc(reads):
            sts = [t.whole]
            if k is None:
                sts += list(t.parts.values())
            else:
                if k not in t.parts:
                    t.parts[k] = _St()
                sts.append(t.parts[k])
            for s in sts:
                if s.w is not None:
                    deps.add(s.w)
            (t.whole if k is None else t.parts[k]).r.append(o)
        for (t, k) in self._acc(writes):
            sts = [t.whole]
            if k is None:
                sts += list(t.parts.values())
            else:
                if k not in t.parts:
                    t.parts[k] = _St()
                sts.append(t.parts[k])
            for s in sts:
                if s.w is not None:
                    deps.add(s.w)
                deps.update(s.r)
            if k is None:
                t.parts = {}
                t.whole.w = o
                t.whole.r = []
            else:
                st = t.parts[k]
                st.w = o
                st.r = []
        deps.discard(o)
        o.deps = deps
        o.waits = []
        o.signaled = False
        self.ops.append(o)
        return o

    def barrier(self):
        lasts = list(self.vq_last.values())
        for s in self.streams:
            self.pending[s] = set(lasts) | self.pending.get(s, set())

    def pe(self, fn, reads=(), writes=()):
        return self.op("pe", fn, reads, writes)

    def act(self, fn, reads=(), writes=()):
        return self.op("act", fn, reads, writes)

    def dve(self, fn, reads=(), writes=()):
        return self.op("dve", fn, reads, writes)

    def pool(self, fn, reads=(), writes=()):
        return self.op("pool", fn, reads, writes)

    def dma(self, out, in_, reads=(), writes=(), q="sp", **kw):
        return self.op(q, lambda e: e.dma_start(out=out, in_=in_, **kw), reads, writes, dma=True)

    def emit(self):
        from contextlib import ExitStack
        nc = self.nc
        waited = {}
        for c in self.ops:
            for p in sorted(c.deps, key=lambda x: x.vidx):
                if p.stream == c.stream and c.stream == "pe" and not p.isdma:
                    continue
                key = (c.stream, p.vq)
                if waited.get(key, -1) >= p.vidx:
                    continue
                waited[key] = p.vidx
                p.signaled = True
                c.waits.append(p)
        lasts = list(self.vq_last.values())
        for p in lasts:
            p.signaled = True
        cnt = {}
        need = {}
        for o in self.ops:
            if o.signaled:
                n = cnt.get(o.vq, 0)
                cnt[o.vq] = n + 1
                o.sem = (o.vq, n // SEM_CHUNK)
                o.val = (n % SEM_CHUNK + 1) * (16 if o.isdma else 1)
                need[o.sem] = True
        with ExitStack() as es:
            sems = {}
            for s in sorted(need.keys()):
                sems[s] = es.enter_context(nc.semaphore("s_%s_%d" % s))
            per = {s: [o for o in self.ops if o.stream == s] for s in self.streams}
            block = es.enter_context(nc.Block())

            def run(stream, eng):
                for o in per[stream]:
                    w = {}
                    for p in o.waits:
                        w[p.sem] = max(w.get(p.sem, 0), p.val)
                    for s, v in w.items():
                        eng.wait_ge(sems[s], v)
                    ins = o.fn(eng)
                    if o.signaled:
                        ins.then_inc(sems[o.sem], 16 if o.isdma else 1)
                if stream == "sp":
                    for p in lasts:
                        eng.wait_ge(sems[p.sem], p.val)

            @block.tensor
            def _(e):
                run("pe", e)

            @block.scalar
            def _(e):
                run("act", e)

            @block.vector
            def _(e):
                run("dve", e)

            @block.gpsimd
            def _(e):
                run("pool", e)

            @block.sync
            def _(e):
                run("sp", e)
        return len(self.ops)


class Prog:
    def __init__(self, n_layers=L, debug=False, phases=None):
        self.nl = n_layers
        self.debug = debug
        self.phases = phases
        nc = bass.Bass("TRN2", target_bir_lowering=False)
        self.nc = nc
        self.kb = KB(nc)
        self.sb_off = SB_BASE
        self.uid = 0
        self.arena = nc.alloc_sbuf_tensor_at("arena", [128, (SB_TOP - SB_BASE) // 2], BF16, offset=SB_BASE)
        self.decl_io()
        self.alloc_psum()

    def tile(self, shape, dt, name="t"):
        self.uid += 1
        nb = 2 if dt == BF16 else 4
        n = 1
        for s in shape[1:]:
            n *= s
        size = (n * nb + 63) // 64 * 64
        off = self.sb_off
        assert off + size <= SB_TOP, ("SBUF overflow", name, off, size)
        self.sb_off += size
        lo = (off - SB_BASE) // 2
        flat = self.arena[:, lo:lo + size // 2]
        if dt != BF16:
            flat = flat.bitcast(dt)
        v = flat[:, 0:n]
        sh = list(shape)
        if len(sh) == 3:
            v = v.rearrange("p (a b) -> p a b", a=sh[1])
        elif len(sh) == 4:
            v = v.rearrange("p (a b c) -> p a b c", a=sh[1], b=sh[2])
        elif len(sh) == 5:
            v = v.rearrange("p (a b c d) -> p a b c d", a=sh[1], b=sh[2], c=sh[3])
        assert sh[0] == 128
        return T(v, name)

    def mark(self):
        return self.sb_off

    def release(self, m):
        self.sb_off = m

    def dram_in(self, name, shape, dt=F32):
        return self.nc.dram_tensor(name, list(shape), dt, kind="ExternalInput")

    def dram_scr(self, name, shape, dt, out=False):
        kind = "ExternalOutput" if (out or self.debug) else "Internal"
        return self.nc.dram_tensor(name, list(shape), dt, kind=kind)

    def decl_io(self):
        nl = self.nl
        d = self.dram_in
        self.i_x = d("xT_in", [128, 8, S])
        self.i_pos = d("pos", [1, S], I32)
        self.i_gains = d("gains", [2 * L + 1, 128, 8])
        self.i_wnf = d("w_nsa_f", [L, D, 1024])
        self.i_wnt = d("w_nsa_t", [L, D, 280])
        self.i_wsf = d("w_swa_f", [L, D, 640])
        self.i_wst = d("w_swa_t", [L, D, 64])
        self.i_wu = d("w_u", [L, D, 512])
        self.i_wgm = d("w_gm", [L, D, 3072])
        self.i_cw1 = d("cw1", [L, 2, 64, 32, 128])
        self.i_cw2 = d("cw2", [L, 2, 128, 64])
        self.i_cpos = d("cposT", [L, 64, 2, 32])
        self.i_sinks = d("sinks", [L, 1, 8])
        self.i_s5a = d("s5a", [L, 128, 3, 16])
        self.i_s5b = d("s5b", [L, 128, 16, 2, 128])
        self.i_s5c = d("s5c", [L, 128, 16, 2, 128])
        self.i_s5d = d("s5d", [L, 128, 2, 4])
        self.i_glu = d("glu_w", [L, 512, 512])
        self.i_wbr = d("w_br", [L, 3, 512, D])
        self.i_wout = d("w_out", [L, D, D])
        self.i_wup = d("w_up", [L, D, 4 * D])
        self.i_wdn = d("w_dn", [L, 4 * D, D])
        self.i_csel = d("csel", [256, 64])
        self.i_nfv = d("nfv", [NT, 128, 64])
        self.i_add = d("addend", [NT, 128, 64])
        self.i_rc = d("ropec", [128, 2])
        s = self.dram_scr
        self.o_out = self.nc.dram_tensor("outT", [128, 8, S], F32, kind="ExternalOutput")
        self.d_x = s("xT", [128, 8, S], F32)
        self.d_h = s("hT", [128, 8, S], BF16)
        self.d_cos = s("cosT", [128, S], F32)
        self.d_sin = s("sinT", [128, S], F32)
        self.d_ya = s("yaT", [128, 4, S], BF16)
        self.d_yb = s("ybT", [128, 4, S], BF16)
        self.d_yc = s("ycT", [128, 4, S], BF16)
        self.d_dbg = s("dbg", [128, 8192], F32)
        self.d_z = s("zT", [128, 4, S], BF16)

    def alloc_psum(self):
        nc = self.nc
        self.pf = [T(nc.alloc_psum_tensor("pf%d" % i, [128, 512], F32), "pf%d" % i) for i in range(7)]
        self.pb = T(nc.alloc_psum_tensor("pb", [128, 1024], BF16), "pb")
        self.pf_rr = 0

    def bank(self, lo=0, hi=7):
        n = hi - lo
        b = self.pf[lo + (self.pf_rr % n)]
        self.pf_rr += 1
        return b

    def mm(self, out, lhsT, rhs, start, stop, reads=(), writes=(), **kw):
        return self.kb.pe(lambda e: e.matmul(out, lhsT=lhsT, rhs=rhs, start=start, stop=stop, **kw), reads, writes)

    def load_w_bf16(self, dst, dst_ap, src_ap, shape, eng="pool"):
        st = self.stage[self.stage_i % len(self.stage)]
        self.stage_i += 1
        n = 1
        for s_ in shape[1:]:
            n *= s_
        assert n <= self.stage_n, (n, self.stage_n)
        flat = st[:, 0:n]
        if len(shape) == 3:
            sv = flat.rearrange("p (a b) -> p a b", a=shape[1])
        elif len(shape) == 4:
            sv = flat.rearrange("p (a b c) -> p a b c", a=shape[1], b=shape[2])
        else:
            sv = flat
        self.kb.dma(sv, src_ap, writes=[st])
        if self.stage_i % 2 == 0:
            self.kb.dve(lambda e: e.tensor_copy(out=dst_ap, in_=sv), reads=[st], writes=[dst])
        else:
            self.kb.act(lambda e: e.activation(out=dst_ap, in_=sv, func=AF.Copy), reads=[st], writes=[dst])

    def setup_stage(self, n_elems, nbuf=2):
        self.stage = [self.tile([128, n_elems], F32, "stage") for _ in range(nbuf)]
        self.stage_n = n_elems
        self.stage_i = 0

    def build(self):
        kb = self.kb
        self.consts()
        base = self.mark()
        for l in range(self.nl):
            for ph, fn in (("norm", self.ph_norm), ("nsa", self.ph_nsa), ("swa", self.ph_swa),
                           ("s5", self.ph_s5), ("merge", self.ph_merge), ("mlp", self.ph_mlp)):
                if self.phases is not None and ph not in self.phases:
                    continue
                if ph == "norm" and l > 0 and self.phases is None:
                    continue
                kb.barrier()
                self.release(base)
                fn(l)
        kb.barrier()
        self.release(base)
        if self.phases is None or "final" in self.phases:
            self.ph_final()
        kb.barrier()
        n = kb.emit()
        return self.nc, n

    def consts(self):
        kb = self.kb
        nc = self.nc
        for k in range(8):
            kb.dma(self.d_x[:, k, :], self.i_x[:, k, :])
        self.ident = self.tile([128, 128], BF16, "ident")
        self.identf = self.tile([128, 128], F32, "identf")
        self.ones = self.tile([128, 128], BF16, "ones")
        self.zeros = self.tile([128, 512], BF16, "zeros")
        self.maskD = self.tile([128, 4, 128], BF16, "maskD")
        self.maskU = self.tile([128, 4, 128], BF16, "maskU")
        self.ewide = self.tile([128, S], BF16, "ewide")
        self.rc = self.tile([128, 2], F32, "rc")
        self.eps = self.tile([128, 1], F32, "eps")
        kb.pool(lambda e: e.memset(self.ident[:], 1.0), writes=[self.ident])
        kb.pool(lambda e: e.affine_select(out=self.ident[:], in_=self.ident[:], pattern=[[1, 128]],
                                          compare_op=ALU.is_equal, fill=0.0, base=0, channel_multiplier=-1),
                reads=[self.ident], writes=[self.ident])
        kb.pool(lambda e: e.memset(self.identf[:], 1.0), writes=[self.identf])
        kb.pool(lambda e: e.affine_select(out=self.identf[:], in_=self.identf[:], pattern=[[1, 128]],
                                          compare_op=ALU.is_equal, fill=0.0, base=0, channel_multiplier=-1),
                reads=[self.identf], writes=[self.identf])
        self.permf = self.tile([128, 128], F32, "permf")
        for q4 in range(4):
            src = q4 ^ 1
            kb.pool(lambda e, q4=q4, src=src: e.tensor_copy(out=self.permf[:, 32 * q4:32 * q4 + 32], in_=self.identf[:, 32 * src:32 * src + 32]),
                    reads=[self.identf], writes=[self.permf])
        kb.pool(lambda e: e.memset(self.ones[:], 1.0), writes=[self.ones])
        kb.pool(lambda e: e.memset(self.zeros[:], 0.0), writes=[self.zeros])
        kb.pool(lambda e: e.memset(self.eps[:], 1e-6), writes=[self.eps])
        kb.pool(lambda e: e.memset(self.maskD[:], 0.0), writes=[self.maskD])
        kb.pool(lambda e: e.affine_select(out=self.maskD[:], in_=self.maskD[:], pattern=[[0, 4], [1, 128]],
                                          compare_op=ALU.is_ge, fill=NEG, base=0, channel_multiplier=-1),
                reads=[self.maskD], writes=[self.maskD])
        kb.pool(lambda e: e.memset(self.maskU[:], 0.0), writes=[self.maskU])
        kb.pool(lambda e: e.affine_select(out=self.maskU[:], in_=self.maskU[:], pattern=[[0, 4], [-1, 128]],
                                          compare_op=ALU.is_ge, fill=NEG, base=-1, channel_multiplier=1),
                reads=[self.maskU], writes=[self.maskU])
        for hf in range(2):
            sl = slice(64 * hf, 64 * hf + 64)
            kb.pool(lambda e, sl=sl: e.memset(self.ewide[sl, :], 1.0), writes=[self.ewide])
            kb.pool(lambda e, sl=sl: e.affine_select(out=self.ewide[sl, :], in_=self.ewide[sl, :], pattern=[[1, S]],
                                                     compare_op=ALU.is_ge, fill=0.0, base=0, channel_multiplier=-64),
                    reads=[self.ewide], writes=[self.ewide])
            kb.pool(lambda e, sl=sl: e.affine_select(out=self.ewide[sl, :], in_=self.ewide[sl, :], pattern=[[-1, S]],
                                                     compare_op=ALU.is_ge, fill=0.0, base=63, channel_multiplier=64),
                    reads=[self.ewide], writes=[self.ewide])
        kb.dma(self.rc[:], self.i_rc[:, :], writes=[self.rc])
        self.csel = self.tile([128, 2, 64], BF16, "csel")
        m = self.mark()
        cs = self.tile([128, 2, 64], F32, "csst")
        kb.dma(cs[:], self.i_csel.ap().rearrange("(t p) j -> p t j", p=128), writes=[cs])
        kb.dve(lambda e: e.tensor_copy(out=self.csel[:], in_=cs[:]), reads=[cs], writes=[self.csel])
        self.rope_tables()
        kb.barrier()
        self.release(m)

    def rope_tables(self):
        kb = self.kb
        HI = 6.28125
        LO = 2.0 * math.pi - HI
        for half in range(2):
            m = self.mark()
            n = 2048
            sl = slice(half * n, (half + 1) * n)
            pi_ = self.tile([128, n], I32, "posi")
            pf_ = self.tile([128, n], F32, "posf")
            ang = self.tile([128, n], F32, "ang")
            kf = self.tile([128, n], F32, "kf")
            ki = self.tile([128, n], I32, "ki")
            r = self.tile([128, n], F32, "r")
            msk = self.tile([128, n], F32, "msk")
            kb.dma(pi_[:], self.i_pos[:, sl].partition_broadcast(128) if False else self.i_pos[0:1, sl].broadcast_to([128, n]), writes=[pi_])
            kb.dve(lambda e: e.tensor_copy(out=pf_[:], in_=pi_[:]), reads=[pi_], writes=[pf_])
            kb.dve(lambda e: e.tensor_scalar(out=ang[:], in0=pf_[:], scalar1=self.rc[:, 0:1], scalar2=None, op0=ALU.mult),
                   reads=[pf_, self.rc], writes=[ang])
            kb.dve(lambda e: e.tensor_scalar(out=ki[:], in0=ang[:], scalar1=1.0 / (2 * math.pi), scalar2=None, op0=ALU.mult),
                   reads=[ang], writes=[ki])
            kb.dve(lambda e: e.tensor_copy(out=kf[:], in_=ki[:]), reads=[ki], writes=[kf])
            kb.dve(lambda e: e.scalar_tensor_tensor(out=ang[:], in0=kf[:], scalar=-HI, in1=ang[:], op0=ALU.mult, op1=ALU.add),
                   reads=[kf, ang], writes=[ang])
            kb.dve(lambda e: e.scalar_tensor_tensor(out=ang[:], in0=kf[:], scalar=-LO, in1=ang[:], op0=ALU.mult, op1=ALU.add),
                   reads=[kf, ang], writes=[ang])
            for which in range(2):
                kb.dve(lambda e, which=which: e.tensor_scalar(out=r[:], in0=ang[:], scalar1=(math.pi / 2) * which, scalar2=None, op0=ALU.add),
                       reads=[ang], writes=[r])
                for rep in range(2):
                    kb.dve(lambda e: e.tensor_scalar(out=msk[:], in0=r[:], scalar1=math.pi, scalar2=None, op0=ALU.is_gt),
                           reads=[r], writes=[msk])
                    kb.dve(lambda e: e.scalar_tensor_tensor(out=r[:], in0=msk[:], scalar=-2 * math.pi, in1=r[:], op0=ALU.mult, op1=ALU.add),
                           reads=[msk, r], writes=[r])
                    kb.dve(lambda e: e.tensor_scalar(out=msk[:], in0=r[:], scalar1=-math.pi, scalar2=None, op0=ALU.is_lt),
                           reads=[r], writes=[msk])
                    kb.dve(lambda e: e.scalar_tensor_tensor(out=r[:], in0=msk[:], scalar=2 * math.pi, in1=r[:], op0=ALU.mult, op1=ALU.add),
                           reads=[msk, r], writes=[r])
                kb.dve(lambda e: e.tensor_scalar(out=r[:], in0=r[:], scalar1=3.1415925, scalar2=-3.1415925, op0=ALU.min, op1=ALU.max),
                       reads=[r], writes=[r])
                kb.act(lambda e: e.activation(out=msk[:], in_=r[:], func=AF.Sin), reads=[r], writes=[msk])
                if which == 0:
                    kb.dve(lambda e: e.tensor_scalar(out=msk[:], in0=msk[:], scalar1=self.rc[:, 1:2], scalar2=None, op0=ALU.mult),
                           reads=[msk, self.rc], writes=[msk])
                    kb.dma(self.d_sin[:, sl], msk[:], reads=[msk])
                else:
                    kb.dma(self.d_cos[:, sl], msk[:], reads=[msk])
            kb.barrier()
            self.release(m)

    def rmsnorm_group(self, xg, hg, gain, sq, rstd):
        kb = self.kb
        ps = self.bank()
        for k in range(8):
            kb.act(lambda e, k=k: e.activation(out=sq[:, k, :], in_=xg[:, k, :], func=AF.Square), reads=[xg], writes=[(sq, k)])
        for k in range(8):
            self.mm(ps[:], self.ones[:], sq[:, k, :], k == 0, k == 7, reads=[(sq, k), self.ones], writes=[ps])
        kb.act(lambda e: e.activation(out=rstd[:], in_=ps[:], func=AF.Sqrt, scale=1.0 / D, bias=self.eps[:]),
               reads=[ps, self.eps], writes=[rstd])
        kb.dve(lambda e: e.reciprocal(out=rstd[:], in_=rstd[:]), reads=[rstd], writes=[rstd])
        for k in range(8):
            kb.dve(lambda e, k=k: e.scalar_tensor_tensor(out=hg[:, k, :], in0=xg[:, k, :], scalar=gain[:, k:k + 1], in1=rstd[:],
                                                         op0=ALU.mult, op1=ALU.mult),
                   reads=[xg, gain, rstd], writes=[(hg, k)])

    def ph_norm(self, l):
        kb = self.kb
        gain = self.tile([128, 8], F32, "gain")
        kb.dma(gain[:], self.i_gains[l], writes=[gain])
        xs = [self.tile([128, 8, 512], F32, "xg") for _ in range(2)]
        hs = [self.tile([128, 8, 512], BF16, "hg") for _ in range(2)]
        sqs = [self.tile([128, 8, 512], BF16, "sq") for _ in range(2)]
        rstds = [self.tile([128, 512], F32, "rstd") for _ in range(2)]
        for g in range(NG):
            xg, hg = xs[g % 2], hs[g % 2]
            sl = slice(g * 512, (g + 1) * 512)
            kb.dma(xg[:], self.d_x[:, :, sl], writes=[xg])
            self.rmsnorm_group(xg, hg, gain, sqs[g % 2], rstds[g % 2])
            kb.dma(self.d_h[:, :, sl], hg[:], reads=[hg])

    def rope_chunk(self, ps, dst_ap, dst, cs, sn, xf, rot, t1):
        kb = self.kb
        kb.act(lambda e: e.activation(out=xf[:], in_=ps[:], func=AF.Copy), reads=[ps], writes=[xf])
        pr = self.bank()
        self.mm(pr[:], self.permf[:], xf[:], True, True, reads=[xf, self.permf], writes=[pr])
        kb.dve(lambda e: e.tensor_tensor(out=t1[:], in0=xf[:], in1=cs[:], op=ALU.mult), reads=[xf, cs], writes=[t1])
        kb.dve(lambda e: e.tensor_tensor(out=rot[:], in0=pr[:], in1=sn[:], op=ALU.mult), reads=[pr, sn], writes=[rot])
        kb.dve(lambda e: e.tensor_tensor(out=dst_ap, in0=t1[:], in1=rot[:], op=ALU.add), reads=[t1, rot], writes=[dst])

    def proj_feature(self, wsb, ncol_chunks, hg, consume):
        for c in range(ncol_chunks):
            ps = self.bank()
            for k in range(8):
                self.mm(ps[:], wsb[:, k, c * 128:(c + 1) * 128], hg[:, k, :], k == 0, k == 7, reads=[hg, wsb], writes=[ps])
            consume(c, ps)

    def ph_nsa(self, l):
        kb = self.kb
        wf = self.tile([128, 8, 1024], BF16, "wnf")
        wt = self.tile([128, 8, 280], BF16, "wnt")
        w1 = self.tile([128, 2, 32, 128], BF16, "cw1")
        w2k = self.tile([128, 128], BF16, "cw2k")
        w2v = self.tile([128, 64], BF16, "cw2v")
        cpos = self.tile([128, 2, 32], BF16, "cpos")
        qT = self.tile([128, 4, S], BF16, "qTa")
        kT = self.tile([128, 3, S], BF16, "kTa")
        v0T = self.tile([128, S], BF16, "v0T")
        vtok = self.tile([128, NT, 2, 2, 65], BF16, "vtok")
        gates = self.tile([128, NT, 24], F32, "gates")
        kc = self.tile([128, 256], BF16, "kc")
        vcaug = self.tile([128, 2, 2, 128], BF16, "vcaug")
        m_w = self.mark()
        self.setup_stage(4096)
        for k in range(8):
            self.load_w_bf16(wf, wf[:, k, :], self.i_wnf[l, k * 128:(k + 1) * 128, :], [128, 1024])
        self.load_w_bf16(wt, wt[:, :, :], self.i_wnt[l].rearrange("(k p) c -> p k c", p=128), [128, 8, 280])
        for kv in range(2):
            for hf in range(2):
                st = self.stage[self.stage_i % 2]
                self.stage_i += 1
                sv = st[64 * hf:64 * hf + 64, 0:4096].rearrange("p (a b) -> p a b", a=32)
                kb.dma(sv, self.i_cw1[l, kv], writes=[st])
                kb.pool(lambda e, sv=sv, kv=kv, hf=hf: e.tensor_copy(out=w1[64 * hf:64 * hf + 64, kv, :, :], in_=sv), reads=[st], writes=[w1])
        st = self.stage[self.stage_i % 2]
        self.stage_i += 1
        kb.dma(st[:, 0:64], self.i_cw2[l, 0], writes=[st])
        kb.dma(st[:, 64:128], self.i_cw2[l, 1], writes=[st])
        kb.pool(lambda e, st=st: e.tensor_copy(out=w2k[:, 0:64], in_=st[:, 0:64]), reads=[st], writes=[w2k])
        kb.pool(lambda e, st=st: e.tensor_copy(out=w2k[:, 64:128], in_=st[:, 0:64]), reads=[st], writes=[w2k])
        kb.pool(lambda e, st=st: e.tensor_copy(out=w2v[:], in_=st[:, 64:128]), reads=[st], writes=[w2v])
        st = self.stage[self.stage_i % 2]
        self.stage_i += 1
        for hf in range(2):
            kb.dma(st[64 * hf:64 * hf + 64, 0:64].rearrange("p (a b) -> p a b", a=2), self.i_cpos[l], writes=[st])
        kb.pool(lambda e, st=st: e.tensor_copy(out=cpos[:], in_=st[:, 0:64].rearrange("p (a b) -> p a b", a=2)), reads=[st], writes=[cpos])
        kb.pool(lambda e: e.memset(vtok[:, :, :, :, 64:65], 1.0), writes=[vtok])
        for nt in range(2):
            for hk in range(2):
                kb.pool(lambda e, nt=nt, hk=hk: e.tensor_copy(out=vcaug[:, nt, hk, 64:128], in_=self.csel[:, nt, :]),
                        reads=[self.csel], writes=[vcaug])
        kb.barrier()
        hs = [self.tile([128, 8, 512], BF16, "hg") for _ in range(2)]
        css = [self.tile([128, 512], F32, "cs") for _ in range(2)]
        sns = [self.tile([128, 512], F32, "sn") for _ in range(2)]
        xf = [self.tile([128, 512], F32, "xf") for _ in range(2)]
        rot = [self.tile([128, 512], F32, "rot") for _ in range(2)]
        t1 = [self.tile([128, 512], F32, "t1") for _ in range(2)]
        cnt = [0]
        for g in range(NG):
            hg, cs, sn = hs[g % 2], css[g % 2], sns[g % 2]
            sl = slice(g * 512, (g + 1) * 512)
            kb.dma(hg[:], self.d_h[:, :, sl], writes=[hg])
            kb.dma(cs[:], self.d_cos[:, sl], writes=[cs])
            kb.dma(sn[:], self.d_sin[:, sl], writes=[sn])

            def consume(c, ps, sl=sl, cs=cs, sn=sn):
                i = cnt[0] % 2
                cnt[0] += 1
                if c < 4:
                    self.rope_chunk(ps, qT[:, c, sl], (qT, ("w", c, sl.start)), cs, sn, xf[i], rot[i], t1[i])
                elif c < 7:
                    self.rope_chunk(ps, kT[:, c - 4, sl], (kT, ("w", c, sl.start)), cs, sn, xf[i], rot[i], t1[i])
                else:
                    kb.act(lambda e: e.activation(out=v0T[:, sl], in_=ps[:], func=AF.Copy), reads=[ps], writes=[(v0T, sl.start)])
            self.proj_feature(wf, 8, hg, consume)
            for tt in range(4):
                ti = g * 4 + tt
                ps = self.bank()
                for k in range(8):
                    self.mm(ps[:, 0:280], hg[:, k, tt * 128:(tt + 1) * 128], wt[:, k, :], k == 0, k == 7, reads=[hg], writes=[ps])
                kb.act(lambda e, ti=ti, ps=ps: e.activation(out=vtok[:, ti, :, :, 0:64],
                                                            in_=ps[:, 0:256].rearrange("p (a b c) -> p a b c", a=2, b=2),
                                                            func=AF.Copy), reads=[ps], writes=[(vtok, ti)])
                kb.act(lambda e, ti=ti, ps=ps: e.activation(out=gates[:, ti, :], in_=ps[:, 256:280], func=AF.Sigmoid),
                       reads=[ps], writes=[(gates, ti)])
        kb.barrier()
        gel = self.tile([128, 256], BF16, "gel")
        hpre = self.tile([128, 256], F32, "hpre")
        hsq = self.tile([128, 256], F32, "hsq")
        bias = self.tile([128, 1], F32, "cbias")
        kb.pool(lambda e: e.memset(gel[:], 0.0), writes=[gel])
        for kv in range(2):
            psb = self.bank()
            for ll in range(32):
                self.mm(psb[:, 0:1], w1[0:64, kv, ll, :], cpos[0:64, kv, ll:ll + 1], ll == 0, ll == 31, reads=[w1, cpos], writes=[psb])
            kb.act(lambda e, psb=psb: e.activation(out=bias[:], in_=psb[:, 0:1], func=AF.Copy), reads=[psb], writes=[bias])
            src = kT if kv == 0 else v0T
            for hd in range(2):
                ps = self.bank()
                lo = 64 * hd
                for ll in range(32):
                    if kv == 0:
                        rhs = kT[lo:lo + 64, 0, ll:ll + 16 * 254 + 1:16]
                    else:
                        rhs = v0T[lo:lo + 64, ll:ll + 16 * 254 + 1:16]
                    self.mm(ps[:, 0:255], w1[lo:lo + 64, kv, ll, :], rhs, ll == 0, ll == 31, reads=[w1], writes=[ps])
                kb.act(lambda e, ps=ps: e.activation(out=hpre[:, 0:255], in_=ps[:, 0:255], func=AF.Identity, bias=bias[:]),
                       reads=[ps, bias], writes=[hpre])
                kb.dve(lambda e: e.tensor_tensor(out=hsq[:, 0:255], in0=hpre[:, 0:255], in1=hpre[:, 0:255], op=ALU.mult), reads=[hpre], writes=[hsq])
                kb.dve(lambda e: e.tensor_scalar(out=hsq[:, 0:255], in0=hsq[:, 0:255], scalar1=0.044715, scalar2=1.0, op0=ALU.mult, op1=ALU.add),
                       reads=[hsq], writes=[hsq])
                kb.dve(lambda e: e.tensor_tensor(out=hsq[:, 0:255], in0=hsq[:, 0:255], in1=hpre[:, 0:255], op=ALU.mult), reads=[hsq, hpre], writes=[hsq])
                kb.act(lambda e: e.activation(out=hsq[:, 0:255], in_=hsq[:, 0:255], func=AF.Sigmoid, scale=1.5957691216), reads=[hsq], writes=[hsq])
                kb.dve(lambda e: e.tensor_tensor(out=gel[:, 0:255], in0=hsq[:, 0:255], in1=hpre[:, 0:255], op=ALU.mult), reads=[hsq, hpre], writes=[gel])
                if kv == 0:
                    p2 = self.bank()
                    self.mm(p2[:, 0:256], w2k[:], gel[:], True, True, reads=[w2k, gel], writes=[p2])
                    kb.act(lambda e, p2=p2, lo=lo: e.activation(out=kc[lo:lo + 64, :], in_=p2[lo:lo + 64, 0:256], func=AF.Copy),
                           reads=[p2], writes=[kc])
                else:
                    for nt in range(2):
                        p2 = self.bank()
                        self.mm(p2[:, 0:64], gel[:, nt * 128:(nt + 1) * 128], w2v[:], True, True, reads=[w2v, gel], writes=[p2])
                        kb.act(lambda e, p2=p2, nt=nt, hd=hd: e.activation(out=vcaug[:, nt, hd, 0:64], in_=p2[:, 0:64], func=AF.Copy),
                               reads=[p2], writes=[vcaug])
        kb.barrier()
        self.release(m_w)
        kst = [self.tile([128, S], BF16, "kst") for _ in range(2)]
        kb.pool(lambda e: e.tensor_copy(out=kst[0][0:64, :], in_=kT[0:64, 1, :]), writes=[kst[0]])
        kb.act(lambda e: e.activation(out=kst[0][64:128, :], in_=self.ewide[64:128, :], func=AF.Copy), writes=[kst[0]])
        kb.act(lambda e: e.activation(out=kst[1][0:64, :], in_=self.ewide[0:64, :], func=AF.Copy), writes=[kst[1]])
        kb.pool(lambda e: e.tensor_copy(out=kst[1][64:128, :], in_=kT[64:128, 1, :]), writes=[kst[1]])
        NE = 8
        LA = 3
        nfv = [self.tile([128, 64], F32, "nfv") for _ in range(2)]
        add = [self.tile([128, 64], F32, "add") for _ in range(2)]
        eT = [self.tile([128, 512], BF16, "eT") for _ in range(NE)]
        oc = [self.tile([128, 2, 4, 128], F32, "oc") for _ in range(2)]
        osw = [[self.tile([128, 2, 4, 65], F32, "osw") for _ in range(2)] for _ in range(2)]
        den = [self.tile([128, 3, 8], F32, "den") for _ in range(2)]
        rden = [self.tile([128, 3, 8], F32, "rden") for _ in range(2)]
        imp = self.tile([128, 2, 64], F32, "imp")
        sc = self.tile([128, 2, 64], F32, "sc")
        wk = self.tile([128, 2, 64], F32, "wk")
        m8 = self.tile([128, 8], F32, "m8")
        nmk = self.tile([128, 2, 64], BF16, "nmk")
        qst = [[self.tile([128, 4, 128], BF16, "qst") for _ in range(2)] for _ in range(2)]
        y = self.tile([128, 8, 64], F32, "y")
        ytmp = self.tile([128, 8, 64], F32, "ytmp")
        yb = self.tile([128, 512], BF16, "yb16")
        yT = [self.tile([128, 4, 128], BF16, "yT") for _ in range(2)]
        fac = self.tile([128, 3, 8], F32, "fac")
        sbanks = self.pf[0:4]
        abanks = self.pf[4:7]
        cnt = {"s": 0, "e": 0, "a": 0}
        cur_po = {}
        units = []
        def add_units(i, kind):
            if kind == "cmp":
                kts = [0] if i < 16 else [0, 1]
            elif kind == "win":
                kts = list(range(max(0, i - 4), i + 1))
            else:
                kts = list(range(i + 1))
            for hk in range(2):
                for n_i, kt in enumerate(kts):
                    units.append(dict(i=i, kind=kind, hk=hk, kt=kt, first=(n_i == 0), last=(n_i == len(kts) - 1)))
        add_units(0, "cmp")
        add_units(0, "win")
        for i in range(NT):
            if i + 1 < NT:
                add_units(i + 1, "cmp")
            add_units(i, "sel")
            if i + 1 < NT:
                add_units(i + 1, "win")

        def score_fn(u):
            i, kind, hk, kt = u["i"], u["kind"], u["hk"], u["kt"]
            lo = 64 * hk
            qs = slice(i * 128, (i + 1) * 128)
            ps = sbanks[cnt["s"] % 4]
            cnt["s"] += 1
            if kind == "cmp":
                if hk == 0 and u["first"]:
                    kb.dma(nfv[i % 2][:], self.i_nfv[i], writes=[nfv[i % 2]])
                    kb.dma(add[i % 2][:], self.i_add[i], writes=[add[i % 2]])
                self.mm(ps[:], kc[lo:lo + 64, kt * 128:(kt + 1) * 128], qT[lo:lo + 64, :, qs], True, True, writes=[ps])
            else:
                br = 0 if kind == "sel" else 1
                ks = slice(kt * 128, (kt + 1) * 128)
                extra = []
                if br == 0:
                    extra.append("sel")
                if kt == i:
                    extra.append("D")
                if br == 1 and kt == i - 4:
                    extra.append("U")
                if br == 0:
                    extra.remove("sel")
                    qs_ = qst[i % 2][hk]
                    self.mm(ps[:], kst[hk][:, ks], qs_[:], True, len(extra) == 0, reads=[qs_], writes=[ps])
                else:
                    self.mm(ps[:], kT[lo:lo + 64, 1 + br, ks], qT[lo:lo + 64, :, qs], True, len(extra) == 0, writes=[ps])
                for xi, kd in enumerate(extra):
                    last = xi == len(extra) - 1
                    if kd == "D":
                        self.mm(ps[:], self.ident[:], self.maskD[:], False, last, writes=[ps])
                    else:
                        self.mm(ps[:], self.ident[:], self.maskU[:], False, last, writes=[ps])
            e_ = eT[cnt["e"] % NE]
            cnt["e"] += 1
            kb.act(lambda e: e.activation(out=e_[:], in_=ps[:], func=AF.Exp, scale=0.125), reads=[ps], writes=[e_])
            if kind == "cmp":
                kb.pool(lambda e: e.affine_select(
                    out=e_[:].rearrange("p (a n) -> p a n", a=4), in_=e_[:].rearrange("p (a n) -> p a n", a=4),
                    pattern=[[0, 4], [1, 128]], compare_op=ALU.is_ge, fill=0.0,
                    base=128 * i - 31 - 16 * 128 * kt, channel_multiplier=-16), reads=[e_], writes=[e_])
            u["e"] = e_

        def chain(i):
            oc_, den_, rden_ = oc[i % 2], den[i % 2], rden[i % 2]
            nf, ad = nfv[i % 2], add[i % 2]
            kb.dve(lambda e: e.tensor_reduce(out=den_[:, 0, :], in_=oc_[:, :, :, 64:128].rearrange("p a b c -> p (a b) c"), axis=AX.X, op=ALU.add),
                   reads=[oc_], writes=[den_])
            kb.dve(lambda e: e.tensor_scalar(out=rden_[:, 0, :], in0=den_[:, 0, :], scalar1=1e-30, scalar2=None, op0=ALU.max), reads=[den_], writes=[rden_])
            kb.dve(lambda e: e.reciprocal(out=rden_[:, 0, :], in_=rden_[:, 0, :]), reads=[rden_], writes=[rden_])
            for hk in range(2):
                for g4 in range(4):
                    hd = hk * 4 + g4
                    if g4 == 0:
                        kb.dve(lambda e, hk=hk, hd=hd: e.tensor_scalar(out=imp[:, hk, :], in0=oc_[:, hk, 0, 64:128], scalar1=rden_[:, 0, hd:hd + 1],
                                                                       scalar2=None, op0=ALU.mult), reads=[oc_, rden_], writes=[imp])
                    else:
                        kb.dve(lambda e, hk=hk, hd=hd, g4=g4: e.scalar_tensor_tensor(out=imp[:, hk, :], in0=oc_[:, hk, g4, 64:128],
                                                                                     scalar=rden_[:, 0, hd:hd + 1], in1=imp[:, hk, :],
                                                                                     op0=ALU.mult, op1=ALU.add), reads=[oc_, rden_, imp], writes=[imp])
            for hk in range(2):
                kb.dve(lambda e, hk=hk: e.tensor_tensor(out=sc[:, hk, :], in0=imp[:, hk, :], in1=nf[:], op=ALU.mult), reads=[imp, nf], writes=[sc])
                kb.dve(lambda e, hk=hk: e.tensor_tensor(out=sc[:, hk, :], in0=sc[:, hk, :], in1=ad[:], op=ALU.add), reads=[sc, ad], writes=[sc])
                kb.dve(lambda e, hk=hk: e.max(out=m8[:], in_=sc[:, hk, :]), reads=[sc], writes=[m8])
                kb.dve(lambda e, hk=hk: e.match_replace(out=wk[:, hk, :], in_to_replace=m8[:], in_values=sc[:, hk, :], imm_value=-3e38),
                       reads=[sc, m8], writes=[wk])
                kb.dve(lambda e, hk=hk: e.max(out=m8[:], in_=wk[:, hk, :]), reads=[wk], writes=[m8])
                kb.dve(lambda e, hk=hk: e.match_replace(out=wk[:, hk, :], in_to_replace=m8[:], in_values=wk[:, hk, :], imm_value=-3e38),
                       reads=[wk, m8], writes=[wk])
                kb.dve(lambda e, hk=hk: e.tensor_tensor(out=wk[:, hk, :], in0=wk[:, hk, :], in1=sc[:, hk, :], op=ALU.is_equal), reads=[wk, sc], writes=[wk])
                kb.dve(lambda e, hk=hk: e.tensor_scalar(out=nmk[:, 1 - hk, :], in0=wk[:, hk, :], scalar1=NEG, scalar2=None, op0=ALU.mult), reads=[wk], writes=[nmk])
            kb.pe(lambda e: e.transpose(self.pb[:, 0:128], nmk[:].rearrange("p a b -> p (a b)"), self.ident[:]), reads=[nmk, self.ident], writes=[self.pb])
            qs = slice(i * 128, (i + 1) * 128)
            for hk in range(2):
                lo = 64 * hk
                mo = 64 * (1 - hk)
                qs_ = qst[i % 2][hk]
                kb.pool(lambda e, lo=lo, qs_=qs_: e.tensor_copy(out=qs_[lo:lo + 64, :, :], in_=qT[lo:lo + 64, :, qs]), writes=[qs_])
                kb.act(lambda e, mo=mo, qs_=qs_: e.activation(out=qs_[mo:mo + 64, :, :],
                                                              in_=self.pb[mo:mo + 64, 0:128].rearrange("p (o n) -> p o n", o=1).broadcast_to([64, 4, 128]),
                                                              func=AF.Copy), reads=[self.pb], writes=[qs_])

        def combine(i):
            oc_, den_, rden_, osw_ = oc[i % 2], den[i % 2], rden[i % 2], osw[i % 2]
            qs = slice(i * 128, (i + 1) * 128)
            for br in range(2):
                kb.dve(lambda e, br=br: e.tensor_copy(out=den_[:, 1 + br, :], in_=osw_[br][:, :, :, 64:65].rearrange("p a b c -> p (a b c)")),
                       reads=[osw_[br]], writes=[den_])
            kb.dve(lambda e: e.reciprocal(out=rden_[:, 1:3, :], in_=den_[:, 1:3, :]), reads=[den_], writes=[rden_])
            kb.dve(lambda e: e.tensor_tensor(out=fac[:], in0=rden_[:], in1=gates[:, i, :].rearrange("p (a b) -> p a b", a=3), op=ALU.mult),
                   reads=[rden_, gates], writes=[fac])
            kb.dve(lambda e: e.tensor_tensor(out=y[:], in0=oc_[:, :, :, 0:64].rearrange("p a b c -> p (a b) c"),
                                             in1=fac[:, 0, :].rearrange("p (a o) -> p a o", o=1).broadcast_to([128, 8, 64]), op=ALU.mult),
                   reads=[oc_, fac], writes=[y])
            for br in range(2):
                kb.dve(lambda e, br=br: e.tensor_tensor(out=ytmp[:], in0=osw_[br][:, :, :, 0:64].rearrange("p a b c -> p (a b) c"),
                                                        in1=fac[:, 1 + br, :].rearrange("p (a o) -> p a o", o=1).broadcast_to([128, 8, 64]), op=ALU.mult),
                       reads=[osw_[br], fac], writes=[ytmp])
                if br == 0:
                    kb.dve(lambda e: e.tensor_tensor(out=y[:], in0=y[:], in1=ytmp[:], op=ALU.add), reads=[y, ytmp], writes=[y])
                else:
                    kb.dve(lambda e: e.tensor_tensor(out=yb[:].rearrange("p (a b) -> p a b", a=8), in0=y[:], in1=ytmp[:], op=ALU.add),
                           reads=[y, ytmp], writes=[yb])
            self.emit_yT(yb, yT[i % 2], self.d_ya, qs)

        def pv_fn(u):
            i, kind, hk, kt = u["i"], u["kind"], u["hk"], u["kt"]
            e_ = u["e"]
            key = (kind, hk)
            if u["first"]:
                po = abanks[cnt["a"] % 3]
                cnt["a"] += 1
                cur_po[key] = po
                self.mm(po[:], self.zeros[:, 0:128], self.zeros[:], True, False, reads=[self.zeros], writes=[po])
            po = cur_po[key]
            fin = u["last"]
            if kind == "cmp":
                for g4 in range(4):
                    self.mm(po[:, g4 * 128:(g4 + 1) * 128], e_[:, g4 * 128:(g4 + 1) * 128], vcaug[:, kt, hk, :], False, (fin and g4 == 3),
                            reads=[e_], writes=[po])
                if fin:
                    oc_ = oc[i % 2]
                    kb.act(lambda e: e.activation(out=oc_[:, hk, :, :], in_=po[:].rearrange("p (a b) -> p a b", a=4), func=AF.Copy),
                           reads=[po], writes=[oc_])
                    if hk == 1:
                        chain(i)
            else:
                br = 0 if kind == "sel" else 1
                for g4 in range(4):
                    self.mm(po[:, g4 * 65:(g4 + 1) * 65], e_[:, g4 * 128:(g4 + 1) * 128], vtok[:, kt, br, hk, :], False, (fin and g4 == 3),
                            reads=[e_], writes=[po])
                if fin:
                    o_ = osw[i % 2][br]
                    kb.act(lambda e: e.activation(out=o_[:, hk, :, :], in_=po[:, 0:260].rearrange("p (a b) -> p a b", a=4), func=AF.Copy),
                           reads=[po], writes=[o_])
                    if kind == "sel" and hk == 1:
                        combine(i)

        for idx in range(len(units) + LA):
            if idx - LA >= 0:
                pv_fn(units[idx - LA])
            if idx < len(units):
                score_fn(units[idx])

    def emit_yT(self, yb, yT, dst, qs):
        kb = self.kb
        pbh = self.pb
        for c in range(4):
            kb.pe(lambda e, c=c: e.transpose(self.pb[:, 512 + c * 128:512 + (c + 1) * 128], yb[:, c * 128:(c + 1) * 128], self.ident[:]),
                  reads=[yb, self.ident], writes=[pbh])
        kb.act(lambda e: e.activation(out=yT[:], in_=self.pb[:, 512:1024].rearrange("p (a b) -> p a b", a=4), func=AF.Copy), reads=[pbh], writes=[yT])
        kb.dma(dst[:, :, qs], yT[:], reads=[yT])

    def ph_swa(self, l):
        kb = self.kb
        wf = self.tile([128, 8, 640], BF16, "wsf")
        wt = self.tile([128, 8, 64], BF16, "wst")
        qT = self.tile([128, 4, S], BF16, "qTb")
        kT = self.tile([128, S], BF16, "kTb")
        vtok = self.tile([128, NT, 65], BF16, "vtokb")
        esink = self.tile([128, 8], F32, "esink")
        m_w = self.mark()
        self.setup_stage(5120)
        self.load_w_bf16(wf, wf[:, :, :], self.i_wsf[l].rearrange("(k p) c -> p k c", p=128), [128, 8, 640])
        self.load_w_bf16(wt, wt[:, :, :], self.i_wst[l].rearrange("(k p) c -> p k c", p=128), [128, 8, 64])
        kb.dma(esink[:], self.i_sinks[l, 0:1, :].broadcast_to([128, 8]), writes=[esink])
        kb.act(lambda e: e.activation(out=esink[:], in_=esink[:], func=AF.Exp), reads=[esink], writes=[esink])
        kb.pool(lambda e: e.memset(vtok[:, :, 64:65], 1.0), writes=[vtok])
        kb.barrier()
        hs = [self.tile([128, 8, 512], BF16, "hg") for _ in range(2)]
        css = [self.tile([128, 512], F32, "cs") for _ in range(2)]
        sns = [self.tile([128, 512], F32, "sn") for _ in range(2)]
        xf = [self.tile([128, 512], F32, "xf") for _ in range(2)]
        rot = [self.tile([128, 512], F32, "rot") for _ in range(2)]
        t1 = [self.tile([128, 512], F32, "t1") for _ in range(2)]
        cnt = [0]
        for g in range(NG):
            hg, cs, sn = hs[g % 2], css[g % 2], sns[g % 2]
            sl = slice(g * 512, (g + 1) * 512)
            kb.dma(hg[:], self.d_h[:, :, sl], writes=[hg])
            kb.dma(cs[:], self.d_cos[:, sl], writes=[cs])
            kb.dma(sn[:], self.d_sin[:, sl], writes=[sn])

            def consume(c, ps, sl=sl, cs=cs, sn=sn):
                i = cnt[0] % 2
                cnt[0] += 1
                if c < 4:
                    self.rope_chunk(ps, qT[:, c, sl], (qT, ("w", c, sl.start)), cs, sn, xf[i], rot[i], t1[i])
                else:
                    self.rope_chunk(ps, kT[:, sl], (kT, ("w", sl.start)), cs, sn, xf[i], rot[i], t1[i])
            self.proj_feature(wf, 5, hg, consume)
            for tt in range(4):
                ti = g * 4 + tt
                ps = self.bank()
                for k in range(8):
                    self.mm(ps[:, 0:64], hg[:, k, tt * 128:(tt + 1) * 128], wt[:, k, :], k == 0, k == 7, reads=[hg], writes=[ps])
                kb.act(lambda e, ti=ti, ps=ps: e.activation(out=vtok[:, ti, 0:64], in_=ps[:, 0:64], func=AF.Copy), reads=[ps], writes=[(vtok, ti)])
        kb.barrier()
        self.release(m_w)
        NE = 8
        LA = 3
        eT = [self.tile([128, 512], BF16, "eT") for _ in range(NE)]
        ob = [self.tile([128, 2, 4, 65], F32, "ob") for _ in range(2)]
        den = self.tile([128, 2, 4], F32, "denb")
        yb = self.tile([128, 512], BF16, "yb16")
        yT = [self.tile([128, 4, 128], BF16, "yT") for _ in range(2)]
        sbanks = self.pf[0:4]
        abanks = self.pf[4:7]
        cnt = {"s": 0, "e": 0, "a": 0}
        cur_po = {}
        units = []
        for i in range(NT):
            kts = [kt for kt in (i - 1, i) if kt >= 0]
            for hf in range(2):
                for n_i, kt in enumerate(kts):
                    units.append(dict(i=i, hf=hf, kt=kt, first=(n_i == 0), last=(n_i == len(kts) - 1)))

        def score_fn(u):
            i, hf, kt = u["i"], u["hf"], u["kt"]
            lo = 64 * hf
            qs = slice(i * 128, (i + 1) * 128)
            ks = slice(kt * 128, (kt + 1) * 128)
            ps = sbanks[cnt["s"] % 4]
            cnt["s"] += 1
            self.mm(ps[:], kT[lo:lo + 64, ks], qT[lo:lo + 64, :, qs], True, False, writes=[ps])
            self.mm(ps[:], self.ident[:], (self.maskD if kt == i else self.maskU)[:], False, True, writes=[ps])
            e_ = eT[cnt["e"] % NE]
            cnt["e"] += 1
            kb.act(lambda e: e.activation(out=e_[:], in_=ps[:], func=AF.Exp, scale=0.125), reads=[ps], writes=[e_])
            u["e"] = e_

        def finish(i):
            ob_ = ob[i % 2]
            qs = slice(i * 128, (i + 1) * 128)
            kb.dve(lambda e: e.tensor_tensor(out=den[:], in0=ob_[:, :, :, 64:65].rearrange("p a b c -> p a (b c)"),
                                             in1=esink[:].rearrange("p (c h) -> p h c", h=2), op=ALU.add), reads=[ob_, esink], writes=[den])
            kb.dve(lambda e: e.reciprocal(out=den[:], in_=den[:]), reads=[den], writes=[den])
            kb.dve(lambda e: e.tensor_tensor(out=yb[:].rearrange("p (c h d) -> p h c d", c=4, h=2), in0=ob_[:, :, :, 0:64],
                                             in1=den[:].rearrange("p a (b o) -> p a b o", o=1).broadcast_to([128, 2, 4, 64]), op=ALU.mult),
                   reads=[ob_, den], writes=[yb])
            self.emit_yT(yb, yT[i % 2], self.d_yb, qs)

        def pv_fn(u):
            i, hf, kt = u["i"], u["hf"], u["kt"]
            e_ = u["e"]
            if u["first"]:
                po = abanks[cnt["a"] % 3]
                cnt["a"] += 1
                cur_po[hf] = po
                self.mm(po[:], self.zeros[:, 0:128], self.zeros[:], True, False, reads=[self.zeros], writes=[po])
            po = cur_po[hf]
            fin = u["last"]
            for c in range(4):
                self.mm(po[:, c * 65:(c + 1) * 65], e_[:, c * 128:(c + 1) * 128], vtok[:, kt, :], False, (fin and c == 3),
                        reads=[e_], writes=[po])
            if fin:
                ob_ = ob[i % 2]
                kb.act(lambda e: e.activation(out=ob_[:, hf, :, :], in_=po[:, 0:260].rearrange("p (a b) -> p a b", a=4), func=AF.Copy),
                       reads=[po], writes=[ob_])
                if hf == 1:
                    finish(i)

        for idx in range(len(units) + LA):
            if idx - LA >= 0:
                pv_fn(units[idx - LA])
            if idx < len(units):
                score_fn(units[idx])

    def ph_s5(self, l):
        kb = self.kb
        NP = 16
        uT = self.tile([128, 4, S], BF16, "uT")
        bbar = self.tile([128, NP, 2, 128], BF16, "bbar")
        pwT = self.tile([128, 17, 3, NP], F32, "pwT")
        par = self.tile([128, 3, NP], F32, "s5par")
        dsk = self.tile([128, 2, 4], F32, "s5d")
        bT = self.tile([128, NP, 2, 128], BF16, "bT")
        cw = self.tile([128, NP, 2, 128], BF16, "cw")
        glu = self.tile([128, 4, 512], BF16, "glu")
        pw = self.tile([128, 12, 3, NP], F32, "pw")
        m_w = self.mark()
        bw = self.tile([128, NP, 2, 128], F32, "bw")
        wu = self.tile([128, 8, 512], BF16, "wu")
        self.setup_stage(4096)
        self.load_w_bf16(wu, wu[:, :, :], self.i_wu[l].rearrange("(k p) c -> p k c", p=128), [128, 8, 512])
        self.load_w_bf16(glu, glu[:, :, :], self.i_glu[l].rearrange("(k p) c -> p k c", p=128), [128, 4, 512])
        kb.dma(par[:], self.i_s5a[l], writes=[par])
        kb.dma(dsk[:], self.i_s5d[l], writes=[dsk])
        kb.dma(bw[:], self.i_s5b[l], writes=[bw])
        st = self.stage[self.stage_i % 2]
        self.stage_i += 1
        sv = st[:, 0:4096].rearrange("p (a b c) -> p a b c", a=NP, b=2)
        kb.dma(sv, self.i_s5c[l], writes=[st])
        kb.pool(lambda e: e.tensor_copy(out=cw[:, :, 0, :], in_=sv[:, :, 0, :]), reads=[st], writes=[cw])
        kb.pool(lambda e: e.tensor_scalar(out=cw[:, :, 1, :], in0=sv[:, :, 1, :], scalar1=-1.0, scalar2=None, op0=ALU.mult), reads=[st], writes=[cw])
        hs = [self.tile([128, 8, 512], BF16, "hg") for _ in range(2)]
        for g in range(NG):
            hg = hs[g % 2]
            sl = slice(g * 512, (g + 1) * 512)
            kb.dma(hg[:], self.d_h[:, :, sl], writes=[hg])

            def consume(c, ps, sl=sl):
                kb.act(lambda e: e.activation(out=uT[:, c, sl], in_=ps[:], func=AF.Copy), reads=[ps], writes=[(uT, (c, sl.start))])
            self.proj_feature(wu, 4, hg, consume)
        import os as _os
        sub = int(_os.environ.get("S5SUB", "99"))
        if sub <= 0:
            return
        def t16(name):
            return self.tile([128, NP], F32, name)
        dt, mag, phi, ar, ai, kf, r, msk, cr, ci, den, nr, t_a, t_b = [t16("s5_%d" % j) for j in range(14)]
        ki = self.tile([128, NP], I32, "s5ki")
        A_re, A_im = par[:, 0, :], par[:, 1, :]
        V = kb.dve
        V(lambda e: e.tensor_copy(out=dt[:], in_=par[:, 2, :]), reads=[par], writes=[dt])
        kb.act(lambda e: e.activation(out=dt[:], in_=dt[:], func=AF.Exp), reads=[dt], writes=[dt])
        V(lambda e: e.tensor_tensor(out=mag[:], in0=dt[:], in1=A_re, op=ALU.mult), reads=[dt, par], writes=[mag])
        kb.act(lambda e: e.activation(out=mag[:], in_=mag[:], func=AF.Exp), reads=[mag], writes=[mag])
        V(lambda e: e.tensor_tensor(out=phi[:], in0=dt[:], in1=A_im, op=ALU.mult), reads=[dt, par], writes=[phi])
        HI = 6.28125
        LO = 2.0 * math.pi - HI

        def sin_of(dst, src_t, shift):
            V(lambda e: e.tensor_scalar(out=t_a[:], in0=src_t[:], scalar1=shift, scalar2=None, op0=ALU.add), reads=[src_t], writes=[t_a])
            V(lambda e: e.tensor_scalar(out=ki[:], in0=t_a[:], scalar1=1.0 / (2 * math.pi), scalar2=None, op0=ALU.mult), reads=[t_a], writes=[ki])
            V(lambda e: e.tensor_copy(out=kf[:], in_=ki[:]), reads=[ki], writes=[kf])
            V(lambda e: e.scalar_tensor_tensor(out=r[:], in0=kf[:], scalar=-HI, in1=t_a[:], op0=ALU.mult, op1=ALU.add), reads=[kf, t_a], writes=[r])
            V(lambda e: e.scalar_tensor_tensor(out=r[:], in0=kf[:], scalar=-LO, in1=r[:], op0=ALU.mult, op1=ALU.add), reads=[kf, r], writes=[r])
            V(lambda e: e.tensor_scalar(out=msk[:], in0=r[:], scalar1=math.pi, scalar2=None, op0=ALU.is_gt), reads=[r], writes=[msk])
            V(lambda e: e.scalar_tensor_tensor(out=r[:], in0=msk[:], scalar=-2 * math.pi, in1=r[:], op0=ALU.mult, op1=ALU.add), reads=[msk, r], writes=[r])
            V(lambda e: e.tensor_scalar(out=msk[:], in0=r[:], scalar1=-math.pi, scalar2=None, op0=ALU.is_lt), reads=[r], writes=[msk])
            V(lambda e: e.scalar_tensor_tensor(out=r[:], in0=msk[:], scalar=2 * math.pi, in1=r[:], op0=ALU.mult, op1=ALU.add), reads=[msk, r], writes=[r])
            V(lambda e: e.tensor_scalar(out=r[:], in0=r[:], scalar1=3.1415925, scalar2=-3.1415925, op0=ALU.min, op1=ALU.max), reads=[r], writes=[r])
            kb.act(lambda e: e.activation(out=dst[:], in_=r[:], func=AF.Sin), reads=[r], writes=[dst])
        if sub <= 1:
            return
        sin_of(ai, phi, 0.0)
        sin_of(ar, phi, math.pi / 2)
        if sub <= 2:
            return
        V(lambda e: e.tensor_tensor(out=ar[:], in0=ar[:], in1=mag[:], op=ALU.mult), reads=[ar, mag], writes=[ar])
        V(lambda e: e.tensor_tensor(out=ai[:], in0=ai[:], in1=mag[:], op=ALU.mult), reads=[ai, mag], writes=[ai])
        V(lambda e: e.tensor_scalar(out=nr[:], in0=ar[:], scalar1=-1.0, scalar2=None, op0=ALU.add), reads=[ar], writes=[nr])
        V(lambda e: e.tensor_tensor(out=den[:], in0=A_re, in1=A_re, op=ALU.mult), reads=[par], writes=[den])
        V(lambda e: e.tensor_tensor(out=t_a[:], in0=A_im, in1=A_im, op=ALU.mult), reads=[par], writes=[t_a])
        V(lambda e: e.tensor_tensor(out=den[:], in0=den[:], in1=t_a[:], op=ALU.add), reads=[den, t_a], writes=[den])
        V(lambda e: e.reciprocal(out=den[:], in_=den[:]), reads=[den], writes=[den])
        V(lambda e: e.tensor_tensor(out=cr[:], in0=nr[:], in1=A_re, op=ALU.mult), reads=[nr, par], writes=[cr])
        V(lambda e: e.tensor_tensor(out=t_a[:], in0=ai[:], in1=A_im, op=ALU.mult), reads=[ai, par], writes=[t_a])
        V(lambda e: e.tensor_tensor(out=cr[:], in0=cr[:], in1=t_a[:], op=ALU.add), reads=[cr, t_a], writes=[cr])
        V(lambda e: e.tensor_tensor(out=cr[:], in0=cr[:], in1=den[:], op=ALU.mult), reads=[cr, den], writes=[cr])
        V(lambda e: e.tensor_tensor(out=ci[:], in0=ai[:], in1=A_re, op=ALU.mult), reads=[ai, par], writes=[ci])
        V(lambda e: e.tensor_tensor(out=t_a[:], in0=nr[:], in1=A_im, op=ALU.mult), reads=[nr, par], writes=[t_a])
        V(lambda e: e.tensor_tensor(out=ci[:], in0=ci[:], in1=t_a[:], op=ALU.subtract), reads=[ci, t_a], writes=[ci])
        V(lambda e: e.tensor_tensor(out=ci[:], in0=ci[:], in1=den[:], op=ALU.mult), reads=[ci, den], writes=[ci])
        if sub <= 3:
            return
        tb1 = self.tile([128, NP, 128], F32, "tb1")
        tb2 = self.tile([128, NP, 128], F32, "tb2")

        def bc(tl):
            return tl[:].rearrange("p (a o) -> p a o", o=1).broadcast_to([128, NP, 128])
        V(lambda e: e.tensor_tensor(out=tb1[:], in0=bw[:, :, 0, :], in1=bc(cr), op=ALU.mult), reads=[bw, cr], writes=[tb1])
        V(lambda e: e.tensor_tensor(out=tb2[:], in0=bw[:, :, 1, :], in1=bc(ci), op=ALU.mult), reads=[bw, ci], writes=[tb2])
        V(lambda e: e.tensor_tensor(out=bbar[:, :, 0, :], in0=tb1[:], in1=tb2[:], op=ALU.subtract), reads=[tb1, tb2], writes=[bbar])
        V(lambda e: e.tensor_tensor(out=tb1[:], in0=bw[:, :, 1, :], in1=bc(cr), op=ALU.mult), reads=[bw, cr], writes=[tb1])
        V(lambda e: e.tensor_tensor(out=tb2[:], in0=bw[:, :, 0, :], in1=bc(ci), op=ALU.mult), reads=[bw, ci], writes=[tb2])
        V(lambda e: e.tensor_tensor(out=bbar[:, :, 1, :], in0=tb1[:], in1=tb2[:], op=ALU.add), reads=[tb1, tb2], writes=[bbar])
        if sub <= 4:
            return
        for P in range(NP):
            for c2 in range(2):
                var = int(_os.environ.get("S5VAR", "0"))
                j = (P * 2 + c2) % 4
                if var == 1:
                    j = 0
                if var == 2:
                    j = 4 + (P * 2 + c2) % 4
                pk = self.pb
                kb.pe(lambda e, P=P, c2=c2, j=j: e.transpose(self.pb[:, j * 128:(j + 1) * 128], bbar[:, P, c2, :], self.ident[:]),
                      reads=[bbar, self.ident], writes=[pk])
                kb.act(lambda e, P=P, c2=c2, j=j: e.activation(out=bT[:, P, c2, :], in_=self.pb[:, j * 128:(j + 1) * 128], func=AF.Copy),
                       reads=[pk], writes=[bT])
        if sub <= 5:
            return
        V(lambda e: e.tensor_copy(out=pw[:, 0, 0, :], in_=ar[:]), reads=[ar], writes=[pw])
        V(lambda e: e.tensor_copy(out=pw[:, 0, 1, :], in_=ai[:]), reads=[ai], writes=[pw])
        for d_ in range(1, 12):
            pr, pi_ = pw[:, d_ - 1, 0, :], pw[:, d_ - 1, 1, :]
            V(lambda e, pr=pr: e.tensor_tensor(out=t_a[:], in0=pr, in1=pr, op=ALU.mult), reads=[pw], writes=[t_a])
            V(lambda e, pi_=pi_: e.tensor_tensor(out=t_b[:], in0=pi_, in1=pi_, op=ALU.mult), reads=[pw], writes=[t_b])
            V(lambda e, d_=d_: e.tensor_tensor(out=pw[:, d_, 0, :], in0=t_a[:], in1=t_b[:], op=ALU.subtract), reads=[t_a, t_b], writes=[pw])
            V(lambda e, pr=pr, pi_=pi_: e.tensor_tensor(out=t_a[:], in0=pr, in1=pi_, op=ALU.mult), reads=[pw], writes=[t_a])
            V(lambda e, d_=d_: e.tensor_scalar(out=pw[:, d_, 1, :], in0=t_a[:], scalar1=2.0, scalar2=None, op0=ALU.mult), reads=[t_a], writes=[pw])
        V(lambda e: e.tensor_scalar(out=pw[:, :, 2, :], in0=pw[:, :, 1, :], scalar1=-1.0, scalar2=None, op0=ALU.mult), reads=[pw], writes=[pw])
        V(lambda e: e.memset(pwT[:, 0, 0, :], 1.0), writes=[pwT])
        V(lambda e: e.memset(pwT[:, 0, 1, :], 0.0), writes=[pwT])
        for tau in range(1, 17):
            pr, pi_ = pwT[:, tau - 1, 0, :], pwT[:, tau - 1, 1, :]
            V(lambda e, pr=pr: e.tensor_tensor(out=t_a[:], in0=pr, in1=ar[:], op=ALU.mult), reads=[pwT, ar], writes=[t_a])
            V(lambda e, pi_=pi_: e.tensor_tensor(out=t_b[:], in0=pi_, in1=ai[:], op=ALU.mult), reads=[pwT, ai], writes=[t_b])
            V(lambda e, tau=tau: e.tensor_tensor(out=pwT[:, tau, 0, :], in0=t_a[:], in1=t_b[:], op=ALU.subtract), reads=[t_a, t_b], writes=[pwT])
            V(lambda e, pr=pr: e.tensor_tensor(out=t_a[:], in0=pr, in1=ai[:], op=ALU.mult), reads=[pwT, ai], writes=[t_a])
            V(lambda e, pi_=pi_: e.tensor_tensor(out=t_b[:], in0=pi_, in1=ar[:], op=ALU.mult), reads=[pwT, ar], writes=[t_b])
            V(lambda e, tau=tau: e.tensor_tensor(out=pwT[:, tau, 1, :], in0=t_a[:], in1=t_b[:], op=ALU.add), reads=[t_a, t_b], writes=[pwT])
        V(lambda e: e.tensor_scalar(out=pwT[:, :, 2, :], in0=pwT[:, :, 1, :], scalar1=-1.0, scalar2=None, op0=ALU.mult), reads=[pwT], writes=[pwT])
        kb.barrier()
        self.release(m_w)
        TC = 16
        NC = S // TC
        Bu = [self.tile([128, TC, NC], F32, "Bur"), self.tile([128, TC, NC], F32, "Bui")]
        upm = self.tile([128, TC, NC], BF16, "upm")
        SA = [self.tile([128, NC], F32, "SAr"), self.tile([128, NC], F32, "SAi")]
        SB = [self.tile([128, NC], F32, "SBr"), self.tile([128, NC], F32, "SBi")]
        Xb = [self.tile([128, 2, NC + 1], BF16, "Xb") for _ in range(2)]
        obs = [self.tile([128, 17, 2, 128], BF16, "obs") for _ in range(2)]
        tm1 = self.tile([128, 17, 128], F32, "tm1")
        tm2 = self.tile([128, 17, 128], F32, "tm2")
        Kacc = self.tile([128, 16, 128], F32, "Kacc")
        Ksb = self.tile([128, 16, 128], BF16, "Ksb")
        yc = self.tile([128, S], F32, "ycf")
        ztc = self.tile([128, S], BF16, "ztc")
        tgl = Bu[0]
        tglf = Bu[0][:].rearrange("p t c -> p (t c)")
        for xb_ in Xb:
            kb.pool(lambda e, xb_=xb_: e.memset(xb_[:, :, 0:1], 0.0), writes=[xb_])

        def bc_c(ap2):
            return ap2.rearrange("p (o c) -> p o c", o=1).broadcast_to([128, 17, 128])

        def bc_t(ap2):
            return ap2.rearrange("p (t o) -> p t o", o=1).broadcast_to([128, 17, 128])
        ycv = yc[:].rearrange("p (t c) -> p t c", t=TC)
        ztv = ztc[:].rearrange("p (c t) -> p t c", t=TC)
        for ch in range(4):
            kb.pool(lambda e, ch=ch: e.tensor_copy(out=upm[:], in_=uT[:, ch, :].rearrange("p (c t) -> p t c", t=TC)), writes=[upm])
            kb.act(lambda e, ch=ch: e.activation(out=ycv, in_=upm[:], func=AF.Copy, scale=dsk[:, 0, ch:ch + 1]),
                   reads=[dsk, upm], writes=[yc])
            for pq in range(4):
                P = ch * 4 + pq
                xb_ = Xb[P % 2]
                ob_ = obs[P % 2]
                for c2 in range(2):
                    for g in range(NG):
                        ps = self.bank()
                        self.mm(ps[:], bT[:, P, c2, :], upm[:, 2 * g:2 * g + 2, :], True, True, reads=[upm], writes=[ps])
                        kb.act(lambda e, ps=ps, c2=c2, g=g: e.activation(out=Bu[c2][:, 2 * g:2 * g + 2, :], in_=ps[:].rearrange("p (a b) -> p a b", a=2),
                                                                         func=AF.Copy), reads=[ps], writes=[(Bu[c2], g)])
                for j in range(TC):
                    tau = TC - 1 - j
                    s_r = pwT[:, tau, 0, P:P + 1]
                    s_i = pwT[:, tau, 1, P:P + 1]
                    xr, xi = Bu[0][:, j, :], Bu[1][:, j, :]
                    if j == 0:
                        V(lambda e, s_r=s_r, xr=xr: e.tensor_scalar(out=SA[0][:], in0=xr, scalar1=s_r, scalar2=None, op0=ALU.mult), reads=[Bu[0], pwT], writes=[SA[0]])
                        V(lambda e, s_r=s_r, xi=xi: e.tensor_scalar(out=SA[1][:], in0=xi, scalar1=s_r, scalar2=None, op0=ALU.mult), reads=[Bu[1], pwT], writes=[SA[1]])
                    else:
                        V(lambda e, s_r=s_r, xr=xr: e.scalar_tensor_tensor(out=SA[0][:], in0=xr, scalar=s_r, in1=SA[0][:], op0=ALU.mult, op1=ALU.add),
                          reads=[Bu[0], pwT, SA[0]], writes=[SA[0]])
                        V(lambda e, s_r=s_r, xi=xi: e.scalar_tensor_tensor(out=SA[1][:], in0=xi, scalar=s_r, in1=SA[1][:], op0=ALU.mult, op1=ALU.add),
                          reads=[Bu[1], pwT, SA[1]], writes=[SA[1]])
                    s_ni = pwT[:, tau, 2, P:P + 1]
                    V(lambda e, s_ni=s_ni, xi=xi: e.scalar_tensor_tensor(out=SA[0][:], in0=xi, scalar=s_ni, in1=SA[0][:], op0=ALU.mult, op1=ALU.add),
                      reads=[Bu[1], pwT, SA[0]], writes=[SA[0]])
                    V(lambda e, s_i=s_i, xr=xr: e.scalar_tensor_tensor(out=SA[1][:], in0=xr, scalar=s_i, in1=SA[1][:], op0=ALU.mult, op1=ALU.add),
                      reads=[Bu[0], pwT, SA[1]], writes=[SA[1]])
                A, B = SA, SB
                for d_ in range(8):
                    sh = 1 << d_
                    s_ar, s_ai, s_nai = pw[:, 4 + d_, 0, P:P + 1], pw[:, 4 + d_, 1, P:P + 1], pw[:, 4 + d_, 2, P:P + 1]
                    V(lambda e, A=A, B=B, sh=sh, s=s_ar: e.scalar_tensor_tensor(out=B[0][:, sh:], in0=A[0][:, 0:NC - sh], scalar=s, in1=A[0][:, sh:],
                                                                              op0=ALU.mult, op1=ALU.add), reads=[A[0], pw], writes=[B[0]])
                    V(lambda e, A=A, B=B, sh=sh, s=s_nai: e.scalar_tensor_tensor(out=B[0][:, sh:], in0=A[1][:, 0:NC - sh], scalar=s, in1=B[0][:, sh:],
                                                                               op0=ALU.mult, op1=ALU.add), reads=[A[1], B[0], pw], writes=[B[0]])
                    V(lambda e, A=A, B=B, sh=sh, s=s_ar: e.scalar_tensor_tensor(out=B[1][:, sh:], in0=A[1][:, 0:NC - sh], scalar=s, in1=A[1][:, sh:],
                                                                              op0=ALU.mult, op1=ALU.add), reads=[A[1], pw], writes=[B[1]])
                    V(lambda e, A=A, B=B, sh=sh, s=s_ai: e.scalar_tensor_tensor(out=B[1][:, sh:], in0=A[0][:, 0:NC - sh], scalar=s, in1=B[1][:, sh:],
                                                                              op0=ALU.mult, op1=ALU.add), reads=[A[0], B[1], pw], writes=[B[1]])
                    kb.pool(lambda e, A=A, B=B, sh=sh: e.tensor_copy(out=B[0][:, 0:sh], in_=A[0][:, 0:sh]), reads=[A[0]], writes=[B[0]])
                    kb.pool(lambda e, A=A, B=B, sh=sh: e.tensor_copy(out=B[1][:, 0:sh], in_=A[1][:, 0:sh]), reads=[A[1]], writes=[B[1]])
                    A, B = B, A
                for c2 in range(2):
                    kb.pool(lambda e, A=A, c2=c2, xb_=xb_: e.tensor_copy(out=xb_[:, c2, 1:NC + 1], in_=A[c2][:]), reads=[A[c2]], writes=[xb_])
                c0, c1 = bc_c(cw[:, P, 0, :]), bc_c(cw[:, P, 1, :])
                p_r, p_i = bc_t(pwT[:, :, 0, P]), bc_t(pwT[:, :, 1, P])
                V(lambda e, c0=c0, p_r=p_r: e.tensor_tensor(out=tm1[:], in0=c0, in1=p_r, op=ALU.mult), reads=[cw, pwT], writes=[tm1])
                V(lambda e, c1=c1, p_i=p_i: e.tensor_tensor(out=tm2[:], in0=c1, in1=p_i, op=ALU.mult), reads=[cw, pwT], writes=[tm2])
                V(lambda e, ob_=ob_: e.tensor_tensor(out=ob_[:, :, 0, :], in0=tm1[:], in1=tm2[:], op=ALU.add), reads=[tm1, tm2], writes=[ob_])
                V(lambda e, c1=c1, p_r=p_r: e.tensor_tensor(out=tm1[:], in0=c1, in1=p_r, op=ALU.mult), reads=[cw, pwT], writes=[tm1])
                V(lambda e, c0=c0, p_i=p_i: e.tensor_tensor(out=tm2[:], in0=c0, in1=p_i, op=ALU.mult), reads=[cw, pwT], writes=[tm2])
                V(lambda e, ob_=ob_: e.tensor_tensor(out=ob_[:, :, 1, :], in0=tm1[:], in1=tm2[:], op=ALU.subtract), reads=[tm1, tm2], writes=[ob_])
                for i in range(TC):
                    ps = self.bank()
                    self.mm(ps[:, 0:NC], ob_[:, i + 1, 0, :], xb_[:, 0, 0:NC], True, False, reads=[ob_, xb_], writes=[ps])
                    self.mm(ps[:, 0:NC], ob_[:, i + 1, 1, :], xb_[:, 1, 0:NC], False, True, reads=[ob_, xb_], writes=[ps])
                    V(lambda e, ps=ps, i=i: e.tensor_tensor(out=ycv[:, i, :], in0=ycv[:, i, :], in1=ps[:, 0:NC], op=ALU.add), reads=[ps, yc], writes=[yc])
                for t4 in range(4):
                    ps = self.bank()
                    self.mm(ps[:], bbar[:, P, 0, :], ob_[:, t4 * 4:(t4 + 1) * 4, 0, :], True, False, reads=[ob_, bbar], writes=[ps])
                    self.mm(ps[:], bbar[:, P, 1, :], ob_[:, t4 * 4:(t4 + 1) * 4, 1, :], False, True, reads=[ob_, bbar], writes=[ps])
                    kv = Kacc[:, t4 * 4:(t4 + 1) * 4, :]
                    if pq == 0:
                        kb.act(lambda e, ps=ps, kv=kv: e.activation(out=kv, in_=ps[:].rearrange("p (a b) -> p a b", a=4), func=AF.Copy),
                               reads=[ps], writes=[(Kacc, t4)])
                    else:
                        V(lambda e, ps=ps, kv=kv: e.tensor_tensor(out=kv, in0=kv, in1=ps[:].rearrange("p (a b) -> p a b", a=4), op=ALU.add),
                          reads=[ps, (Kacc, t4)], writes=[(Kacc, t4)])
            kb.pool(lambda e: e.tensor_copy(out=Ksb[:], in_=Kacc[:]), reads=[Kacc], writes=[Ksb])
            for i in range(TC):
                ps = self.bank()
                for j in range(i + 1):
                    self.mm(ps[:, 0:NC], Ksb[:, i - j, :], upm[:, j, :], j == 0, j == i, reads=[Ksb, upm], writes=[ps])
                V(lambda e, ps=ps, i=i: e.tensor_tensor(out=ycv[:, i, :], in0=ycv[:, i, :], in1=ps[:, 0:NC], op=ALU.add), reads=[ps, yc], writes=[yc])
            for g in range(NG):
                sl = slice(g * 512, (g + 1) * 512)
                kb.pool(lambda e, sl=sl, g=g: e.tensor_tensor(out=tglf[:, sl], in0=yc[:, sl], in1=yc[:, sl], op=ALU.mult), reads=[yc], writes=[(tgl, g)])
                kb.pool(lambda e, sl=sl, g=g: e.tensor_scalar(out=tglf[:, sl], in0=tglf[:, sl], scalar1=0.044715, scalar2=1.0, op0=ALU.mult, op1=ALU.add),
                        reads=[(tgl, g)], writes=[(tgl, g)])
                kb.pool(lambda e, sl=sl, g=g: e.tensor_tensor(out=tglf[:, sl], in0=tglf[:, sl], in1=yc[:, sl], op=ALU.mult), reads=[(tgl, g), yc],
                        writes=[(tgl, g)])
                kb.act(lambda e, sl=sl, g=g: e.activation(out=tglf[:, sl], in_=tglf[:, sl], func=AF.Sigmoid, scale=1.5957691216), reads=[(tgl, g)],
                       writes=[(tgl, g)])
                kb.pool(lambda e, sl=sl, g=g: e.tensor_tensor(out=ztv[:, 2 * g:2 * g + 2, :], in0=tglf[:, sl].rearrange("p (a b) -> p a b", a=2),
                                                              in1=yc[:, sl].rearrange("p (a b) -> p a b", a=2), op=ALU.mult),
                        reads=[(tgl, g), yc], writes=[(ztc, sl.start)])
            kb.dma(self.d_z[:, ch, :], ztc[:], reads=[ztc])
        kb.barrier()
        self.release(m_w)
        sg = [self.tile([128, 512], F32, "sg") for _ in range(2)]
        og = [self.tile([128, 4, 512], BF16, "og") for _ in range(2)]
        zg = [self.tile([128, 4, 512], BF16, "zg") for _ in range(2)]
        for g in range(NG):
            sl = slice(g * 512, (g + 1) * 512)
            o_ = og[g % 2]
            z_ = zg[g % 2]
            kb.dma(z_[:], self.d_z[:, :, sl], writes=[z_])
            for co in range(4):
                ps = self.bank()
                for k4 in range(4):
                    self.mm(ps[:], glu[:, k4, co * 128:(co + 1) * 128], z_[:, k4, :], k4 == 0, k4 == 3, reads=[z_], writes=[ps])
                s_ = sg[co % 2]
                kb.act(lambda e, ps=ps, s_=s_, co=co: e.activation(out=s_[:], in_=ps[:], func=AF.Sigmoid, bias=dsk[:, 1, co:co + 1]),
                       reads=[ps, dsk], writes=[s_])
                V(lambda e, s_=s_, co=co, o_=o_, z_=z_: e.tensor_tensor(out=o_[:, co, :], in0=s_[:], in1=z_[:, co, :], op=ALU.mult),
                  reads=[s_, z_], writes=[(o_, co)])
            kb.dma(self.d_yc[:, :, sl], o_[:], reads=[o_])

    def ph_merge(self, l):
        kb = self.kb
        wgm = self.tile([128, 8, 3072], BF16, "wgm")
        wbr = self.tile([128, 3, 4, D], BF16, "wbr")
        wo = self.tile([128, 8, D], BF16, "wo")
        m_st = self.mark()
        self.setup_stage(4096)
        for k in range(8):
            self.load_w_bf16(wgm, wgm[:, k, :], self.i_wgm[l, k * 128:(k + 1) * 128, :], [128, 3072])
        for b in range(3):
            self.load_w_bf16(wbr, wbr[:, b, :, :], self.i_wbr[l, b].rearrange("(k p) c -> p k c", p=128), [128, 4, D])
        for k2 in range(2):
            self.load_w_bf16(wo, wo[:, k2 * 4:(k2 + 1) * 4, :], self.i_wout[l, k2 * 512:(k2 + 1) * 512, :].rearrange("(k p) c -> p k c", p=128), [128, 4, D])
        kb.barrier()
        self.release(m_st)
        hs = [self.tile([128, 8, 512], BF16, "hg") for _ in range(2)]
        ys = [[self.tile([128, 4, 512], BF16, "yg") for _ in range(3)] for _ in range(2)]
        xs = [self.tile([128, 8, 512], F32, "xg") for _ in range(2)]
        mg = self.tile([128, 8, 512], BF16, "mg")
        sgt = [self.tile([128, 512], F32, "sgt") for _ in range(3)]
        acc = self.tile([128, 512], F32, "acc")
        tmp = self.tile([128, 512], F32, "tmp")
        ysrc = [self.d_ya, self.d_yb, self.d_yc]
        for g in range(NG):
            sl = slice(g * 512, (g + 1) * 512)
            hg, xg, yg = hs[g % 2], xs[g % 2], ys[g % 2]
            kb.dma(hg[:], self.d_h[:, :, sl], writes=[hg])
            kb.dma(xg[:], self.d_x[:, :, sl], writes=[xg])
            for b in range(3):
                kb.dma(yg[b][:], ysrc[b][:, :, sl], writes=[yg[b]])
            for dc in range(8):
                cs_ = slice(dc * 128, (dc + 1) * 128)
                for b in range(3):
                    ps = self.bank()
                    for k in range(8):
                        self.mm(ps[:], wgm[:, k, b * D + dc * 128:b * D + (dc + 1) * 128], hg[:, k, :], k == 0, k == 7, reads=[hg], writes=[ps])
                    kb.act(lambda e, ps=ps, b=b: e.activation(out=sgt[b][:], in_=ps[:], func=AF.Sigmoid), reads=[ps], writes=[sgt[b]])
                for b in range(3):
                    ps = self.bank()
                    for k in range(4):
                        self.mm(ps[:], wbr[:, b, k, cs_], yg[b][:, k, :], k == 0, k == 3, reads=[yg[b]], writes=[ps])
                    if b == 0:
                        kb.dve(lambda e, ps=ps: e.tensor_tensor(out=acc[:], in0=sgt[0][:], in1=ps[:], op=ALU.mult), reads=[sgt[0], ps], writes=[acc])
                    elif b == 1:
                        kb.dve(lambda e, ps=ps: e.tensor_tensor(out=tmp[:], in0=sgt[1][:], in1=ps[:], op=ALU.mult), reads=[sgt[1], ps], writes=[tmp])
                        kb.dve(lambda e: e.tensor_tensor(out=acc[:], in0=acc[:], in1=tmp[:], op=ALU.add), reads=[acc, tmp], writes=[acc])
                    else:
                        kb.dve(lambda e, ps=ps: e.tensor_tensor(out=tmp[:], in0=sgt[2][:], in1=ps[:], op=ALU.mult), reads=[sgt[2], ps], writes=[tmp])
                        kb.dve(lambda e, dc=dc: e.tensor_tensor(out=mg[:, dc, :], in0=acc[:], in1=tmp[:], op=ALU.add), reads=[acc, tmp], writes=[(mg, dc)])
            for do in range(8):
                ps = self.bank()
                for k in range(8):
                    self.mm(ps[:], wo[:, k, do * 128:(do + 1) * 128], mg[:, k, :], k == 0, k == 7, reads=[(mg, k)], writes=[ps])
                kb.dve(lambda e, ps=ps, do=do, xg=xg: e.tensor_tensor(out=xg[:, do, :], in0=xg[:, do, :], in1=ps[:], op=ALU.add),
                       reads=[ps, (xg, do)], writes=[(xg, do)])
            kb.dma(self.d_x[:, :, sl], xg[:], reads=[xg])

    def ph_mlp(self, l):
        kb = self.kb
        wup = self.tile([128, 8, 4 * D], BF16, "wup")
        wdn = self.tile([128, 32, D], BF16, "wdn")
        gain = self.tile([128, 8], F32, "gain")
        m_st = self.mark()
        self.setup_stage(4096)
        for k in range(8):
            self.load_w_bf16(wup, wup[:, k, :], self.i_wup[l, k * 128:(k + 1) * 128, :], [128, 4096])
        for k4 in range(8):
            self.load_w_bf16(wdn, wdn[:, k4 * 4:(k4 + 1) * 4, :], self.i_wdn[l, k4 * 512:(k4 + 1) * 512, :].rearrange("(k p) c -> p k c", p=128), [128, 4, D])
        kb.dma(gain[:], self.i_gains[L + l], writes=[gain])
        kb.barrier()
        self.release(m_st)
        gain2 = self.tile([128, 8], F32, "gain2")
        if l + 1 < self.nl:
            kb.dma(gain2[:], self.i_gains[l + 1], writes=[gain2])
        xs = [self.tile([128, 8, 512], F32, "xg") for _ in range(1)]
        hg = self.tile([128, 8, 512], BF16, "hg")
        rstd = self.tile([128, 512], F32, "rstd")
        a = self.tile([128, 32, 512], BF16, "aT")
        sq = a
        rl = [self.tile([128, 512], F32, "rl") for _ in range(2)]
        for g in range(NG):
            sl = slice(g * 512, (g + 1) * 512)
            xg = xs[0]
            kb.dma(xg[:], self.d_x[:, :, sl], writes=[xg])
            self.rmsnorm_group(xg, hg, gain, sq, rstd)
            for f in range(32):
                ps = self.bank()
                for k in range(8):
                    self.mm(ps[:], wup[:, k, f * 128:(f + 1) * 128], hg[:, k, :], k == 0, k == 7, reads=[(hg, k)], writes=[ps])
                r_ = rl[f % 2]
                kb.act(lambda e, ps=ps, r_=r_: e.activation(out=r_[:], in_=ps[:], func=AF.Relu), reads=[ps], writes=[r_])
                kb.pool(lambda e, r_=r_, f=f: e.tensor_tensor(out=a[:, f, :], in0=r_[:], in1=r_[:], op=ALU.mult), reads=[r_], writes=[(a, f)])
            for do in range(8):
                ps = self.bank()
                for f in range(32):
                    self.mm(ps[:], wdn[:, f, do * 128:(do + 1) * 128], a[:, f, :], f == 0, f == 31, reads=[(a, f)], writes=[ps])
                kb.dve(lambda e, ps=ps, do=do, xg=xg: e.tensor_tensor(out=xg[:, do, :], in0=xg[:, do, :], in1=ps[:], op=ALU.add),
                       reads=[ps, (xg, do)], writes=[(xg, do)])
            kb.dma(self.d_x[:, :, sl], xg[:], reads=[xg])
            if l + 1 < self.nl and self.phases is None:
                self.rmsnorm_group(xg, hg, gain2, sq, rstd)
                kb.dma(self.d_h[:, :, sl], hg[:], reads=[hg])

    def ph_final(self):
        kb = self.kb
        gain = self.tile([128, 8], F32, "gain")
        kb.dma(gain[:], self.i_gains[2 * L], writes=[gain])
        xs = [self.tile([128, 8, 512], F32, "xg") for _ in range(2)]
        os_ = [self.tile([128, 8, 512], F32, "og") for _ in range(2)]
        sqs = [self.tile([128, 8, 512], BF16, "sq") for _ in range(2)]
        rstds = [self.tile([128, 512], F32, "rstd") for _ in range(2)]
        for g in range(NG):
            sl = slice(g * 512, (g + 1) * 512)
            xg, og = xs[g % 2], os_[g % 2]
            kb.dma(xg[:], self.d_x[:, :, sl], writes=[xg])
            self.rmsnorm_group(xg, og, gain, sqs[g % 2], rstds[g % 2])
            kb.dma(self.o_out[:, :, sl], og[:], reads=[og])


def prep_shared(inp):
    f = np.float32
    w_in = np.asarray(inp["w_in"], f)
    o = {}
    o["gains"] = np.ascontiguousarray(np.concatenate([
        np.asarray(inp["norm_mix"], f).reshape(L, 8, 128).transpose(0, 2, 1),
        np.asarray(inp["norm_mlp"], f).reshape(L, 8, 128).transpose(0, 2, 1),
        np.asarray(inp["norm_final"], f).reshape(1, 8, 128).transpose(0, 2, 1)], 0))
    qa = w_in[:, :, 0:512].reshape(L, D, 8, 64)
    qa_perm = np.stack([np.concatenate([qa[:, :, c], qa[:, :, 4 + c]], -1) for c in range(4)], 2).reshape(L, D, 512)
    kv = w_in[:, :, 512:1280]
    k0, v0 = kv[:, :, 0:128], kv[:, :, 128:256]
    k1, v1 = kv[:, :, 256:384], kv[:, :, 384:512]
    k2, v2 = kv[:, :, 512:640], kv[:, :, 640:768]
    ga = w_in[:, :, 1280:1304]
    o["w_nsa_f"] = np.ascontiguousarray(np.concatenate([qa_perm, k0, k1, k2, v0], -1))
    o["w_nsa_t"] = np.ascontiguousarray(np.concatenate([v1, v2, ga], -1))
    qb = w_in[:, :, 1304:1816]
    kvb = w_in[:, :, 1816:1944]
    o["w_swa_f"] = np.ascontiguousarray(np.concatenate([qb, kvb[:, :, 0:64], kvb[:, :, 0:64]], -1))
    o["w_swa_t"] = np.ascontiguousarray(kvb[:, :, 64:128])
    o["w_u"] = np.ascontiguousarray(w_in[:, :, 1944:2456])
    o["w_gm"] = np.ascontiguousarray(w_in[:, :, 2456:5528])
    o["cw1"] = np.ascontiguousarray(np.asarray(inp["nsa_cmp_w1"], f).reshape(L, 2, 32, 64, 128).transpose(0, 1, 3, 2, 4))
    o["cw2"] = np.ascontiguousarray(np.asarray(inp["nsa_cmp_w2"], f))
    o["cposT"] = np.ascontiguousarray(np.asarray(inp["nsa_cmp_pos"], f).transpose(0, 3, 1, 2))
    o["sinks"] = np.ascontiguousarray(np.asarray(inp["swa_sinks"], f).reshape(L, 1, 8))
    def pl(a):
        return np.asarray(a, f).reshape(L, 16, 2, 64).transpose(0, 2, 3, 1).reshape(L, 128, 16)
    ldt = np.repeat(np.asarray(inp["s5_log_dt"], f)[:, :, None], 64, 2)
    o["s5a"] = np.ascontiguousarray(np.stack([pl(inp["s5_a_re"]), pl(inp["s5_a_im"]), pl(ldt)], 2))
    def bl(a):
        a = np.asarray(a, f).reshape(L, 16, 2, 64, 16)
        out = np.zeros((L, 2, 64, 16, 4, 2, 16), f)
        for P in range(16):
            for g2 in range(2):
                out[:, g2, :, P, P % 4, g2, :] = a[:, P, g2]
        return out.reshape(L, 128, 16, 128)
    o["s5b"] = np.ascontiguousarray(np.stack([bl(inp["s5_b_re"]), bl(inp["s5_b_im"])], 3))
    cre = np.asarray(inp["s5_c_re"], f).transpose(0, 1, 3, 2)
    cim = np.asarray(inp["s5_c_im"], f).transpose(0, 1, 3, 2)
    o["s5c"] = np.ascontiguousarray(np.stack([bl(cre), bl(cim)], 3))
    o["s5d"] = np.ascontiguousarray(np.stack([np.asarray(inp["s5_d"], f).reshape(L, 4, 128).transpose(0, 2, 1),
                                              np.asarray(inp["s5_glu_b"], f).reshape(L, 4, 128).transpose(0, 2, 1)], 2))
    o["glu_w"] = np.ascontiguousarray(np.asarray(inp["s5_glu_w"], f))
    o["w_br"] = np.ascontiguousarray(np.stack([np.asarray(inp["w_branch_a"], f), np.asarray(inp["w_branch_b"], f),
                                               np.asarray(inp["w_branch_c"], f)], 1))
    o["w_out"] = np.ascontiguousarray(np.asarray(inp["w_out"], f))
    o["w_up"] = np.ascontiguousarray(np.asarray(inp["w_mlp_up"], f))
    o["w_dn"] = np.ascontiguousarray(np.asarray(inp["w_mlp_down"], f))
    n = np.arange(256)[:, None]
    j = np.arange(64)[None, :]
    ov = np.minimum(16 * n + 32, 64 * j + 64) - np.maximum(16 * n, 64 * j)
    cs = np.clip(ov, 0, None).astype(f) / 32.0
    cs[255] = 0
    o["csel"] = cs
    qpos = np.arange(S)
    cur = (qpos // 64)[:, None]
    jj = np.arange(64)[None, :]
    valid = jj <= cur
    forced = (jj == 0) | (jj == cur) | (jj == cur - 1)
    o["nfv"] = np.ascontiguousarray((valid & ~forced).astype(f).reshape(NT, 128, 64))
    o["addend"] = np.ascontiguousarray(np.where(forced & valid, 1e4, np.where(valid, 0.0, -1e30)).astype(f).reshape(NT, 128, 64))
    p = np.arange(128)
    invf = (10000.0 ** (-(np.arange(32, dtype=f)) / 32.0)).astype(f)
    rc = np.zeros((128, 2), f)
    rc[:, 0] = invf[p % 32]
    rc[:, 1] = np.where((p % 64) < 32, -1.0, 1.0)
    o["ropec"] = rc
    return o


_CACHE = {}


def get_prog(key, **kw):
    if key not in _CACHE:
        pr = Prog(**kw)
        nc, n = pr.build()
        _CACHE[key] = (pr, nc)
    return _CACHE[key]


def kernel(**inputs):
    shared = prep_shared(inputs)
    x = np.asarray(inputs["x"], np.float32)
    pos = np.asarray(inputs["positions"], np.int32)
    pr, nc = get_prog("full")
    in_maps = []
    for b in range(8):
        m = dict(shared)
        m["xT_in"] = np.ascontiguousarray(x[b].T.reshape(8, 128, S).transpose(1, 0, 2))
        m["pos"] = np.ascontiguousarray(pos[b].reshape(1, S))
        in_maps.append(m)
    res = run_bass_kernel_spmd(nc, in_maps, core_ids=list(range(8)))
    out = np.empty((8, S, D), np.float32)
    for b in range(8):
        o = np.asarray(res.results[b]["outT"])
        out[b] = o.transpose(2, 1, 0).reshape(S, D)
    return out
```
